# Optimizing a Trainium2 kernel written in Bass

```python
import math
import jax, jax.numpy as jnp
from jax import lax
import numpy as np

D_MODEL = 1024
BATCH = 8
SEQ = 2048
DEPTH = 1

D_MIX = D_MODEL
FOX_HEADS = 8
FOX_HEAD_DIM = 64
FOX_WIDTH = FOX_HEADS * FOX_HEAD_DIM
MLSTM_HEADS = 4
MLSTM_HEAD_DIM = 128
MLSTM_WIDTH = MLSTM_HEADS * MLSTM_HEAD_DIM
MLSTM_CONV = 4
MLSTM_CHUNK = 64
Q_BLOCK = 128
N_GATE = FOX_HEADS + 2 * MLSTM_HEADS
IN_COLS = 3 * FOX_WIDTH + FOX_HEADS + 3 * MLSTM_WIDTH + 2 * MLSTM_HEADS + MLSTM_WIDTH
D_FF = 2816
FFN_CONV = 3
N_MOD = 6
EPS = 1e-6

kernel_name = "hybrid_fox_mlstm_convffn_layer"


def rmsnorm(x, g):
    xf = x.astype(jnp.float32)
    y = xf * lax.rsqrt(jnp.mean(xf * xf, axis=-1, keepdims=True) + EPS)
    return (y * g.astype(jnp.float32)).astype(x.dtype)


def causal_dwconv(x, w, b):
    K, C = w.shape
    y = lax.conv_general_dilated(x, w.reshape(K, 1, C).astype(x.dtype), window_strides=(1,),
                                 padding=[(K - 1, 0)], dimension_numbers=('NWC', 'WIO', 'NWC'),
                                 feature_group_count=C)
    return y + b.astype(x.dtype)


def split_heads(a, H):
    B, S, _ = a.shape
    return a.reshape(B, S, H, -1).transpose(0, 2, 1, 3)


def merge_heads(a):
    B, H, S, d = a.shape
    return a.transpose(0, 2, 1, 3).reshape(B, S, H * d)


def fox_attention(q, k, v, log_f):
    S = q.shape[2]
    dh = q.shape[3]
    F = jnp.cumsum(log_f.astype(jnp.float32), axis=-1)
    scale = dh ** -0.5
    outs = []
    for i in range(S // Q_BLOCK):
        q0, q1 = i * Q_BLOCK, (i + 1) * Q_BLOCK
        s = jnp.einsum('bhqd,bhkd->bhqk', q[:, :, q0:q1], k[:, :, :q1]).astype(jnp.float32) * scale
        s = s + F[:, :, q0:q1, None] - F[:, :, None, :q1]
        mask = (q0 + jnp.arange(Q_BLOCK))[:, None] >= jnp.arange(q1)[None, :]
        s = jnp.where(mask, s, -jnp.inf)
        p = jax.nn.softmax(s, axis=-1)
        outs.append(jnp.einsum('bhqk,bhkd->bhqd', p.astype(v.dtype), v[:, :, :q1]))
    return jnp.concatenate(outs, axis=2)


def mlstm_chunkwise(q, k, v, i_pre, f_pre):
    dtype = q.dtype
    B, H, S, d = q.shape
    L = MLSTM_CHUNK
    NC = S // L
    qf = q.astype(jnp.float32)
    kf = k.astype(jnp.float32) * (d ** -0.5)
    vf = v.astype(jnp.float32)
    ig = i_pre.astype(jnp.float32)
    lf = jax.nn.log_sigmoid(f_pre.astype(jnp.float32))

    def chunks(a):
        return jnp.moveaxis(a.reshape(B, H, NC, L, *a.shape[3:]), 2, 0)

    causal = jnp.tril(jnp.ones((L, L), dtype=bool))

    def step(carry, inp):
        C, n, m = carry
        qb, kb, vb, ib, lfb = inp
        b = jnp.cumsum(lfb, axis=-1)
        Dm = jnp.where(causal, b[..., :, None] - b[..., None, :] + ib[..., None, :], -jnp.inf)
        inter = b + m[..., None]
        m_row = jnp.maximum(inter, jnp.max(Dm, axis=-1))
        w_intra = jnp.exp(Dm - m_row[..., None])
        w_inter = jnp.exp(inter - m_row)
        qk = jnp.einsum('bhld,bhsd->bhls', qb, kb) * w_intra
        num = jnp.einsum('bhls,bhsd->bhld', qk, vb) + w_inter[..., None] * jnp.einsum('bhvd,bhld->bhlv', C, qb)
        den = jnp.sum(qk, axis=-1) + w_inter * jnp.einsum('bhd,bhld->bhl', n, qb)
        h = num / jnp.maximum(jnp.abs(den), jnp.exp(-m_row))[..., None]
        bL = b[..., -1]
        g = bL[..., None] - b + ib
        m_new = jnp.maximum(bL + m, jnp.max(g, axis=-1))
        wg = jnp.exp(g - m_new[..., None])
        decay = jnp.exp(bL + m - m_new)
        C_new = decay[..., None, None] * C + jnp.einsum('bhs,bhsv,bhsd->bhvd', wg, vb, kb)
        n_new = decay[..., None] * n + jnp.einsum('bhs,bhsd->bhd', wg, kb)
        return (C_new, n_new, m_new), h

    init = (jnp.zeros((B, H, d, d), jnp.float32), jnp.zeros((B, H, d), jnp.float32),
            jnp.zeros((B, H), jnp.float32))
    _, hs = lax.scan(step, init, (chunks(qf), chunks(kf), chunks(vf), chunks(ig), chunks(lf)))
    return jnp.moveaxis(hs, 0, 2).reshape(B, H, S, d).astype(dtype)


def setup_inputs(seed: int = 0) -> dict:
    key = jax.random.key(seed)
    ks = jax.random.split(key, 24)
    f32 = jnp.float32
    nrm = lambda k, shape: jax.random.normal(k, shape, f32)
    x = nrm(ks[0], (BATCH, SEQ, D_MODEL))
    c = nrm(ks[1], (BATCH, D_MODEL))
    w_ada = nrm(ks[2], (DEPTH, D_MODEL, N_MOD * D_MODEL)) * (0.5 * D_MODEL ** -0.5)
    b_ada = 0.02 * nrm(ks[3], (DEPTH, N_MOD * D_MODEL))
    g_pre_mix = 1.0 + 0.05 * nrm(ks[4], (DEPTH, D_MODEL))
    g_post_mix = 1.0 + 0.05 * nrm(ks[5], (DEPTH, D_MODEL))
    w_in = nrm(ks[6], (DEPTH, D_MODEL, IN_COLS)) * D_MODEL ** -0.5
    b_gate = jnp.concatenate([
        3.0 + 0.5 * nrm(ks[7], (DEPTH, FOX_HEADS)),
        0.1 * nrm(ks[8], (DEPTH, MLSTM_HEADS)),
        3.0 + 0.5 * nrm(ks[9], (DEPTH, MLSTM_HEADS))],
        axis=-1)
    mlstm_conv_w = nrm(ks[10], (DEPTH, MLSTM_CONV, 2 * MLSTM_WIDTH)) * MLSTM_CONV ** -0.5
    mlstm_conv_b = 0.02 * nrm(ks[11], (DEPTH, 2 * MLSTM_WIDTH))
    g_mlstm_head = 1.0 + 0.05 * nrm(ks[12], (DEPTH, MLSTM_WIDTH))
    w_out = nrm(ks[13], (DEPTH, D_MIX, D_MODEL)) * D_MIX ** -0.5
    g_pre_ffn = 1.0 + 0.05 * nrm(ks[14], (DEPTH, D_MODEL))
    g_post_ffn = 1.0 + 0.05 * nrm(ks[15], (DEPTH, D_MODEL))
    w_up = nrm(ks[16], (DEPTH, D_MODEL, 2 * D_FF)) * D_MODEL ** -0.5
    ffn_conv_w = nrm(ks[17], (DEPTH, FFN_CONV, 2 * D_FF)) * FFN_CONV ** -0.5
    ffn_conv_b = 0.02 * nrm(ks[18], (DEPTH, 2 * D_FF))
    w_down = nrm(ks[19], (DEPTH, D_FF, D_MODEL)) * D_FF ** -0.5
    return {"x": x, "c": c, "w_ada": w_ada, "b_ada": b_ada, "g_pre_mix": g_pre_mix,
            "g_post_mix": g_post_mix, "w_in": w_in, "b_gate": b_gate, "mlstm_conv_w": mlstm_conv_w,
            "mlstm_conv_b": mlstm_conv_b, "g_mlstm_head": g_mlstm_head, "w_out": w_out,
            "g_pre_ffn": g_pre_ffn, "g_post_ffn": g_post_ffn, "w_up": w_up, "ffn_conv_w": ffn_conv_w,
            "ffn_conv_b": ffn_conv_b, "w_down": w_down}


def reference(x, c, w_ada, b_ada, g_pre_mix, g_post_mix, w_in, b_gate, mlstm_conv_w, mlstm_conv_b,
              g_mlstm_head, w_out, g_pre_ffn, g_post_ffn, w_up, ffn_conv_w, ffn_conv_b, w_down):
    B, S, _ = x.shape
    sizes = (FOX_WIDTH, FOX_WIDTH, FOX_WIDTH, FOX_HEADS, MLSTM_WIDTH, MLSTM_WIDTH, MLSTM_WIDTH,
             MLSTM_HEADS, MLSTM_HEADS, MLSTM_WIDTH)
    cuts = [sum(sizes[:i + 1]) for i in range(len(sizes) - 1)]
    for l in range(DEPTH):
        mod = jax.nn.silu(c) @ w_ada[l] + b_ada[l]
        sh1, sc1, gt1, sh2, sc2, gt2 = [m[:, None, :] for m in jnp.split(mod, N_MOD, axis=-1)]

        h = rmsnorm(x, g_pre_mix[l]) * (1.0 + sc1) + sh1
        z = h @ w_in[l]
        fq, fk, fv, ff, mq, mk, mv, mi, mf, mo = jnp.split(z, cuts, axis=-1)
        bg = b_gate[l]
        fox_logf = jax.nn.log_sigmoid((ff + bg[:FOX_HEADS]).astype(jnp.float32))
        fox_o = fox_attention(split_heads(fq, FOX_HEADS), split_heads(fk, FOX_HEADS),
                              split_heads(fv, FOX_HEADS), fox_logf.transpose(0, 2, 1))
        fox_o = merge_heads(fox_o)
        mqk = jax.nn.silu(causal_dwconv(jnp.concatenate([mq, mk], axis=-1), mlstm_conv_w[l], mlstm_conv_b[l]))
        mq_c, mk_c = jnp.split(mqk, 2, axis=-1)
        i_pre = (mi + bg[FOX_HEADS:FOX_HEADS + MLSTM_HEADS]).transpose(0, 2, 1)
        f_pre = (mf + bg[FOX_HEADS + MLSTM_HEADS:]).transpose(0, 2, 1)
        m_h = mlstm_chunkwise(split_heads(mq_c, MLSTM_HEADS), split_heads(mk_c, MLSTM_HEADS),
                              split_heads(mv, MLSTM_HEADS), i_pre, f_pre)
        m_hf = m_h.astype(jnp.float32)
        m_hf = m_hf * lax.rsqrt(jnp.mean(m_hf * m_hf, axis=-1, keepdims=True) + EPS)
        mlstm_o = (merge_heads(m_hf) * g_mlstm_head[l].astype(jnp.float32)).astype(x.dtype)
        mlstm_o = jax.nn.sigmoid(mo) * mlstm_o
        mix = jnp.concatenate([fox_o, mlstm_o], axis=-1) @ w_out[l]
        x = x + gt1 * rmsnorm(mix, g_post_mix[l])

        h = rmsnorm(x, g_pre_ffn[l]) * (1.0 + sc2) + sh2
        u = causal_dwconv(h @ w_up[l], ffn_conv_w[l], ffn_conv_b[l])
        ua, ub = jnp.split(u, 2, axis=-1)
        y = (jax.nn.silu(ua) * ub) @ w_down[l]
        x = x + gt2 * rmsnorm(y, g_post_ffn[l])
    return x
```

```python
import bisect
import math
from contextlib import ExitStack

import numpy as np
import concourse.bass as bass
import concourse.mybir as mybir
from concourse.bass_utils import run_bass_kernel_spmd

F32 = mybir.dt.float32
BF16 = mybir.dt.bfloat16
AF = mybir.ActivationFunctionType
ALU = mybir.AluOpType

NCORES = 8
SEQ = 2048
DM = 1024
DFF = 2816
EPS = 1e-6
NEG = -30000.0
EXPSHIFT = 9.0

C_C = 0
C_BADA = 8
C_GPM = 56
C_GOM = 64
C_GPF = 72
C_GOF = 80
C_MCW = 88
C_MCB = 120
C_GMH = 128
C_FCW = 132
C_FCB = 264
C_BG = 308
NSM = 324


class Buf:
    __slots__ = ("name", "w", "r")

    def __init__(self, name):
        self.name = name
        self.w = None
        self.r = []


class Sched:
    EPOCH = 4000
    NDMA = 12

    def __init__(self, nc, stack):
        self.nc = nc
        self.stack = stack
        self.eng = {"pe": nc.tensor, "act": nc.scalar, "dve": nc.vector,
                    "pool": nc.gpsimd, "sp": nc.sync}
        self.seq = {e: 0 for e in self.eng}
        self.pending = {e: {} for e in self.eng}
        self.signaled = {e: [] for e in self.eng}
        self.sems = {e: [] for e in self.eng}
        self.waited = {}
        self.dma_sems = {}
        self.dma_cnt = {}
        self.ndma = {}
        self.nwaits = 0

    def _sem(self, e, idx):
        while len(self.sems[e]) <= idx:
            self.sems[e].append(self.stack.enter_context(
                self.nc.semaphore(f"s_{e}_{len(self.sems[e])}")))
        return self.sems[e][idx]

    def _target(self, tok):
        if tok[0] == "dma":
            _, q, s, cnt = tok
            return ("dma", q, s), self.dma_sems[q][s], 16 * cnt
        e, n = tok
        sg = self.signaled[e]
        i = bisect.bisect_left(sg, n)
        if i == len(sg):
            ins = self.pending[e][n]
            rank = len(sg)
            ins.then_inc(self._sem(e, rank // self.EPOCH), 1)
            sg.append(n)
            for k in [k for k in self.pending[e] if k <= n]:
                del self.pending[e][k]
            i = rank
        return (e, i // self.EPOCH), self._sem(e, i // self.EPOCH), i % self.EPOCH + 1

    def _wait(self, waiter, toks):
        need = {}
        for tok in toks:
            if tok is None:
                continue
            if tok[0] == "pe" and waiter == "pe":
                continue
            key, sem, val = self._target(tok)
            if key[0] == "dma":
                if self.waited.get((waiter, key), 0) >= val:
                    continue
                if need.get(key, (None, 0))[1] < val:
                    need[key] = (sem, val)
            else:
                e, ep = key
                if self.waited.get((waiter, e), (-1, 0)) >= (ep, val):
                    continue
                k2 = ("eng", e)
                if need.get(k2, (None, (-1, 0)))[1] < (ep, val):
                    need[k2] = (sem, (ep, val))
        for key, (sem, val) in need.items():
            if key[0] == "dma":
                self.eng[waiter].wait_ge(sem, val)
                self.waited[(waiter, key)] = val
            else:
                self.eng[waiter].wait_ge(sem, val[1])
                self.waited[(waiter, key[1])] = val
            self.nwaits += 1

    @staticmethod
    def _deps(reads, writes):
        toks = []
        for b in reads:
            toks.append(b.w)
        for b in writes:
            toks.append(b.w)
            toks.extend(b.r)
        return toks

    @staticmethod
    def _commit(tok, reads, writes):
        for b in writes:
            b.w = tok
            b.r = []
        for b in reads:
            if b not in writes:
                b.r.append(tok)

    def op(self, e, fn, reads=(), writes=()):
        reads = list(reads)
        writes = list(writes)
        self._wait(e, self._deps(reads, writes))
        ins = fn(self.eng[e])
        n = self.seq[e]
        self.seq[e] += 1
        if e == "pe":
            self.pending[e][n] = ins
        else:
            rank = len(self.signaled[e])
            ins.then_inc(self._sem(e, rank // self.EPOCH), 1)
            self.signaled[e].append(n)
        self._commit((e, n), reads, writes)
        return ins

    def dma(self, q, out, in_, reads=(), writes=()):
        reads = list(reads)
        writes = list(writes)
        if q not in self.dma_sems:
            self.dma_sems[q] = [self.stack.enter_context(self.nc.semaphore(f"dq_{q}_{i}"))
                                for i in range(self.NDMA)]
            self.dma_cnt[q] = [0] * self.NDMA
            self.ndma[q] = 0
        s = self.ndma[q] % self.NDMA
        self.ndma[q] += 1
        toks = self._deps(reads, writes)
        if self.dma_cnt[q][s] > 0:
            toks.append(("dma", q, s, self.dma_cnt[q][s]))
        self._wait(q, toks)
        self.dma_cnt[q][s] += 1
        self.eng[q].dma_start(out=out, in_=in_).then_inc(self.dma_sems[q][s], 16)
        self._commit(("dma", q, s, self.dma_cnt[q][s]), reads, writes)

    def _all_tokens(self):
        toks = []
        for e in self.eng:
            if self.seq[e] > 0:
                toks.append((e, self.seq[e] - 1))
        for q in self.dma_sems:
            for s in range(self.NDMA):
                if self.dma_cnt[q][s] > 0:
                    toks.append(("dma", q, s, self.dma_cnt[q][s]))
        return toks

    def barrier(self, engines=("pe", "act", "dve", "pool", "sp")):
        toks = self._all_tokens()
        for e in engines:
            self._wait(e, [t for t in toks if not (t[0] == e)])

    def finish(self, e):
        self._wait(e, self._all_tokens())


def build_nc(debug_taps=False):
    nc = bass.Bass("TRN2", target_bir_lowering=False)
    d_xT = nc.dram_tensor("xT", [DM, SEQ], F32, kind="ExternalInput").ap()
    d_wfm = nc.dram_tensor("wfm", [20 * 128, 1024], F32, kind="ExternalInput").ap()
    d_wtm = nc.dram_tensor("wtm", [128, 8 * 1040], F32, kind="ExternalInput").ap()
    d_wout = nc.dram_tensor("wout", [8 * 128, 1024], F32, kind="ExternalInput").ap()
    d_wup = nc.dram_tensor("wup", [44 * 128, 1024], F32, kind="ExternalInput").ap()
    d_wdown = nc.dram_tensor("wdown", [8 * 128, 22 * 128], F32, kind="ExternalInput").ap()
    d_wada = nc.dram_tensor("wada", [48 * 128, 1024], F32, kind="ExternalInput").ap()
    d_smalls = nc.dram_tensor("smalls", [128, NSM], F32, kind="ExternalInput").ap()
    d_consts = nc.dram_tensor("consts", [128, 512], F32, kind="ExternalInput").ap()
    d_out = nc.dram_tensor("outT", [DM, SEQ], F32, kind="ExternalOutput").ap()
    d_x1 = nc.dram_tensor("x1s", [DM, SEQ], F32).ap()
    d_dbg = nc.dram_tensor("dbg_ot", [128, 8 * 2048], BF16, kind="ExternalOutput").ap() if debug_taps else None

    def xblk(d, blk):
        return d.rearrange("(c p) t -> p c t", p=128)[:, :, blk * 512:(blk + 1) * 512]

    with ExitStack() as st:
        S = Sched(nc, st)

        def sbt(stack, name, F, dt):
            return stack.enter_context(nc.sbuf_tensor("sb_" + name, [128, F], dt))

        def AP(t, F, off, dims, p0=0, npart=128):
            return bass.AP(t, p0 * F + off, [[F, npart]] + [list(d) for d in dims])

        PS = [st.enter_context(nc.psum_tensor(f"ps{i}", [128, 512], F32)) for i in range(8)]
        PB = [Buf(f"ps{i}") for i in range(8)]

        smalls = sbt(st, "smalls", NSM, F32)
        cf = sbt(st, "cf", 512, F32)
        cb = sbt(st, "cb", 512, BF16)
        onesD = sbt(st, "onesD", 128, BF16)
        onesH = sbt(st, "onesH", 128, BF16)
        cs_bf = sbt(st, "cs_bf", 8, BF16)
        modT = sbt(st, "modT", 48, F32)
        der = sbt(st, "der", 32, F32)
        BIGA = sbt(st, "BIGA", 8 * 2048, BF16)
        B_small, B_cf, B_cb = Buf("smalls"), Buf("cf"), Buf("cb")
        B_ones, B_cs, B_mod, B_der = Buf("ones"), Buf("cs"), Buf("mod"), Buf("der")
        B_H = [[Buf(f"h{c}_{b}") for b in range(4)] for c in range(8)]
        B_OT = [[Buf(f"ot{c}_{q}") for q in range(8)] for c in range(8)]

        ident_f = cf[:, 0:128]
        tri_f = cf[:, 128:256]
        ones_f = cf[:, 256:384]
        ident_b = cb[:, 0:128]
        tri_b = cb[:, 128:256]
        ones_b = cb[:, 256:384]
        trineg_b = cb[:, 384:512]

        kcol = sbt(st, "kcol", 8, F32)
        B_kcol = Buf("kcol")
        S.op("dve", lambda e: e.memset(kcol[:, 0:1], EPS), writes=[B_kcol])
        S.op("dve", lambda e: e.memset(kcol[:, 1:2], EXPSHIFT - 0.5 * math.log(128.0)), writes=[B_kcol])
        S.op("dve", lambda e: e.memset(kcol[:, 3:4], EXPSHIFT), writes=[B_kcol])
        S.op("dve", lambda e: e.memset(kcol[:, 4:5], -EXPSHIFT), writes=[B_kcol])
        S.op("dve", lambda e: e.memset(kcol[:, 2:3], 1.0), writes=[B_kcol])

        def rsqrt_eps(out_ap, in_ap, B_in, B_out):
            S.op("act", lambda e: e.activation(out_ap, in_ap, AF.Ln, bias=kcol[:, 0:1]),
                 reads=[B_kcol], writes=[B_in, B_out])
            S.op("act", lambda e: e.activation(out_ap, out_ap, AF.Exp, scale=-0.5), writes=[B_out])

        S.dma("sp", smalls[:], d_smalls, writes=[B_small])
        S.dma("sp", cf[:], d_consts, writes=[B_cf])
        S.dma("pool", cb[:], d_consts, writes=[B_cb])
        S.op("dve", lambda e: e.memset(onesD[:], 2.0 ** -10), writes=[B_ones])
        S.op("dve", lambda e: e.memset(onesH[:], 2.0 ** -7), writes=[B_ones])

        NW = 4
        WA = [sbt(st, f"wa{i}", 1024, BF16) for i in range(NW)]
        B_WA = [Buf(f"wa{i}") for i in range(NW)]
        B_mod2, B_der2 = Buf("mod2"), Buf("der2")
        S.op("act", lambda e: e.activation(cs_bf[:], smalls[:, C_C:C_C + 8], AF.Silu),
             reads=[B_small], writes=[B_cs])

        B_W16 = [Buf(f"w16_{j}") for j in range(16)]

        def mod_chunk(j, bank, B_m):
            sl = j % NW
            if j < 16:
                wsl, B_wsl = BIGA[:, j * 1024:(j + 1) * 1024], B_W16[j]
            else:
                wsl, B_wsl = WA[sl][:], B_WA[sl]
            S.dma("pool", wsl, d_wada[j * 128:(j + 1) * 128, :], writes=[B_wsl])
            for kc in range(8):
                S.op("pe", lambda e, kc=kc: e.matmul(
                    PS[bank][:, 0:1], wsl[:, kc * 128:(kc + 1) * 128], cs_bf[:, kc:kc + 1],
                    start=(kc == 0), stop=(kc == 7)),
                    reads=[B_wsl, B_cs], writes=[PB[bank]])
            if j < 16:
                S.op("act", lambda e: e.activation(modT[:, j:j + 1], PS[bank][:, 0:1], AF.Identity,
                                                   bias=smalls[:, C_BADA + j:C_BADA + j + 1]),
                     reads=[B_small], writes=[PB[bank], B_m])
            else:
                S.op("dve", lambda e: e.tensor_tensor(modT[:, j:j + 1], PS[bank][:, 0:1],
                                                      smalls[:, C_BADA + j:C_BADA + j + 1], ALU.add),
                     reads=[B_small], writes=[PB[bank], B_m])

        for j in range(16):
            mod_chunk(j, 6 + (j % 2), B_mod)

        def mod_finish():
            S.op("dve", lambda e: e.tensor_tensor(
                der[:, 8:16], modT[:, 16:24], smalls[:, C_GOM:C_GOM + 8], ALU.mult),
                reads=[B_mod2, B_small], writes=[B_der2])
            S.op("dve", lambda e: e.scalar_tensor_tensor(
                der[:, 16:24], modT[:, 32:40], 1.0, smalls[:, C_GPF:C_GPF + 8], ALU.add, ALU.mult),
                reads=[B_mod2, B_small], writes=[B_der2])
            S.op("dve", lambda e: e.tensor_tensor(
                der[:, 24:32], modT[:, 40:48], smalls[:, C_GOF:C_GOF + 8], ALU.mult),
                reads=[B_mod2, B_small], writes=[B_der2])

        def norm_mod(XBt, B_XB, blk, gs0, sh0, SQ, B_SQ, RSTD, B_RSTD, TMP, B_TMP, bank):
            S.op("act", lambda e: e.activation(SQ[:, 0:4096], XBt[:, 0:4096], AF.Square),
                 reads=[B_XB], writes=[B_SQ])
            for kc in range(8):
                S.op("pe", lambda e, kc=kc: e.matmul(
                    PS[bank][:, :], onesD[:, :], SQ[:, kc * 512:(kc + 1) * 512],
                    start=(kc == 0), stop=(kc == 7)),
                    reads=[B_SQ, B_ones], writes=[PB[bank]])
            rsqrt_eps(RSTD[:, 0:512], PS[bank][:, :], PB[bank], B_RSTD)
            for c in range(8):
                sl = c % 2
                S.op("dve", lambda e, c=c, sl=sl: e.tensor_tensor(
                    TMP[sl][:, 0:512], XBt[:, c * 512:(c + 1) * 512], RSTD[:, 0:512], ALU.mult),
                    reads=[B_XB, B_RSTD], writes=[B_TMP[sl]])
                S.op("act", lambda e, c=c, sl=sl: e.activation(
                    BIGA[:, c * 2048 + blk * 512: c * 2048 + (blk + 1) * 512], TMP[sl][:, 0:512],
                    AF.Identity, bias=sh0(c), scale=gs0(c)),
                    reads=[B_TMP[sl], B_der, B_mod, B_der2, B_mod2], writes=[B_H[c][blk]])

        with ExitStack() as phB:
            OT = sbt(phB, "OT", 8 * 2048, BF16)
            GT = sbt(phB, "GT", 256, F32)
            GX = sbt(phB, "GX", 256, F32)
            SPc = sbt(phB, "SPc", 256, F32)
            OFFS = sbt(phB, "OFFS", 17 * 16, F32)
            CS = sbt(phB, "CS", 256, F32)
            MID = sbt(phB, "MID", 128, F32)
            BIASF = sbt(phB, "BIASF", 1024, F32)
            AI = sbt(phB, "AI", 64, F32)
            ARG = sbt(phB, "ARG", 512, F32)
            SCL = sbt(phB, "SCL", 512, F32)
            LBA = sbt(phB, "LBA", 64, F32)
            LB = sbt(phB, "LB", 64, F32)
            GINV = sbt(phB, "GINV", 64, F32)
            B_G = Buf("gates")

            with ExitStack() as ph:
                WTM = sbt(ph, "WTM", 8 * 1040, BF16)
                B_WTM = Buf("wtm")
                S.dma("pool", WTM[:], d_wtm, writes=[B_WTM])
                QK = sbt(ph, "QKf", 4 * 2048, BF16)
                QZ = sbt(ph, "QZf", 8 * 2048, BF16)
                B_QK = [[Buf(f"qk{c}_{b}") for b in range(4)] for c in range(8)]
                B_QZinit = Buf("qzinit")
                VA = sbt(ph, "VA", 16 * 1024, BF16)
                B_VA = [Buf(f"va{j}") for j in range(16)]
                B_VAinit = Buf("vainit")
                NWS = 4
                W, B_W = WA, B_WA

                XBv = [OT[:, 0:8192].bitcast(F32), OT[:, 8192:16384].bitcast(F32)]
                S.op("dve", lambda e: e.memset(VA[:], 1.0), writes=B_VA)
                S.op("dve", lambda e: e.memset(AP(QZ, 16384, 5 * 2048, [[4096, 2], [1, 2048]], p0=0, npart=64), 0.0),
                     writes=[B_QZinit] + [B_QK[c][b] for c in (2, 3) for b in range(4)])
                S.op("dve", lambda e: e.memset(AP(QZ, 16384, 4 * 2048, [[4096, 2], [1, 2048]], p0=64, npart=64), 0.0),
                     writes=[B_QZinit] + [B_QK[c][b] for c in (2, 3) for b in range(4)])
                S.op("dve", lambda e: e.scalar_tensor_tensor(
                    der[:, 0:8], modT[:, 8:16], 1.0, smalls[:, C_GPM:C_GPM + 8], ALU.add, ALU.mult),
                    reads=[B_mod, B_small], writes=[B_der])
                SQv = QZ[:, 0:4096]
                RSTDv = QZ[:, 4096:5120].bitcast(F32)
                TMPv = [QZ[:, 5120:6144].bitcast(F32), QZ[:, 6144:7168].bitcast(F32)]
                B_XBs = [Buf(f"xbA{i}") for i in range(2)]
                B_SQA, B_RSTDA = Buf("sqA"), Buf("rstdA")
                B_TMPA = [Buf(f"tmpA{i}") for i in range(2)]
                A_BUFS = B_XBs + [B_SQA, B_RSTDA] + B_TMPA
                kch = [4, 5, 6, 7]
                for ci, ch in enumerate(kch):
                    S.dma("pool", W[ci][:], d_wfm[ch * 128:(ch + 1) * 128, :], writes=[B_W[ci]])
                RSTDw = [RSTDv, QZ[:, 7168:8192].bitcast(F32)]
                B_RSTDw = [B_RSTDA, Buf("rstdA1")]
                A_BUFS = A_BUFS + [B_RSTDw[1]]

                def a_load(blk):
                    xi = blk % 2
                    S.dma("sp", XBv[xi].rearrange("p (c t) -> p c t", c=8), xblk(d_xT, blk),
                          reads=(B_WA if blk == 1 else []), writes=[B_XBs[xi]])

                def a_stats(blk):
                    xi = blk % 2
                    pbank = 6 + (blk % 2)
                    S.op("act", lambda e: e.activation(SQv[:, 0:4096], XBv[xi][:, 0:4096], AF.Square),
                         reads=[B_XBs[xi]], writes=[B_SQA])
                    for kc in range(8):
                        S.op("pe", lambda e, kc=kc: e.matmul(
                            PS[pbank][:, :], onesD[:, :], SQv[:, kc * 512:(kc + 1) * 512],
                            start=(kc == 0), stop=(kc == 7)),
                            reads=[B_SQA, B_ones], writes=[PB[pbank]])
                    rsqrt_eps(RSTDw[xi][:, 0:512], PS[pbank][:, :], PB[pbank], B_RSTDw[xi])

                def a_h(blk):
                    xi = blk % 2
                    for c in range(8):
                        sl = c % 2
                        S.op("dve", lambda e, c=c, sl=sl: e.tensor_tensor(
                            TMPv[sl][:, 0:512], XBv[xi][:, c * 512:(c + 1) * 512], RSTDw[xi][:, 0:512],
                            ALU.mult), reads=[B_XBs[xi], B_RSTDw[xi]], writes=[B_TMPA[sl]])
                        S.op("act", lambda e, c=c, sl=sl: e.activation(
                            BIGA[:, c * 2048 + blk * 512: c * 2048 + (blk + 1) * 512], TMPv[sl][:, 0:512],
                            AF.Identity, bias=modT[:, c:c + 1], scale=der[:, c:c + 1]),
                            reads=[B_TMPA[sl], B_der, B_mod], writes=[B_H[c][blk]])

                bank = 0
                a_load(0)
                a_load(1)
                a_stats(0)
                for blk in range(4):
                    a_h(blk)
                    groups = []
                    for ci, ch in enumerate(kch):
                        b = bank % 4
                        bank += 1
                        for kc in range(8):
                            S.op("pe", lambda e, ci=ci, kc=kc, blk=blk, b=b: e.matmul(
                                PS[b][:, :], W[ci][:, kc * 128:(kc + 1) * 128],
                                BIGA[:, kc * 2048 + blk * 512: kc * 2048 + (blk + 1) * 512],
                                start=(kc == 0), stop=(kc == 7)),
                                reads=[B_W[ci], B_H[kc][blk]], writes=[PB[b]])
                        groups.append((ci, ch, b))
                    if blk + 2 < 4:
                        a_load(blk + 2)
                    if blk + 1 < 4:
                        a_stats(blk + 1)
                    for ci, ch, b in groups:
                        dst = QK[:, (ch - 4) * 2048 + blk * 512: (ch - 4) * 2048 + (blk + 1) * 512]
                        if ci % 2 == 0:
                            S.op("dve", lambda e, dst=dst, b=b: e.tensor_copy(dst, PS[b][:, :]),
                                 writes=[PB[b], B_QK[ch][blk]])
                        else:
                            S.op("act", lambda e, dst=dst, b=b: e.activation(dst, PS[b][:, :], AF.Copy),
                                 writes=[PB[b], B_QK[ch][blk]])
                S.op("dve", lambda e: e.memset(AP(QZ, 16384, 2048, [[4096, 2], [1, 2048]], p0=0, npart=64), 0.0),
                     writes=[B_QZinit] + A_BUFS + [B_QK[c][b] for c in (0, 1) for b in range(4)])
                S.op("act", lambda e: e.memzero(AP(QZ, 16384, 0, [[4096, 2], [1, 2048]], p0=64, npart=64)),
                     writes=[B_QZinit] + A_BUFS + [B_QK[c][b] for c in (0, 1) for b in range(4)])
                for tt in range(16):
                    for kc in range(8):
                        S.op("pe", lambda e, kc=kc, tt=tt: e.matmul(
                            PS[6][:, tt * 16:(tt + 1) * 16],
                            BIGA[:, kc * 2048 + tt * 128: kc * 2048 + (tt + 1) * 128],
                            WTM[:, kc * 1040 + 1024: kc * 1040 + 1040],
                            start=(kc == 0), stop=(kc == 7)),
                            reads=[B_WTM, B_H[kc][tt // 4]], writes=[PB[6]])
                GQ = []

                def GQ_add(*a, **k):
                    GQ.append((a, k))

                def gq_pop(n):
                    for _ in range(n):
                        if GQ:
                            a, k = GQ.pop(0)
                            S.op(*a, **k)
                GQ_add("dve", lambda e: e.tensor_copy(GT[:, 0:256], PS[6][:, 0:256]),
                     writes=[PB[6], B_G])

                GQ_add("dve", lambda e: e.tensor_tensor(
                    AP(GX, 256, 0, [[16, 16], [1, 16]]), AP(GT, 256, 0, [[16, 16], [1, 16]]),
                    AP(smalls, NSM, C_BG, [[0, 16], [1, 16]]), ALU.add),
                    reads=[B_small], writes=[B_G])
                GQ_add("act", lambda e: e.activation(SPc[:, 0:256], GX[:, 0:256], AF.Exp, scale=-1.0),
                     writes=[B_G])
                GQ_add("act", lambda e: e.activation(SPc[:, 0:256], SPc[:, 0:256], AF.Ln, bias=kcol[:, 2:3]),
                     writes=[B_G])
                GQ_add("pe", lambda e: e.matmul(PS[6][:, 0:256], tri_f, SPc[:, 0:256],
                                              start=True, stop=True),
                     reads=[B_G, B_cf], writes=[PB[6]])
                GQ_add("pe", lambda e: e.matmul(PS[7][:, 0:256], ones_f, SPc[:, 0:256],
                                              start=True, stop=True),
                     reads=[B_G, B_cf], writes=[PB[7]])
                GQ_add("dve", lambda e: e.memset(OFFS[:, 0:16], 0.0), writes=[B_G])
                for j in range(16):
                    GQ_add("dve", lambda e, j=j: e.tensor_tensor(
                        OFFS[:, (j + 1) * 16:(j + 2) * 16], PS[7][:, j * 16:(j + 1) * 16],
                        OFFS[:, j * 16:(j + 1) * 16], ALU.add),
                        writes=[B_G, PB[7]])
                GQ_add("dve", lambda e: e.tensor_tensor(CS[:, 0:256], PS[6][:, 0:256],
                                                      OFFS[:, 0:256], ALU.add),
                     writes=[B_G, PB[6]])
                GQ_add("dve", lambda e: e.tensor_tensor(
                    AP(MID, 128, 0, [[16, 8], [1, 16]]), AP(OFFS, 272, 0, [[32, 8], [1, 16]]),
                    AP(OFFS, 272, 32, [[32, 8], [1, 16]]), ALU.add), writes=[B_G])
                GQ_add("dve", lambda e: e.tensor_scalar(MID[:, 0:128], MID[:, 0:128], 0.5, None,
                                                      ALU.mult), writes=[B_G])
                for j in range(16):
                    GQ_add("dve", lambda e, j=j: e.tensor_tensor(
                        AP(BIASF, 1024, j * 64, [[8, 8], [1, 8]]),
                        AP(CS, 256, j * 16, [[0, 8], [1, 8]]),
                        AP(MID, 128, 0, [[16, 8], [1, 8]]), ALU.subtract), writes=[B_G])
                GQ_add("dve", lambda e: e.tensor_scalar(BIASF[:, 0:1024], BIASF[:, 0:1024], EXPSHIFT, 78.0,
                                                      ALU.add, ALU.min), writes=[B_G])
                GQ_add("dve", lambda e: e.tensor_tensor(
                    AP(AI, 64, 0, [[4, 16], [1, 4]]), AP(CS, 256, 12, [[16, 16], [1, 4]]),
                    AP(GX, 256, 8, [[16, 16], [1, 4]]), ALU.add), writes=[B_G])
                for j in range(16):
                    GQ_add("dve", lambda e, j=j: e.tensor_tensor(
                        AP(ARG, 512, j * 32, [[4, 8], [1, 4]]),
                        AP(AI, 64, j * 4, [[0, 8], [1, 4]]),
                        AP(MID, 128, 12, [[16, 8], [1, 4]]), ALU.subtract), writes=[B_G])
                GQ_add("dve", lambda e: e.tensor_scalar(ARG[:, 0:512], ARG[:, 0:512], 72.0, None,
                                                      ALU.min), writes=[B_G])
                GQ_add("act", lambda e: e.activation(SCL[:, 0:512], ARG[:, 0:512], AF.Exp,
                                                   bias=kcol[:, 1:2]), reads=[B_kcol], writes=[B_G])
                GQ_add("dve", lambda e: e.tensor_tensor(
                    AP(LBA, 64, 0, [[8, 8], [4, 2], [1, 4]]),
                    AP(CS, 256, 12, [[32, 8], [16, 2], [1, 4]]),
                    AP(MID, 128, 12, [[16, 8], [0, 2], [1, 4]]), ALU.subtract), writes=[B_G])
                GQ_add("act", lambda e: e.activation(LB[:, 0:64], LBA[:, 0:64], AF.Exp, bias=kcol[:, 3:4]),
                     reads=[B_kcol], writes=[B_G])
                GQ_add("act", lambda e: e.activation(GINV[:, 0:64], LBA[:, 0:64], AF.Exp, bias=kcol[:, 4:5],
                                                   scale=-1.0), reads=[B_kcol], writes=[B_G])

                for ci, ch in enumerate([2, 3, 0, 1]):
                    sl = ci % NWS
                    S.dma("pool", W[sl][:], d_wfm[ch * 128:(ch + 1) * 128, :], writes=[B_W[sl]])
                    for blk in range(4):
                        b = bank % 4
                        bank += 1
                        for kc in range(8):
                            S.op("pe", lambda e, sl=sl, kc=kc, blk=blk, b=b: e.matmul(
                                PS[b][:, :], W[sl][:, kc * 128:(kc + 1) * 128],
                                BIGA[:, kc * 2048 + blk * 512: kc * 2048 + (blk + 1) * 512],
                                start=(kc == 0), stop=(kc == 7)),
                                reads=[B_W[sl], B_H[kc][blk]], writes=[PB[b]])
                        for hh in range(2):
                            hq = 2 * ch + hh
                            S.op("act", lambda e, b=b, hh=hh, hq=hq, blk=blk: e.activation(
                                QZ[hh * 64:(hh + 1) * 64, hq * 2048 + blk * 512: hq * 2048 + (blk + 1) * 512],
                                PS[b][hh * 64:(hh + 1) * 64, :], AF.Identity, scale=0.125),
                                writes=[PB[b], B_QK[ch][blk]])
                        gq_pop(3)
                for tt in range(16):
                    b = 4 + (tt % 2)
                    for kc in range(8):
                        S.op("pe", lambda e, kc=kc, tt=tt, b=b: e.matmul(
                            PS[b][:, :], BIGA[:, kc * 2048 + tt * 128: kc * 2048 + (tt + 1) * 128],
                            WTM[:, kc * 1040: kc * 1040 + 512],
                            start=(kc == 0), stop=(kc == 7)),
                            reads=[B_WTM, B_H[kc][tt // 4]], writes=[PB[b]])
                    S.op("act", lambda e, tt=tt, b=b: e.activation(
                        AP(VA, 16384, tt * 1024, [[256, 4], [1, 64]]),
                        AP(PS[b], 512, 0, [[128, 4], [1, 64]]), AF.Copy),
                        writes=[PB[b], B_VA[tt]])
                    S.op("dve", lambda e, tt=tt, b=b: e.tensor_copy(
                        AP(VA, 16384, tt * 1024 + 128 + 64, [[256, 4], [1, 64]]),
                        AP(PS[b], 512, 64, [[128, 4], [1, 64]])),
                        writes=[PB[b], B_VA[tt]])
                    gq_pop(3)
                gq_pop(10 ** 6)
                PT = [sbt(ph, f"ptf{i}", 256, BF16) for i in range(4)]
                B_PT = [Buf(f"ptf{i}") for i in range(4)]
                RD = [sbt(ph, f"rdf{i}", 256, F32) for i in range(2)]
                B_RD = [Buf(f"rdf{i}") for i in range(2)]
                LA = 3
                items = [(h, qb, j) for h in range(8) for qb in range(8) for j in range(2 * qb + 2)]

                def fox_front(idx):
                    h, qb, j = items[idx]
                    c = h // 2
                    p0 = (h % 2) * 64
                    last = (j == 2 * qb + 1)
                    diag = (j >= 2 * qb)
                    q0 = qb * 256 + (128 if last else 0)
                    nq = 128 if last else 256
                    sbk = idx % 4
                    pt = idx % 4
                    S.op("pe", lambda e: e.matmul(
                        PS[sbk][:, 0:nq],
                        QK[:, c * 2048 + j * 128: c * 2048 + (j + 1) * 128],
                        QZ[:, h * 2048 + q0: h * 2048 + q0 + nq],
                        start=True, stop=(not diag)),
                        reads=[B_QK[4 + c][j // 4], B_QK[c][q0 // 512]], writes=[PB[sbk]])
                    if diag:
                        S.op("pe", lambda e: e.matmul(
                            PS[sbk][:, 0:128], ident_b, trineg_b, start=False, stop=True),
                            reads=[B_cb], writes=[PB[sbk]])
                    bcol = j * 64 + qb * 8 + h
                    S.op("act", lambda e: e.activation(
                        PT[pt][:, 0:nq], PS[sbk][:, 0:nq], AF.Exp, bias=BIASF[:, bcol:bcol + 1]),
                        reads=[B_G], writes=[PB[sbk], B_PT[pt]])

                def fox_back(idx):
                    h, qb, j = items[idx]
                    c = h // 2
                    p0 = (h % 2) * 64
                    last = (j == 2 * qb + 1)
                    nq = 128 if last else 256
                    pt = idx % 4
                    ob = 4 + (qb % 2)
                    oc0 = 128 if last else 0
                    S.op("pe", lambda e: e.matmul(
                        PS[ob][:, oc0:oc0 + nq],
                        VA[:, (j * 8 + h) * 128:(j * 8 + h + 1) * 128], PT[pt][:, 0:nq],
                        start=(j == 0), stop=last),
                        reads=[B_VA[j], B_PT[pt]], writes=[PB[ob]])
                    if last:
                        dp0 = 64 - p0
                        rd = qb % 2
                        S.op("dve", lambda e: e.reciprocal(
                            RD[rd][p0:p0 + 64, 0:256], PS[ob][dp0:dp0 + 64, 0:256]),
                            writes=[PB[ob], B_RD[rd]])
                        S.op("dve", lambda e: e.tensor_tensor(
                            OT[p0:p0 + 64, c * 2048 + qb * 256: c * 2048 + (qb + 1) * 256],
                            PS[ob][p0:p0 + 64, 0:256], RD[rd][p0:p0 + 64, 0:256], ALU.mult),
                            reads=[B_RD[rd]], writes=[PB[ob], B_OT[c][qb]])

                for idx in range(len(items) + LA):
                    if idx < len(items):
                        fox_front(idx)
                    if idx >= LA:
                        fox_back(idx - LA)
                    if idx % 17 == 5 and idx // 17 < 32:
                        mod_chunk(16 + idx // 17, 6 + ((idx // 17) % 2), B_mod2)
                mod_finish()
                for ch_ in range(3):
                    S.dma("pool", WA[ch_][:], d_wfm[(8 + ch_) * 128:(9 + ch_) * 128, :], writes=[B_WA[ch_]])
                S.barrier(engines=("act", "dve", "pool", "sp"))

            WO = sbt(phB, "WO", 8 * 1024, BF16)
            B_WO = [Buf(f"wo{m}") for m in range(8)]
            with ExitStack() as ph:
                QKm = sbt(ph, "QKm", 8 * 2048, BF16)
                B_QKm = [[Buf(f"qkm{c}_{b}") for b in range(4)] for c in range(8)]
                MV = sbt(ph, "MV", 16 * 512, BF16)
                B_MV = [Buf(f"mv{j}") for j in range(16)]
                MO = sbt(ph, "MO", 4 * 2048, BF16)
                B_MO = [[Buf(f"mo{c}_{b}") for b in range(4)] for c in range(4)]
                with ExitStack() as ph3:
                    WTM = sbt(ph3, "WTM2", 8 * 512, BF16)
                    B_WTM = Buf("wtm2")
                    S.dma("pool", AP(WTM, 4096, 0, [[512, 8], [1, 512]]),
                          d_wtm.rearrange("p (k c) -> p k c", c=1040)[:, :, 512:1024], writes=[B_WTM])
                    NWS = 3
                    W, B_W = WA, B_WA
                    PRE = [sbt(ph3, f"pre{i}", 2048 + 3, BF16) for i in range(2)]
                    B_PRE = [[Buf(f"pre{i}_{b}") for b in range(4)] for i in range(2)]
                    B_PREpad = [Buf(f"prepad{i}") for i in range(2)]
                    DG = [sbt(ph3, f"dg{i}", 512, BF16) for i in range(2)]
                    B_DG = [Buf(f"dg{i}") for i in range(2)]
                    QF = [sbt(ph3, f"qf{i}", 512, F32) for i in range(2)]
                    B_QF = [Buf(f"qf{i}") for i in range(2)]
                    DGQ = [sbt(ph3, f"dgq{i}", 128, F32) for i in range(8)]
                    B_DGQ = [Buf(f"dgq{i}") for i in range(8)]
                    qfc = [0]
                    dgc = [0]
                    GBANKS = [5, 6, 7]
                    for i in range(2):
                        S.op("pool", lambda e, i=i: e.memset(PRE[i][:, 0:3], 0.0), writes=[B_PREpad[i]])
                    bank = 0
                    for ch in range(8):
                        sl = ch % NWS
                        pr = ch % 2
                        if ch >= 3:
                            S.dma("pool", W[sl][:], d_wfm[(8 + ch) * 128:(9 + ch) * 128, :], writes=[B_W[sl]])
                        for tp in range(4):
                            S.op("dve", lambda e, pr=pr, tp=tp, ch=ch: e.tensor_scalar(
                                DG[pr][:, tp * 128:(tp + 1) * 128], ident_b,
                                smalls[:, C_MCW + ch * 4 + tp: C_MCW + ch * 4 + tp + 1], None, ALU.mult),
                                reads=[B_cb, B_small], writes=[B_DG[pr]])

                        def g_dve(ch_, blk_):
                            for k4 in range(4):
                                tt = blk_ * 4 + k4
                                dgi = (blk_ % 2) * 4 + k4
                                S.op("dve", lambda e, dgi=dgi, tt=tt: e.tensor_scalar(
                                    DGQ[dgi][:, 0:128], ident_f, GINV[:, tt * 4 + ch_: tt * 4 + ch_ + 1], None,
                                    ALU.mult), reads=[B_cf, B_G], writes=[B_DGQ[dgi]])

                        def g_pe(ch_, blk_):
                            gb_ = GBANKS[(ch_ * 4 + blk_) % 3]
                            for k4 in range(4):
                                dgi = (blk_ % 2) * 4 + k4
                                S.op("pe", lambda e, dgi=dgi, k4=k4: e.matmul(
                                    PS[gb_][:, k4 * 128:(k4 + 1) * 128], ones_f, DGQ[dgi][:, 0:128],
                                    start=True, stop=True),
                                    reads=[B_cf, B_DGQ[dgi]], writes=[PB[gb_]])
                        if ch < 4:
                            g_dve(ch, 0)
                            g_dve(ch, 1)
                        for blk in range(4):
                            b = bank % 3
                            bank += 1
                            for kc in range(8):
                                S.op("pe", lambda e, sl=sl, kc=kc, blk=blk, b=b: e.matmul(
                                    PS[b][:, :], W[sl][:, kc * 128:(kc + 1) * 128],
                                    BIGA[:, kc * 2048 + blk * 512: kc * 2048 + (blk + 1) * 512],
                                    start=(kc == 0), stop=(kc == 7)),
                                    reads=[B_W[sl], B_H[kc][blk]], writes=[PB[b]])
                            dst = PRE[pr][:, 3 + blk * 512: 3 + (blk + 1) * 512]
                            if blk % 2 == 0:
                                S.op("act", lambda e, dst=dst, b=b: e.activation(dst, PS[b][:, :], AF.Copy),
                                     writes=[PB[b], B_PRE[pr][blk]])
                            else:
                                S.op("dve", lambda e, dst=dst, b=b: e.tensor_copy(dst, PS[b][:, :]),
                                     writes=[PB[b], B_PRE[pr][blk]])
                        for blk in range(4):
                            b = 3 + (blk % 2)
                            rds = [B_DG[pr], B_PRE[pr][blk], B_PREpad[pr]]
                            if blk > 0:
                                rds.append(B_PRE[pr][blk - 1])
                            for tp in range(4):
                                S.op("pe", lambda e, pr=pr, tp=tp, blk=blk, b=b: e.matmul(
                                    PS[b][:, :], DG[pr][:, tp * 128:(tp + 1) * 128],
                                    PRE[pr][:, blk * 512 + tp: blk * 512 + tp + 512],
                                    start=(tp == 0), stop=(tp == 3)),
                                    reads=rds, writes=[PB[b]])
                            if ch >= 4:
                                S.op("act", lambda e, ch=ch, blk=blk, b=b: e.activation(
                                    QKm[:, ch * 2048 + blk * 512: ch * 2048 + (blk + 1) * 512], PS[b][:, :],
                                    AF.Silu, bias=smalls[:, C_MCB + ch: C_MCB + ch + 1]),
                                    reads=[B_small], writes=[PB[b], B_QKm[ch][blk]])
                            else:
                                qf = qfc[0] % 2
                                qfc[0] += 1
                                gb = GBANKS[(ch * 4 + blk) % 3]
                                S.op("act", lambda e, ch=ch, b=b, qf=qf: e.activation(
                                    QF[qf][:, 0:512], PS[b][:, :],
                                    AF.Silu, bias=smalls[:, C_MCB + ch: C_MCB + ch + 1]),
                                    reads=[B_small], writes=[PB[b], B_QF[qf]])
                                if blk == 0:
                                    g_pe(ch, 0)
                                if blk < 3:
                                    g_pe(ch, blk + 1)
                                S.op("dve", lambda e, ch=ch, blk=blk, qf=qf, gb=gb: e.tensor_tensor(
                                    QKm[:, ch * 2048 + blk * 512: ch * 2048 + (blk + 1) * 512],
                                    QF[qf][:, 0:512], PS[gb][:, 0:512], ALU.mult),
                                    reads=[B_QF[qf]], writes=[PB[gb], B_QKm[ch][blk]])
                                if blk < 2:
                                    g_dve(ch, blk + 2)
                    for ch in range(4):
                        sl = (ch + 2) % NWS
                        S.dma("pool", W[sl][:], d_wfm[(16 + ch) * 128:(17 + ch) * 128, :], writes=[B_W[sl]])
                        for blk in range(4):
                            b = bank % 3
                            bank += 1
                            for kc in range(8):
                                S.op("pe", lambda e, sl=sl, kc=kc, blk=blk, b=b: e.matmul(
                                    PS[b][:, :], W[sl][:, kc * 128:(kc + 1) * 128],
                                    BIGA[:, kc * 2048 + blk * 512: kc * 2048 + (blk + 1) * 512],
                                    start=(kc == 0), stop=(kc == 7)),
                                    reads=[B_W[sl], B_H[kc][blk]], writes=[PB[b]])
                            S.op("act", lambda e, ch=ch, blk=blk, b=b: e.activation(
                                MO[:, ch * 2048 + blk * 512: ch * 2048 + (blk + 1) * 512], PS[b][:, :],
                                AF.Sigmoid), writes=[PB[b], B_MO[ch][blk]])
                    for tt in range(16):
                        b = 5 + (tt % 2)
                        for kc in range(8):
                            S.op("pe", lambda e, kc=kc, tt=tt, b=b: e.matmul(
                                PS[b][:, :], BIGA[:, kc * 2048 + tt * 128: kc * 2048 + (tt + 1) * 128],
                                WTM[:, kc * 512:(kc + 1) * 512],
                                start=(kc == 0), stop=(kc == 7)),
                                reads=[B_WTM, B_H[kc][tt // 4]], writes=[PB[b]])
                        if tt % 2 == 0:
                            S.op("dve", lambda e, tt=tt, b=b: e.tensor_copy(
                                MV[:, tt * 512:(tt + 1) * 512], PS[b][:, :]), writes=[PB[b], B_MV[tt]])
                        else:
                            S.op("act", lambda e, tt=tt, b=b: e.activation(
                                MV[:, tt * 512:(tt + 1) * 512], PS[b][:, :], AF.Copy),
                                writes=[PB[b], B_MV[tt]])

                    S.barrier(engines=("act", "dve", "pool", "sp"))
                for m in range(8):
                    S.dma("pool", WO[:, m * 1024:(m + 1) * 1024], d_wout[m * 128:(m + 1) * 128, :],
                          writes=[B_WO[m]])
                PT = [sbt(ph, f"ptm{i}", 256, BF16) for i in range(6)]
                B_PT = [Buf(f"ptm{i}") for i in range(6)]
                DGL = [sbt(ph, f"dgl{i}", 128, F32) for i in range(6)]
                B_DGL = [Buf(f"dgl{i}") for i in range(6)]
                SBANKS = [0, 1, 2, 5]
                NDBANKS = [3, 4, 6]
                T1 = [sbt(ph, f"T1_{i}", 256, F32) for i in range(3)]
                HM = [sbt(ph, f"HM_{i}", 256, F32) for i in range(3)]
                SQh = [sbt(ph, f"SQh_{i}", 256, BF16) for i in range(3)]
                RS = [sbt(ph, f"RS_{i}", 256, F32) for i in range(3)]
                T2 = [sbt(ph, f"T2_{i}", 256, F32) for i in range(3)]
                B_T1 = [Buf(f"T1_{i}") for i in range(3)]
                B_HM = [Buf(f"HM_{i}") for i in range(3)]
                B_SQh = [Buf(f"SQh_{i}") for i in range(3)]
                B_RS = [Buf(f"RS_{i}") for i in range(3)]
                B_T2 = [Buf(f"T2_{i}") for i in range(3)]
                LA = 4
                blocks = [(hm, qb) for hm in range(4) for qb in range(8)]
                items = []
                for n, (hm, qb) in enumerate(blocks):
                    for j in range(2 * qb + 2):
                        items.append((n, hm, qb, j))
                sctr = [0]
                ss_bank = {}

                def m_front(idx):
                    n, hm, qb, j = items[idx]
                    last = (j == 2 * qb + 1)
                    diag = (j >= 2 * qb)
                    q0 = qb * 256 + (128 if last else 0)
                    nq = 128 if last else 256
                    sbk = SBANKS[sctr[0] % 4]
                    sctr[0] += 1
                    pt = idx % 6
                    S.op("pe", lambda e: e.matmul(
                        PS[sbk][:, 0:nq],
                        QKm[:, (4 + hm) * 2048 + j * 128:(4 + hm) * 2048 + (j + 1) * 128],
                        QKm[:, hm * 2048 + q0: hm * 2048 + q0 + nq],
                        start=True, stop=True),
                        reads=[B_QKm[4 + hm][j // 4], B_QKm[hm][q0 // 512]], writes=[PB[sbk]])
                    scol = SCL[:, j * 32 + qb * 4 + hm: j * 32 + qb * 4 + hm + 1]
                    if diag:
                        S.op("dve", lambda e: e.scalar_tensor_tensor(
                            PT[pt][:, 0:128], tri_f, scol, PS[sbk][:, 0:128], ALU.mult, ALU.mult),
                            reads=[B_G, B_cf], writes=[PB[sbk], B_PT[pt]])
                        if nq == 256:
                            S.op("act", lambda e: e.activation(
                                PT[pt][:, 128:256], PS[sbk][:, 128:256], AF.Identity, scale=scol),
                                reads=[B_G], writes=[PB[sbk], B_PT[pt]])
                    elif idx % 2 == 0:
                        S.op("act", lambda e: e.activation(
                            PT[pt][:, 0:nq], PS[sbk][:, 0:nq], AF.Identity, scale=scol),
                            reads=[B_G], writes=[PB[sbk], B_PT[pt]])
                    else:
                        S.op("dve", lambda e: e.tensor_scalar(
                            PT[pt][:, 0:nq], PS[sbk][:, 0:nq], scol, None, ALU.mult),
                            reads=[B_G], writes=[PB[sbk], B_PT[pt]])

                def m_back(idx):
                    n, hm, qb, j = items[idx]
                    last = (j == 2 * qb + 1)
                    nq = 128 if last else 256
                    pt = idx % 6
                    nbk = NDBANKS[n % 3]
                    oc0 = 128 if last else 0
                    S.op("pe", lambda e: e.matmul(
                        PS[nbk][:, oc0:oc0 + nq],
                        MV[:, j * 512 + hm * 128: j * 512 + (hm + 1) * 128], PT[pt][:, 0:nq],
                        start=(j == 0), stop=last, skip_group_check=True),
                        reads=[B_MV[j], B_PT[pt]], writes=[PB[nbk]])
                    S.op("pe", lambda e: e.matmul(
                        PS[nbk][:, 256 + oc0:256 + oc0 + nq], ones_b, PT[pt][:, 0:nq],
                        start=False, stop=last, skip_group_check=True),
                        reads=[B_cb, B_PT[pt]], writes=[PB[nbk]])

                def m_lb_pool(n):
                    hm, qb = blocks[n]
                    for k2 in range(2):
                        tt = 2 * qb + k2
                        dg = 2 * (n % 3) + k2
                        S.op("pool", lambda e: e.tensor_scalar(
                            DGL[dg][:, 0:128], ident_f, LB[:, tt * 4 + hm: tt * 4 + hm + 1], None,
                            ALU.mult), reads=[B_cf, B_G], writes=[B_DGL[dg]])

                def m_lb(n):
                    par = n % 2
                    for k2 in range(2):
                        dg = 2 * (n % 3) + k2
                        S.op("pe", lambda e: e.matmul(
                            PS[7][:, par * 256 + k2 * 128: par * 256 + (k2 + 1) * 128], ones_f,
                            DGL[dg][:, 0:128], start=True, stop=True),
                            reads=[B_cf, B_DGL[dg]], writes=[PB[7]])

                def m_stages(n):
                    hm, qb = blocks[n]
                    par = n % 2
                    st3 = n % 3
                    nbk = NDBANKS[st3]
                    sb = {}

                    def a0():
                        S.op("act", lambda e: e.activation(T1[st3][:, 0:256], PS[nbk][:, 256:512], AF.Abs),
                             writes=[PB[nbk], B_T1[st3]])

                    def a1():
                        S.op("dve", lambda e: e.tensor_scalar(
                            T1[st3][:, 0:256], T1[st3][:, 0:256], 1.0, None, ALU.max), writes=[B_T1[st3]])

                    def b0():
                        S.op("act", lambda e: e.activation(T1[st3][:, 0:256], T1[st3][:, 0:256], AF.Ln),
                             writes=[B_T1[st3]])

                    def b1():
                        S.op("act", lambda e: e.activation(T1[st3][:, 0:256], T1[st3][:, 0:256], AF.Exp,
                                                           scale=-1.0), writes=[B_T1[st3]])

                    def c0():
                        S.op("dve", lambda e: e.tensor_tensor(
                            HM[st3][:, 0:256], PS[nbk][:, 0:256], T1[st3][:, 0:256], ALU.mult),
                            reads=[B_T1[st3]], writes=[PB[nbk], B_HM[st3]])

                    def d0():
                        S.op("act", lambda e: e.activation(SQh[st3][:, 0:256], HM[st3][:, 0:256], AF.Square),
                             reads=[B_HM[st3]], writes=[B_SQh[st3]])

                    def e0():
                        sbk = SBANKS[sctr[0] % 4]
                        sctr[0] += 1
                        sb["b"] = sbk
                        S.op("pe", lambda e: e.matmul(PS[sbk][:, 0:256], onesH[:, :], SQh[st3][:, 0:256],
                                                      start=True, stop=True),
                             reads=[B_ones, B_SQh[st3]], writes=[PB[sbk]])

                    def e1():
                        sbk = sb["b"]
                        S.op("act", lambda e: e.activation(RS[st3][:, 0:256], PS[sbk][:, 0:256], AF.Ln,
                                                           bias=kcol[:, 0:1]),
                             reads=[B_kcol], writes=[PB[sbk], B_RS[st3]])

                    def f0():
                        S.op("act", lambda e: e.activation(RS[st3][:, 0:256], RS[st3][:, 0:256], AF.Exp,
                                                           scale=-0.5), writes=[B_RS[st3]])

                    def g0():
                        S.op("dve", lambda e: e.tensor_tensor(
                            T2[st3][:, 0:256], HM[st3][:, 0:256], RS[st3][:, 0:256], ALU.mult),
                            reads=[B_HM[st3], B_RS[st3]], writes=[B_T2[st3]])

                    def g1():
                        S.op("dve", lambda e: e.scalar_tensor_tensor(
                            OT[:, (4 + hm) * 2048 + qb * 256:(4 + hm) * 2048 + (qb + 1) * 256],
                            T2[st3][:, 0:256], smalls[:, C_GMH + hm: C_GMH + hm + 1],
                            MO[:, hm * 2048 + qb * 256: hm * 2048 + (qb + 1) * 256],
                            ALU.mult, ALU.mult),
                            reads=[B_T2[st3], B_small, B_MO[hm][qb // 2]], writes=[B_OT[4 + hm][qb]])
                    return [(0, a0), (1, a1), (2, b0), (3, b1), (4, c0), (5, d0), (6, e0), (7, e1),
                            (8, f0), (9, g0), (10, g1)]

                first_idx = {}
                last_idx = {}
                for idx, (n, hm, qb, j) in enumerate(items):
                    first_idx.setdefault(n, idx)
                    last_idx[n] = idx
                lb_after_front = {}
                for n in range(len(blocks) - 1):
                    lb_after_front[first_idx[n] + LA] = n + 1
                e1_after_back = {last_idx[n]: n for n in range(len(blocks))}
                pending = []
                NIT = len(items)
                it = 0
                while it < NIT + LA or pending:
                    if it < NIT:
                        m_front(it)
                    due = sorted([p for p in pending if p[0] <= it], key=lambda p: (p[0], p[1]))
                    for p in due:
                        p[2]()
                        pending.remove(p)
                    if LA <= it < NIT + LA:
                        m_back(it - LA)
                        if (it - LA) in e1_after_back:
                            n = e1_after_back[it - LA]
                            for k, (dly, fn) in enumerate(m_stages(n)):
                                if dly == 0:
                                    fn()
                                else:
                                    pending.append((it + dly, n * 10 + k, fn))
                    it += 1
                S.barrier(engines=("act", "dve", "pool", "sp"))
            S.barrier(engines=("act", "dve", "pool", "sp"))

            if debug_taps:
                S.dma("sp", d_dbg, OT[:], reads=[B_OT[c][q] for c in range(8) for q in range(8)])
            with ExitStack() as ph:
                MIX = [sbt(ph, f"MIX{i}", 4096, F32) for i in range(2)]
                B_MIX = [[Buf(f"mix{i}_{m}") for m in range(8)] for i in range(2)]
                SQ1 = sbt(ph, "sqC", 4096, BF16)
                B_SQ1 = [Buf(f"sqC_{m}") for m in range(8)]
                SQ2 = sbt(ph, "sqC2", 4096, BF16)
                B_SQ2 = Buf("sqC2")
                XB = [sbt(ph, f"xbC{i}", 4096, F32) for i in range(2)]
                B_XBs = [Buf(f"xbC{i}") for i in range(2)]
                RSTD = sbt(ph, "rstdC", 512, F32)
                B_RSTD = Buf("rstdC")
                RSTD2 = sbt(ph, "rstdC2", 512, F32)
                B_RSTD2 = Buf("rstdC2")
                TMPx = [sbt(ph, f"tmpCx{i}", 512, F32) for i in range(2)]
                B_TMPx = [Buf(f"tmpCx{i}") for i in range(2)]
                TMPs = [sbt(ph, f"tmpCs{i}", 512, F32) for i in range(2)]
                B_TMPs = [Buf(f"tmpCs{i}") for i in range(2)]

                def c_chain(blk):
                    i = blk % 2
                    ops = []

                    def load():
                        S.dma("sp", AP(XB[i], 4096, 0, [[512, 8], [1, 512]]), xblk(d_xT, blk),
                              writes=[B_XBs[i]])

                    def s1(m):
                        def f():
                            b = m % 4
                            for kc in range(8):
                                S.op("pe", lambda e, kc=kc: e.matmul(
                                    PS[b][:, :], WO[:, m * 1024 + kc * 128: m * 1024 + (kc + 1) * 128],
                                    OT[:, kc * 2048 + blk * 512: kc * 2048 + (blk + 1) * 512],
                                    start=(kc == 0), stop=(kc == 7)),
                                    reads=[B_WO[m], B_OT[kc][2 * blk], B_OT[kc][2 * blk + 1]],
                                    writes=[PB[b]])
                            S.op("act", lambda e: e.activation(
                                MIX[i][:, m * 512:(m + 1) * 512], PS[b][:, :], AF.Copy),
                                writes=[PB[b], B_MIX[i][m]])
                            S.op("act", lambda e: e.activation(
                                SQ1[:, m * 512:(m + 1) * 512], PS[b][:, :], AF.Square),
                                writes=[PB[b], B_SQ1[m]])
                        return f

                    def ss_ln():
                        for m in range(8):
                            S.op("pe", lambda e, m=m: e.matmul(
                                PS[4][:, :], onesD[:, :], SQ1[:, m * 512:(m + 1) * 512],
                                start=(m == 0), stop=(m == 7)),
                                reads=[B_SQ1[m], B_ones], writes=[PB[4]])
                        S.op("act", lambda e: e.activation(RSTD[:, 0:512], PS[4][:, :], AF.Ln, bias=kcol[:, 0:1]),
                             reads=[B_kcol], writes=[PB[4], B_RSTD])

                    def rs_exp():
                        S.op("act", lambda e: e.activation(RSTD[:, 0:512], RSTD[:, 0:512], AF.Exp, scale=-0.5),
                             writes=[B_RSTD])

                    def x1(m):
                        def f():
                            sl = m % 2
                            S.op("dve", lambda e: e.tensor_tensor(
                                TMPx[sl][:, 0:512], MIX[i][:, m * 512:(m + 1) * 512], RSTD[:, 0:512], ALU.mult),
                                reads=[B_MIX[i][m], B_RSTD], writes=[B_TMPx[sl]])
                            S.op("dve", lambda e: e.scalar_tensor_tensor(
                                XB[i][:, m * 512:(m + 1) * 512], TMPx[sl][:, 0:512], der[:, 8 + m: 9 + m],
                                XB[i][:, m * 512:(m + 1) * 512], ALU.mult, ALU.add),
                                reads=[B_TMPx[sl], B_der2], writes=[B_XBs[i]])
                        return f

                    def store_sq():
                        S.dma("sp", xblk(d_x1, blk), AP(XB[i], 4096, 0, [[512, 8], [1, 512]]),
                              reads=[B_XBs[i]])
                        S.op("act", lambda e: e.activation(SQ2[:, 0:4096], XB[i][:, 0:4096], AF.Square),
                             reads=[B_XBs[i]], writes=[B_SQ2])

                    def ss2_ln():
                        for kc in range(8):
                            S.op("pe", lambda e, kc=kc: e.matmul(
                                PS[5][:, :], onesD[:, :], SQ2[:, kc * 512:(kc + 1) * 512],
                                start=(kc == 0), stop=(kc == 7)),
                                reads=[B_SQ2, B_ones], writes=[PB[5]])
                        S.op("act", lambda e: e.activation(RSTD2[:, 0:512], PS[5][:, :], AF.Ln, bias=kcol[:, 0:1]),
                             reads=[B_kcol], writes=[PB[5], B_RSTD2])

                    def rs2_exp():
                        S.op("act", lambda e: e.activation(RSTD2[:, 0:512], RSTD2[:, 0:512], AF.Exp, scale=-0.5),
                             writes=[B_RSTD2])

                    def s2b(c):
                        def f():
                            sl = c % 2
                            S.op("dve", lambda e: e.tensor_tensor(
                                TMPs[sl][:, 0:512], XB[i][:, c * 512:(c + 1) * 512], RSTD2[:, 0:512], ALU.mult),
                                reads=[B_XBs[i], B_RSTD2], writes=[B_TMPs[sl]])
                            S.op("act", lambda e: e.activation(
                                BIGA[:, c * 2048 + blk * 512: c * 2048 + (blk + 1) * 512], TMPs[sl][:, 0:512],
                                AF.Identity, bias=modT[:, 24 + c:25 + c], scale=der[:, 16 + c:17 + c]),
                                reads=[B_TMPs[sl], B_der2, B_mod2], writes=[B_H[c][blk]])
                        return f

                    noop = lambda: None
                    ops += [s1(m) for m in range(8)]
                    ops[5 if blk >= 2 else 0] = (lambda g=ops[5 if blk >= 2 else 0]: (load(), g()))
                    ops += [ss_ln, rs_exp]
                    ops += [x1(m) for m in range(8)]
                    ops += [store_sq, ss2_ln, rs2_exp]
                    ops += [s2b(c) for c in range(8)]
                    return ops

                chains = [c_chain(blk) for blk in range(4)]
                STAG = 12
                for tick in range(3 * STAG + 29):
                    for blk in range(4):
                        k = tick - blk * STAG
                        if 0 <= k < 29:
                            chains[blk][k]()
                for k_, ch_ in enumerate([0, 22, 1, 23]):
                    S.dma("pool", WA[k_][:], d_wup[ch_ * 128:(ch_ + 1) * 128, :], writes=[B_WA[k_]])
                S.barrier(engines=("act", "dve", "pool", "sp"))

        with ExitStack() as ph:
            G = sbt(ph, "G", 22 * 2048, BF16)
            B_Gc = [[Buf(f"g{i}_{b}") for b in range(4)] for i in range(22)]
            WD2 = sbt(ph, "wd2", 3 * 2816, BF16)
            B_WD = [Buf(f"wd{m}") for m in range(8)]
            with ExitStack() as ph2:
                NWS = 4
                W, B_W = WA, B_WA
                PRE = [sbt(ph2, f"pref{i}", 2048 + 2, BF16) for i in range(4)]
                B_PRE = [[Buf(f"pref{i}_{b}") for b in range(4)] for i in range(4)]
                B_PREpad = [Buf(f"prefpad{i}") for i in range(4)]
                DG = [sbt(ph2, f"dgf{i}", 384, BF16) for i in range(4)]
                B_DG = [Buf(f"dgf{i}") for i in range(4)]
                SAFt = sbt(ph2, "SAFt", 2048, F32)
                B_SAFt = [Buf(f"saft{b}") for b in range(4)]
                for i in range(4):
                    S.op("pool", lambda e, i=i: e.memset(PRE[i][:, 0:2], 0.0), writes=[B_PREpad[i]])
                bank = 0
                wcnt = 0
                for i in range(22):
                    if i == 3:
                        for m in range(5, 8):
                            S.dma("pool", WD2[:, (m - 5) * 2816:(m - 4) * 2816],
                                  d_wdown[m * 128:(m + 1) * 128, :], writes=[B_WD[m]])
                    for half in range(2):
                        ch = i + 22 * half
                        sl = wcnt % NWS
                        pr = wcnt % 4
                        wcnt += 1
                        if wcnt > 4:
                            S.dma("pool", W[sl][:], d_wup[ch * 128:(ch + 1) * 128, :], writes=[B_W[sl]])
                        for tp in range(3):
                            S.op("dve", lambda e, pr=pr, tp=tp, ch=ch: e.tensor_scalar(
                                DG[pr][:, tp * 128:(tp + 1) * 128], ident_b,
                                smalls[:, C_FCW + ch * 3 + tp: C_FCW + ch * 3 + tp + 1], None, ALU.mult),
                                reads=[B_cb, B_small], writes=[B_DG[pr]])
                        for blk in range(4):
                            b = bank % 4
                            bank += 1
                            for kc in range(8):
                                S.op("pe", lambda e, sl=sl, kc=kc, blk=blk, b=b: e.matmul(
                                    PS[b][:, :], W[sl][:, kc * 128:(kc + 1) * 128],
                                    BIGA[:, kc * 2048 + blk * 512: kc * 2048 + (blk + 1) * 512],
                                    start=(kc == 0), stop=(kc == 7)),
                                    reads=[B_W[sl], B_H[kc][blk]], writes=[PB[b]])
                            dst = PRE[pr][:, 2 + blk * 512: 2 + (blk + 1) * 512]
                            if blk % 2 == 0:
                                S.op("act", lambda e, dst=dst, b=b: e.activation(dst, PS[b][:, :], AF.Copy),
                                     writes=[PB[b], B_PRE[pr][blk]])
                            else:
                                S.op("dve", lambda e, dst=dst, b=b: e.tensor_copy(dst, PS[b][:, :]),
                                     writes=[PB[b], B_PRE[pr][blk]])
                        for blk in range(4):
                            b = 4 + 2 * half + (blk % 2)
                            rds = [B_DG[pr], B_PRE[pr][blk], B_PREpad[pr]]
                            if blk > 0:
                                rds.append(B_PRE[pr][blk - 1])
                            for tp in range(3):
                                S.op("pe", lambda e, pr=pr, tp=tp, blk=blk, b=b: e.matmul(
                                    PS[b][:, :], DG[pr][:, tp * 128:(tp + 1) * 128],
                                    PRE[pr][:, blk * 512 + tp: blk * 512 + tp + 512],
                                    start=(tp == 0), stop=(tp == 2)),
                                    reads=rds, writes=[PB[b]])
                            if half == 0:
                                S.op("act", lambda e, ch=ch, blk=blk, b=b: e.activation(
                                    SAFt[:, blk * 512:(blk + 1) * 512], PS[b][:, :], AF.Silu,
                                    bias=smalls[:, C_FCB + ch: C_FCB + ch + 1]),
                                    reads=[B_small], writes=[PB[b], B_SAFt[blk]])
                            else:
                                S.op("dve", lambda e, ch=ch, blk=blk, b=b, i=i: e.scalar_tensor_tensor(
                                    G[:, i * 2048 + blk * 512: i * 2048 + (blk + 1) * 512], PS[b][:, :],
                                    smalls[:, C_FCB + ch: C_FCB + ch + 1],
                                    SAFt[:, blk * 512:(blk + 1) * 512], ALU.add, ALU.mult),
                                    reads=[B_small, B_SAFt[blk]], writes=[PB[b], B_Gc[i][blk]])
                S.barrier(engines=("act", "dve", "pool", "sp"))

            with ExitStack() as ph2:

                def wd_ap(m, kc):
                    if m < 5:
                        return BIGA[:, m * 2816 + kc * 128: m * 2816 + (kc + 1) * 128]
                    return WD2[:, (m - 5) * 2816 + kc * 128:(m - 5) * 2816 + (kc + 1) * 128]
                for m in range(5):
                    S.dma("pool", BIGA[:, m * 2816:(m + 1) * 2816], d_wdown[m * 128:(m + 1) * 128, :],
                          writes=[B_WD[m]])
                Y = [sbt(ph2, f"Y{i}", 4096, F32) for i in range(2)]
                B_Y = [[Buf(f"y{i}_{m}") for m in range(8)] for i in range(2)]
                SQ = [sbt(ph2, f"sqE{i}", 512, BF16) for i in range(2)]
                B_SQ = [Buf(f"sqE{i}") for i in range(2)]
                XB = sbt(ph2, "xbE", 4096, F32)
                B_XBm = [Buf(f"xbE{m}") for m in range(8)]
                RSTD = sbt(ph2, "rstdE", 512, F32)
                B_RSTD = Buf("rstdE")
                TMP = [sbt(ph2, f"tmpE{i}", 512, F32) for i in range(2)]
                B_TMP = [Buf(f"tmpE{i}") for i in range(2)]

                MORD = [5, 6, 7, 0, 1, 2, 3, 4]

                def e_ss(blk, mi):
                    sq = mi % 2
                    S.op("pe", lambda e: e.matmul(
                        PS[4 + (blk % 2)][:, :], onesD[:, :], SQ[sq][:, 0:512],
                        start=(mi == 0), stop=(mi == 7)),
                        reads=[B_SQ[sq], B_ones], writes=[PB[4 + (blk % 2)]])

                def e_load(blk):
                    S.dma("sp", AP(XB, 4096, 0, [[512, 8], [1, 512]]), xblk(d_x1, blk), writes=B_XBm)

                def e_chain(blk):
                    yb = blk % 2
                    rsqrt_eps(RSTD[:, 0:512], PS[4 + (blk % 2)][:, :], PB[4 + (blk % 2)], B_RSTD)
                    for m in range(8):
                        sl = m % 2
                        S.op("dve", lambda e: e.tensor_tensor(
                            TMP[sl][:, 0:512], Y[yb][:, m * 512:(m + 1) * 512], RSTD[:, 0:512], ALU.mult),
                            reads=[B_Y[yb][m], B_RSTD], writes=[B_TMP[sl]])
                        S.op("dve", lambda e: e.scalar_tensor_tensor(
                            XB[:, m * 512:(m + 1) * 512], TMP[sl][:, 0:512], der[:, 24 + m: 25 + m],
                            XB[:, m * 512:(m + 1) * 512], ALU.mult, ALU.add),
                            reads=[B_TMP[sl], B_der2], writes=[B_XBm[m]])
                        S.dma("sp", xblk(d_out, blk)[:, m, :], XB[:, m * 512:(m + 1) * 512], reads=[B_XBm[m]])

                for blk in range(4):
                    yb = blk % 2
                    for mi in range(8):
                        m = MORD[mi]
                        b = mi % 4
                        for kc in range(22):
                            S.op("pe", lambda e, kc=kc: e.matmul(
                                PS[b][:, :], wd_ap(m, kc),
                                G[:, kc * 2048 + blk * 512: kc * 2048 + (blk + 1) * 512],
                                start=(kc == 0), stop=(kc == 21)),
                                reads=[B_WD[m], B_Gc[kc][blk]], writes=[PB[b]])
                        S.op("act", lambda e: e.activation(
                            Y[yb][:, m * 512:(m + 1) * 512], PS[b][:, :], AF.Copy),
                            writes=[PB[b], B_Y[yb][m]])
                        if mi == 0 and blk > 0:
                            e_ss(blk - 1, 7)
                            e_load(blk - 1)
                        if mi == 3 and blk > 0:
                            e_chain(blk - 1)
                        if mi > 0:
                            e_ss(blk, mi - 1)
                        sq = mi % 2
                        S.op("act", lambda e: e.activation(SQ[sq][:, 0:512], PS[b][:, :], AF.Square),
                             writes=[PB[b], B_SQ[sq]])
                e_ss(3, 7)
                e_load(3)
                e_chain(3)
                S.barrier(engines=("act", "dve", "pool", "sp"))
        S.finish("sp")
    return nc


def _chunkify(w, n_k):
    K, N = w.shape
    n = N // 128
    a = w.reshape(n_k, 128, n, 128).transpose(2, 1, 0, 3)
    return np.ascontiguousarray(a.reshape(n * 128, n_k * 128))


def _vec(v):
    return np.ascontiguousarray(v.reshape(-1, 128).T)


def _prep(inputs):
    f = lambda k: np.asarray(inputs[k], dtype=np.float32)
    w_in = f("w_in")[0]
    fm_cols = np.r_[0:512, 512:1024, 1544:2056, 2056:2568, 3088:3600]
    wfm = _chunkify(w_in[:, fm_cols], 8)
    tm_cols = np.r_[1024:1536, 2568:3080, 1536:1544, 3080:3084, 3084:3088]
    wtm = np.ascontiguousarray(
        w_in[:, tm_cols].reshape(8, 128, 1040).transpose(1, 0, 2).reshape(128, 8 * 1040))
    wout = _chunkify(f("w_out")[0], 8)
    wup = _chunkify(f("w_up")[0], 8)
    wdown = _chunkify(f("w_down")[0], 22)
    wada = _chunkify(f("w_ada")[0], 8)
    sm = np.zeros((128, NSM), np.float32)
    sm[:, C_BADA:C_BADA + 48] = _vec(f("b_ada")[0])
    sm[:, C_GPM:C_GPM + 8] = _vec(f("g_pre_mix")[0])
    sm[:, C_GOM:C_GOM + 8] = _vec(f("g_post_mix")[0])
    sm[:, C_GPF:C_GPF + 8] = _vec(f("g_pre_ffn")[0])
    sm[:, C_GOF:C_GOF + 8] = _vec(f("g_post_ffn")[0])
    mcw = f("mlstm_conv_w")[0]
    sm[:, C_MCW:C_MCW + 32] = mcw.reshape(4, 8, 128).transpose(2, 1, 0).reshape(128, 32)
    sm[:, C_MCB:C_MCB + 8] = _vec(f("mlstm_conv_b")[0])
    sm[:, C_GMH:C_GMH + 4] = _vec(f("g_mlstm_head")[0])
    fcw = f("ffn_conv_w")[0]
    sm[:, C_FCW:C_FCW + 132] = fcw.reshape(3, 44, 128).transpose(2, 1, 0).reshape(128, 132)
    sm[:, C_FCB:C_FCB + 44] = _vec(f("ffn_conv_b")[0])
    sm[:, C_BG:C_BG + 16] = f("b_gate")[0][None, :]
    consts = np.zeros((128, 512), np.float32)
    k = np.arange(128)[:, None]
    q = np.arange(128)[None, :]
    consts[:, 0:128] = (k == q)
    consts[:, 128:256] = (q >= k)
    consts[:, 256:384] = 1.0
    consts[:, 384:512] = np.where(k > q, NEG, 0.0)
    x = f("x")
    c = f("c")
    in_maps = []
    for b in range(NCORES):
        smb = sm.copy()
        smb[:, C_C:C_C + 8] = _vec(c[b])
        in_maps.append({"xT": np.ascontiguousarray(x[b].T), "wfm": wfm, "wtm": wtm, "wout": wout,
                        "wup": wup, "wdown": wdown, "wada": wada, "smalls": smb, "consts": consts})
    return in_maps


_NC_CACHE = {}


def kernel(**inputs):
    in_maps = _prep(inputs)
    if "nc" not in _NC_CACHE:
        _NC_CACHE["nc"] = build_nc()
    res = run_bass_kernel_spmd(_NC_CACHE["nc"], in_maps, core_ids=list(range(NCORES)))
    out = np.stack([np.ascontiguousarray(r["outT"].T) for r in res.results], axis=0)
    return out.astype(np.float32)
```

```python
import bisect
import math
from contextlib import ExitStack

import numpy as np
import concourse.bass as bass
import concourse.mybir as mybir
from concourse.bass_utils import run_bass_kernel_spmd

F32 = mybir.dt.float32
BF16 = mybir.dt.bfloat16
AF = mybir.ActivationFunctionType
ALU = mybir.AluOpType

NCORES = 8
SEQ = 2048
DM = 1024
DFF = 2816
EPS = 1e-6
NEG = -30000.0
EXPSHIFT = 9.0

C_C = 0
C_BADA = 8
C_GPM = 56
C_GOM = 64
C_GPF = 72
C_GOF = 80
C_MCW = 88
C_MCB = 120
C_GMH = 128
C_FCW = 132
C_FCB = 264
C_BG = 308
NSM = 324


class Buf:
    __slots__ = ("name", "w", "r")

    def __init__(self, name):
        self.name = name
        self.w = None
        self.r = []


class Sched:
    EPOCH = 4000
    NDMA = 12

    def __init__(self, nc, stack):
        self.nc = nc
        self.stack = stack
        self.eng = {"pe": nc.tensor, "act": nc.scalar, "dve": nc.vector,
                    "pool": nc.gpsimd, "sp": nc.sync}
        self.seq = {e: 0 for e in self.eng}
        self.pending = {e: {} for e in self.eng}
        self.signaled = {e: [] for e in self.eng}
        self.sems = {e: [] for e in self.eng}
        self.waited = {}
        self.dma_sems = {}
        self.dma_cnt = {}
        self.ndma = {}
        self.nwaits = 0

    def _sem(self, e, idx):
        while len(self.sems[e]) <= idx:
            self.sems[e].append(self.stack.enter_context(
                self.nc.semaphore(f"s_{e}_{len(self.sems[e])}")))
        return self.sems[e][idx]

    def _target(self, tok):
        if tok[0] == "dma":
            _, q, s, cnt = tok
            return ("dma", q, s), self.dma_sems[q][s], 16 * cnt
        e, n = tok
        sg = self.signaled[e]
        i = bisect.bisect_left(sg, n)
        if i == len(sg):
            ins = self.pending[e][n]
            rank = len(sg)
            ins.then_inc(self._sem(e, rank // self.EPOCH), 1)
            sg.append(n)
            for k in [k for k in self.pending[e] if k <= n]:
                del self.pending[e][k]
            i = rank
        return (e, i // self.EPOCH), self._sem(e, i // self.EPOCH), i % self.EPOCH + 1

    def _wait(self, waiter, toks):
        need = {}
        for tok in toks:
            if tok is None:
                continue
            if tok[0] == "pe" and waiter == "pe":
                continue
            key, sem, val = self._target(tok)
            if key[0] == "dma":
                if self.waited.get((waiter, key), 0) >= val:
                    continue
                if need.get(key, (None, 0))[1] < val:
                    need[key] = (sem, val)
            else:
                e, ep = key
                if self.waited.get((waiter, e), (-1, 0)) >= (ep, val):
                    continue
                k2 = ("eng", e)
                if need.get(k2, (None, (-1, 0)))[1] < (ep, val):
                    need[k2] = (sem, (ep, val))
        for key, (sem, val) in need.items():
            if key[0] == "dma":
                self.eng[waiter].wait_ge(sem, val)
                self.waited[(waiter, key)] = val
            else:
                self.eng[waiter].wait_ge(sem, val[1])
                self.waited[(waiter, key[1])] = val
            self.nwaits += 1

    @staticmethod
    def _deps(reads, writes):
        toks = []
        for b in reads:
            toks.append(b.w)
        for b in writes:
            toks.append(b.w)
            toks.extend(b.r)
        return toks

    @staticmethod
    def _commit(tok, reads, writes):
        for b in writes:
            b.w = tok
            b.r = []
        for b in reads:
            if b not in writes:
                b.r.append(tok)

    def op(self, e, fn, reads=(), writes=()):
        reads = list(reads)
        writes = list(writes)
        self._wait(e, self._deps(reads, writes))
        ins = fn(self.eng[e])
        n = self.seq[e]
        self.seq[e] += 1
        if e == "pe":
            self.pending[e][n] = ins
        else:
            rank = len(self.signaled[e])
            ins.then_inc(self._sem(e, rank // self.EPOCH), 1)
            self.signaled[e].append(n)
        self._commit((e, n), reads, writes)
        return ins

    def dma(self, q, out, in_, reads=(), writes=()):
        reads = list(reads)
        writes = list(writes)
        if q not in self.dma_sems:
            self.dma_sems[q] = [self.stack.enter_context(self.nc.semaphore(f"dq_{q}_{i}"))
                                for i in range(self.NDMA)]
            self.dma_cnt[q] = [0] * self.NDMA
            self.ndma[q] = 0
        s = self.ndma[q] % self.NDMA
        self.ndma[q] += 1
        toks = self._deps(reads, writes)
        if self.dma_cnt[q][s] > 0:
            toks.append(("dma", q, s, self.dma_cnt[q][s]))
        self._wait(q, toks)
        self.dma_cnt[q][s] += 1
        self.eng[q].dma_start(out=out, in_=in_).then_inc(self.dma_sems[q][s], 16)
        self._commit(("dma", q, s, self.dma_cnt[q][s]), reads, writes)

    def _all_tokens(self):
        toks = []
        for e in self.eng:
            if self.seq[e] > 0:
                toks.append((e, self.seq[e] - 1))
        for q in self.dma_sems:
            for s in range(self.NDMA):
                if self.dma_cnt[q][s] > 0:
                    toks.append(("dma", q, s, self.dma_cnt[q][s]))
        return toks

    def barrier(self, engines=("pe", "act", "dve", "pool", "sp")):
        toks = self._all_tokens()
        for e in engines:
            self._wait(e, [t for t in toks if not (t[0] == e)])

    def finish(self, e):
        self._wait(e, self._all_tokens())


def build_nc(debug_taps=False):
    nc = bass.Bass("TRN2", target_bir_lowering=False)
    d_xT = nc.dram_tensor("xT", [DM, SEQ], F32, kind="ExternalInput").ap()
    d_wfm = nc.dram_tensor("wfm", [20 * 128, 1024], F32, kind="ExternalInput").ap()
    d_wtm = nc.dram_tensor("wtm", [128, 8 * 1040], F32, kind="ExternalInput").ap()
    d_wout = nc.dram_tensor("wout", [8 * 128, 1024], F32, kind="ExternalInput").ap()
    d_wup = nc.dram_tensor("wup", [44 * 128, 1024], F32, kind="ExternalInput").ap()
    d_wdown = nc.dram_tensor("wdown", [8 * 128, 22 * 128], F32, kind="ExternalInput").ap()
    d_wada = nc.dram_tensor("wada", [48 * 128, 1024], F32, kind="ExternalInput").ap()
    d_smalls = nc.dram_tensor("smalls", [128, NSM], F32, kind="ExternalInput").ap()
    d_consts = nc.dram_tensor("consts", [128, 512], F32, kind="ExternalInput").ap()
    d_out = nc.dram_tensor("outT", [DM, SEQ], F32, kind="ExternalOutput").ap()
    d_x1 = nc.dram_tensor("x1s", [DM, SEQ], F32).ap()
    d_dbg = nc.dram_tensor("dbg_ot", [128, 8 * 2048], BF16, kind="ExternalOutput").ap() if debug_taps else None

    def xblk(d, blk):
        return d.rearrange("(c p) t -> p c t", p=128)[:, :, blk * 512:(blk + 1) * 512]

    with ExitStack() as st:
        S = Sched(nc, st)

        def sbt(stack, name, F, dt):
            return stack.enter_context(nc.sbuf_tensor("sb_" + name, [128, F], dt))

        def AP(t, F, off, dims, p0=0, npart=128):
            return bass.AP(t, p0 * F + off, [[F, npart]] + [list(d) for d in dims])

        PS = [st.enter_context(nc.psum_tensor(f"ps{i}", [128, 512], F32)) for i in range(8)]
        PB = [Buf(f"ps{i}") for i in range(8)]

        smalls = sbt(st, "smalls", NSM, F32)
        cf = sbt(st, "cf", 512, F32)
        cb = sbt(st, "cb", 512, BF16)
        onesD = sbt(st, "onesD", 128, BF16)
        onesH = sbt(st, "onesH", 128, BF16)
        cs_bf = sbt(st, "cs_bf", 8, BF16)
        modT = sbt(st, "modT", 48, F32)
        der = sbt(st, "der", 32, F32)
        BIGA = sbt(st, "BIGA", 8 * 2048, BF16)
        B_small, B_cf, B_cb = Buf("smalls"), Buf("cf"), Buf("cb")
        B_ones, B_cs, B_mod, B_der = Buf("ones"), Buf("cs"), Buf("mod"), Buf("der")
        B_H = [[Buf(f"h{c}_{b}") for b in range(4)] for c in range(8)]
        B_OT = [[Buf(f"ot{c}_{q}") for q in range(8)] for c in range(8)]

        ident_f = cf[:, 0:128]
        tri_f = cf[:, 128:256]
        ones_f = cf[:, 256:384]
        ident_b = cb[:, 0:128]
        tri_b = cb[:, 128:256]
        ones_b = cb[:, 256:384]
        trineg_b = cb[:, 384:512]

        kcol = sbt(st, "kcol", 8, F32)
        B_kcol = Buf("kcol")
        S.op("dve", lambda e: e.memset(kcol[:, 0:1], EPS), writes=[B_kcol])
        S.op("dve", lambda e: e.memset(kcol[:, 1:2], EXPSHIFT - 0.5 * math.log(128.0)), writes=[B_kcol])
        S.op("dve", lambda e: e.memset(kcol[:, 3:4], EXPSHIFT), writes=[B_kcol])
        S.op("dve", lambda e: e.memset(kcol[:, 4:5], -EXPSHIFT), writes=[B_kcol])
        S.op("dve", lambda e: e.memset(kcol[:, 2:3], 1.0), writes=[B_kcol])

        def rsqrt_eps(out_ap, in_ap, B_in, B_out):
            S.op("act", lambda e: e.activation(out_ap, in_ap, AF.Ln, bias=kcol[:, 0:1]),
                 reads=[B_kcol], writes=[B_in, B_out])
            S.op("act", lambda e: e.activation(out_ap, out_ap, AF.Exp, scale=-0.5), writes=[B_out])

        S.dma("sp", smalls[:], d_smalls, writes=[B_small])
        S.dma("sp", cf[:], d_consts, writes=[B_cf])
        S.dma("pool", cb[:], d_consts, writes=[B_cb])
        S.op("dve", lambda e: e.memset(onesD[:], 2.0 ** -10), writes=[B_ones])
        S.op("dve", lambda e: e.memset(onesH[:], 2.0 ** -7), writes=[B_ones])

        NW = 4
        WA = [sbt(st, f"wa{i}", 1024, BF16) for i in range(NW)]
        B_WA = [Buf(f"wa{i}") for i in range(NW)]
        B_mod2, B_der2 = Buf("mod2"), Buf("der2")
        S.op("act", lambda e: e.activation(cs_bf[:], smalls[:, C_C:C_C + 8], AF.Silu),
             reads=[B_small], writes=[B_cs])

        B_W16 = [Buf(f"w16_{j}") for j in range(16)]

        def mod_chunk(j, bank, B_m):
            sl = j % NW
            if j < 16:
                wsl, B_wsl = BIGA[:, j * 1024:(j + 1) * 1024], B_W16[j]
            else:
                wsl, B_wsl = WA[sl][:], B_WA[sl]
            S.dma("pool", wsl, d_wada[j * 128:(j + 1) * 128, :], writes=[B_wsl])
            for kc in range(8):
                S.op("pe", lambda e, kc=kc: e.matmul(
                    PS[bank][:, 0:1], wsl[:, kc * 128:(kc + 1) * 128], cs_bf[:, kc:kc + 1],
                    start=(kc == 0), stop=(kc == 7)),
                    reads=[B_wsl, B_cs], writes=[PB[bank]])
            if j < 16:
                S.op("act", lambda e: e.activation(modT[:, j:j + 1], PS[bank][:, 0:1], AF.Identity,
                                                   bias=smalls[:, C_BADA + j:C_BADA + j + 1]),
                     reads=[B_small], writes=[PB[bank], B_m])
            else:
                S.op("dve", lambda e: e.tensor_tensor(modT[:, j:j + 1], PS[bank][:, 0:1],
                                                      smalls[:, C_BADA + j:C_BADA + j + 1], ALU.add),
                     reads=[B_small], writes=[PB[bank], B_m])

        for j in range(16):
            mod_chunk(j, 6 + (j % 2), B_mod)

        def mod_finish():
            S.op("dve", lambda e: e.tensor_tensor(
                der[:, 8:16], modT[:, 16:24], smalls[:, C_GOM:C_GOM + 8], ALU.mult),
                reads=[B_mod2, B_small], writes=[B_der2])
            S.op("dve", lambda e: e.scalar_tensor_tensor(
                der[:, 16:24], modT[:, 32:40], 1.0, smalls[:, C_GPF:C_GPF + 8], ALU.add, ALU.mult),
                reads=[B_mod2, B_small], writes=[B_der2])
            S.op("dve", lambda e: e.tensor_tensor(
                der[:, 24:32], modT[:, 40:48], smalls[:, C_GOF:C_GOF + 8], ALU.mult),
                reads=[B_mod2, B_small], writes=[B_der2])

        def norm_mod(XBt, B_XB, blk, gs0, sh0, SQ, B_SQ, RSTD, B_RSTD, TMP, B_TMP, bank):
            S.op("act", lambda e: e.activation(SQ[:, 0:4096], XBt[:, 0:4096], AF.Square),
                 reads=[B_XB], writes=[B_SQ])
            for kc in range(8):
                S.op("pe", lambda e, kc=kc: e.matmul(
                    PS[bank][:, :], onesD[:, :], SQ[:, kc * 512:(kc + 1) * 512],
                    start=(kc == 0), stop=(kc == 7)),
                    reads=[B_SQ, B_ones], writes=[PB[bank]])
            rsqrt_eps(RSTD[:, 0:512], PS[bank][:, :], PB[bank], B_RSTD)
            for c in range(8):
                sl = c % 2
                S.op("dve", lambda e, c=c, sl=sl: e.tensor_tensor(
                    TMP[sl][:, 0:512], XBt[:, c * 512:(c + 1) * 512], RSTD[:, 0:512], ALU.mult),
                    reads=[B_XB, B_RSTD], writes=[B_TMP[sl]])
                S.op("act", lambda e, c=c, sl=sl: e.activation(
                    BIGA[:, c * 2048 + blk * 512: c * 2048 + (blk + 1) * 512], TMP[sl][:, 0:512],
                    AF.Identity, bias=sh0(c), scale=gs0(c)),
                    reads=[B_TMP[sl], B_der, B_mod, B_der2, B_mod2], writes=[B_H[c][blk]])

        with ExitStack() as phB:
            OT = sbt(phB, "OT", 8 * 2048, BF16)
            GT = sbt(phB, "GT", 256, F32)
            GX = sbt(phB, "GX", 256, F32)
            SPc = sbt(phB, "SPc", 256, F32)
            OFFS = sbt(phB, "OFFS", 17 * 16, F32)
            CS = sbt(phB, "CS", 256, F32)
            MID = sbt(phB, "MID", 128, F32)
            BIASF = sbt(phB, "BIASF", 1024, F32)
            AI = sbt(phB, "AI", 64, F32)
            ARG = sbt(phB, "ARG", 512, F32)
            SCL = sbt(phB, "SCL", 512, F32)
            LBA = sbt(phB, "LBA", 64, F32)
            LB = sbt(phB, "LB", 64, F32)
            GINV = sbt(phB, "GINV", 64, F32)
            B_G = Buf("gates")

            with ExitStack() as ph:
                WTM = sbt(ph, "WTM", 8 * 1040, BF16)
                B_WTM = Buf("wtm")
                S.dma("pool", WTM[:], d_wtm, writes=[B_WTM])
                QK = sbt(ph, "QKf", 4 * 2048, BF16)
                QZ = sbt(ph, "QZf", 8 * 2048, BF16)
                B_QK = [[Buf(f"qk{c}_{b}") for b in range(4)] for c in range(8)]
                B_QZinit = Buf("qzinit")
                VA = sbt(ph, "VA", 16 * 1024, BF16)
                B_VA = [Buf(f"va{j}") for j in range(16)]
                B_VAinit = Buf("vainit")
                NWS = 4
                W, B_W = WA, B_WA

                XBv = [OT[:, 0:8192].bitcast(F32), OT[:, 8192:16384].bitcast(F32)]
                S.op("dve", lambda e: e.memset(VA[:], 1.0), writes=B_VA)
                S.op("dve", lambda e: e.memset(AP(QZ, 16384, 5 * 2048, [[4096, 2], [1, 2048]], p0=0, npart=64), 0.0),
                     writes=[B_QZinit] + [B_QK[c][b] for c in (2, 3) for b in range(4)])
                S.op("dve", lambda e: e.memset(AP(QZ, 16384, 4 * 2048, [[4096, 2], [1, 2048]], p0=64, npart=64), 0.0),
                     writes=[B_QZinit] + [B_QK[c][b] for c in (2, 3) for b in range(4)])
                S.op("dve", lambda e: e.scalar_tensor_tensor(
                    der[:, 0:8], modT[:, 8:16], 1.0, smalls[:, C_GPM:C_GPM + 8], ALU.add, ALU.mult),
                    reads=[B_mod, B_small], writes=[B_der])
                SQv = QZ[:, 0:4096]
                RSTDv = QZ[:, 4096:5120].bitcast(F32)
                TMPv = [QZ[:, 5120:6144].bitcast(F32), QZ[:, 6144:7168].bitcast(F32)]
                B_XBs = [Buf(f"xbA{i}") for i in range(2)]
                B_SQA, B_RSTDA = Buf("sqA"), Buf("rstdA")
                B_TMPA = [Buf(f"tmpA{i}") for i in range(2)]
                A_BUFS = B_XBs + [B_SQA, B_RSTDA] + B_TMPA
                kch = [4, 5, 6, 7]
                for ci, ch in enumerate(kch):
                    S.dma("pool", W[ci][:], d_wfm[ch * 128:(ch + 1) * 128, :], writes=[B_W[ci]])
                RSTDw = [RSTDv, QZ[:, 7168:8192].bitcast(F32)]
                B_RSTDw = [B_RSTDA, Buf("rstdA1")]
                A_BUFS = A_BUFS + [B_RSTDw[1]]

                def a_load(blk):
                    xi = blk % 2
                    S.dma("sp", XBv[xi].rearrange("p (c t) -> p c t", c=8), xblk(d_xT, blk),
                          reads=(B_WA if blk == 1 else []), writes=[B_XBs[xi]])

                def a_stats(blk):
                    xi = blk % 2
                    pbank = 6 + (blk % 2)
                    S.op("act", lambda e: e.activation(SQv[:, 0:4096], XBv[xi][:, 0:4096], AF.Square),
                         reads=[B_XBs[xi]], writes=[B_SQA])
                    for kc in range(8):
                        S.op("pe", lambda e, kc=kc: e.matmul(
                            PS[pbank][:, :], onesD[:, :], SQv[:, kc * 512:(kc + 1) * 512],
                            start=(kc == 0), stop=(kc == 7)),
                            reads=[B_SQA, B_ones], writes=[PB[pbank]])
                    rsqrt_eps(RSTDw[xi][:, 0:512], PS[pbank][:, :], PB[pbank], B_RSTDw[xi])

                def a_h(blk):
                    xi = blk % 2
                    for c in range(8):
                        sl = c % 2
                        S.op("dve", lambda e, c=c, sl=sl: e.tensor_tensor(
                            TMPv[sl][:, 0:512], XBv[xi][:, c * 512:(c + 1) * 512], RSTDw[xi][:, 0:512],
                            ALU.mult), reads=[B_XBs[xi], B_RSTDw[xi]], writes=[B_TMPA[sl]])
                        S.op("act", lambda e, c=c, sl=sl: e.activation(
                            BIGA[:, c * 2048 + blk * 512: c * 2048 + (blk + 1) * 512], TMPv[sl][:, 0:512],
                            AF.Identity, bias=modT[:, c:c + 1], scale=der[:, c:c + 1]),
                            reads=[B_TMPA[sl], B_der, B_mod], writes=[B_H[c][blk]])

                bank = 0
                a_load(0)
                a_load(1)
                a_stats(0)
                for blk in range(4):
                    a_h(blk)
                    groups = []
                    for ci, ch in enumerate(kch):
                        b = bank % 4
                        bank += 1
                        for kc in range(8):
                            S.op("pe", lambda e, ci=ci, kc=kc, blk=blk, b=b: e.matmul(
                                PS[b][:, :], W[ci][:, kc * 128:(kc + 1) * 128],
                                BIGA[:, kc * 2048 + blk * 512: kc * 2048 + (blk + 1) * 512],
                                start=(kc == 0), stop=(kc == 7)),
                                reads=[B_W[ci], B_H[kc][blk]], writes=[PB[b]])
                        groups.append((ci, ch, b))
                    if blk + 2 < 4:
                        a_load(blk + 2)
                    if blk + 1 < 4:
                        a_stats(blk + 1)
                    for ci, ch, b in groups:
                        dst = QK[:, (ch - 4) * 2048 + blk * 512: (ch - 4) * 2048 + (blk + 1) * 512]
                        if ci % 2 == 0:
                            S.op("dve", lambda e, dst=dst, b=b: e.tensor_copy(dst, PS[b][:, :]),
                                 writes=[PB[b], B_QK[ch][blk]])
                        else:
                            S.op("act", lambda e, dst=dst, b=b: e.activation(dst, PS[b][:, :], AF.Copy),
                                 writes=[PB[b], B_QK[ch][blk]])
                S.op("dve", lambda e: e.memset(AP(QZ, 16384, 2048, [[4096, 2], [1, 2048]], p0=0, npart=64), 0.0),
                     writes=[B_QZinit] + A_BUFS + [B_QK[c][b] for c in (0, 1) for b in range(4)])
                S.op("act", lambda e: e.memzero(AP(QZ, 16384, 0, [[4096, 2], [1, 2048]], p0=64, npart=64)),
                     writes=[B_QZinit] + A_BUFS + [B_QK[c][b] for c in (0, 1) for b in range(4)])
                for tt in range(16):
                    for kc in range(8):
                        S.op("pe", lambda e, kc=kc, tt=tt: e.matmul(
                            PS[6][:, tt * 16:(tt + 1) * 16],
                            BIGA[:, kc * 2048 + tt * 128: kc * 2048 + (tt + 1) * 128],
                            WTM[:, kc * 1040 + 1024: kc * 1040 + 1040],
                            start=(kc == 0), stop=(kc == 7)),
                            reads=[B_WTM, B_H[kc][tt // 4]], writes=[PB[6]])
                GQ = []

                def GQ_add(*a, **k):
                    GQ.append((a, k))

                def gq_pop(n):
                    for _ in range(n):
                        if GQ:
                            a, k = GQ.pop(0)
                            S.op(*a, **k)
                GQ_add("dve", lambda e: e.tensor_copy(GT[:, 0:256], PS[6][:, 0:256]),
                     writes=[PB[6], B_G])

                GQ_add("dve", lambda e: e.tensor_tensor(
                    AP(GX, 256, 0, [[16, 16], [1, 16]]), AP(GT, 256, 0, [[16, 16], [1, 16]]),
                    AP(smalls, NSM, C_BG, [[0, 16], [1, 16]]), ALU.add),
                    reads=[B_small], writes=[B_G])
                GQ_add("act", lambda e: e.activation(SPc[:, 0:256], GX[:, 0:256], AF.Exp, scale=-1.0),
                     writes=[B_G])
                GQ_add("act", lambda e: e.activation(SPc[:, 0:256], SPc[:, 0:256], AF.Ln, bias=kcol[:, 2:3]),
                     writes=[B_G])
                GQ_add("pe", lambda e: e.matmul(PS[6][:, 0:256], tri_f, SPc[:, 0:256],
                                              start=True, stop=True),
                     reads=[B_G, B_cf], writes=[PB[6]])
                GQ_add("pe", lambda e: e.matmul(PS[7][:, 0:256], ones_f, SPc[:, 0:256],
                                              start=True, stop=True),
                     reads=[B_G, B_cf], writes=[PB[7]])
                GQ_add("dve", lambda e: e.memset(OFFS[:, 0:16], 0.0), writes=[B_G])
                for j in range(16):
                    GQ_add("dve", lambda e, j=j: e.tensor_tensor(
                        OFFS[:, (j + 1) * 16:(j + 2) * 16], PS[7][:, j * 16:(j + 1) * 16],
                        OFFS[:, j * 16:(j + 1) * 16], ALU.add),
                        writes=[B_G, PB[7]])
                GQ_add("dve", lambda e: e.tensor_tensor(CS[:, 0:256], PS[6][:, 0:256],
                                                      OFFS[:, 0:256], ALU.add),
                     writes=[B_G, PB[6]])
                GQ_add("dve", lambda e: e.tensor_tensor(
                    AP(MID, 128, 0, [[16, 8], [1, 16]]), AP(OFFS, 272, 0, [[32, 8], [1, 16]]),
                    AP(OFFS, 272, 32, [[32, 8], [1, 16]]), ALU.add), writes=[B_G])
                GQ_add("dve", lambda e: e.tensor_scalar(MID[:, 0:128], MID[:, 0:128], 0.5, None,
                                                      ALU.mult), writes=[B_G])
                for j in range(16):
                    GQ_add("dve", lambda e, j=j: e.tensor_tensor(
                        AP(BIASF, 1024, j * 64, [[8, 8], [1, 8]]),
                        AP(CS, 256, j * 16, [[0, 8], [1, 8]]),
                        AP(MID, 128, 0, [[16, 8], [1, 8]]), ALU.subtract), writes=[B_G])
                GQ_add("dve", lambda e: e.tensor_scalar(BIASF[:, 0:1024], BIASF[:, 0:1024], EXPSHIFT, 78.0,
                                                      ALU.add, ALU.min), writes=[B_G])
                GQ_add("dve", lambda e: e.tensor_tensor(
                    AP(AI, 64, 0, [[4, 16], [1, 4]]), AP(CS, 256, 12, [[16, 16], [1, 4]]),
                    AP(GX, 256, 8, [[16, 16], [1, 4]]), ALU.add), writes=[B_G])
                for j in range(16):
                    GQ_add("dve", lambda e, j=j: e.tensor_tensor(
                        AP(ARG, 512, j * 32, [[4, 8], [1, 4]]),
                        AP(AI, 64, j * 4, [[0, 8], [1, 4]]),
                        AP(MID, 128, 12, [[16, 8], [1, 4]]), ALU.subtract), writes=[B_G])
                GQ_add("dve", lambda e: e.tensor_scalar(ARG[:, 0:512], ARG[:, 0:512], 72.0, None,
                                                      ALU.min), writes=[B_G])
                GQ_add("act", lambda e: e.activation(SCL[:, 0:512], ARG[:, 0:512], AF.Exp,
                                                   bias=kcol[:, 1:2]), reads=[B_kcol], writes=[B_G])
                GQ_add("dve", lambda e: e.tensor_tensor(
                    AP(LBA, 64, 0, [[8, 8], [4, 2], [1, 4]]),
                    AP(CS, 256, 12, [[32, 8], [16, 2], [1, 4]]),
                    AP(MID, 128, 12, [[16, 8], [0, 2], [1, 4]]), ALU.subtract), writes=[B_G])
                GQ_add("act", lambda e: e.activation(LB[:, 0:64], LBA[:, 0:64], AF.Exp, bias=kcol[:, 3:4]),
                     reads=[B_kcol], writes=[B_G])
                GQ_add("act", lambda e: e.activation(GINV[:, 0:64], LBA[:, 0:64], AF.Exp, bias=kcol[:, 4:5],
                                                   scale=-1.0), reads=[B_kcol], writes=[B_G])

                for ci, ch in enumerate([2, 3, 0, 1]):
                    sl = ci % NWS
                    S.dma("pool", W[sl][:], d_wfm[ch * 128:(ch + 1) * 128, :], writes=[B_W[sl]])
                    for blk in range(4):
                        b = bank % 4
                        bank += 1
                        for kc in range(8):
                            S.op("pe", lambda e, sl=sl, kc=kc, blk=blk, b=b: e.matmul(
                                PS[b][:, :], W[sl][:, kc * 128:(kc + 1) * 128],
                                BIGA[:, kc * 2048 + blk * 512: kc * 2048 + (blk + 1) * 512],
                                start=(kc == 0), stop=(kc == 7)),
                                reads=[B_W[sl], B_H[kc][blk]], writes=[PB[b]])
                        for hh in range(2):
                            hq = 2 * ch + hh
                            S.op("act", lambda e, b=b, hh=hh, hq=hq, blk=blk: e.activation(
                                QZ[hh * 64:(hh + 1) * 64, hq * 2048 + blk * 512: hq * 2048 + (blk + 1) * 512],
                                PS[b][hh * 64:(hh + 1) * 64, :], AF.Identity, scale=0.125),
                                writes=[PB[b], B_QK[ch][blk]])
                        gq_pop(3)
                for tt in range(16):
                    b = 4 + (tt % 2)
                    for kc in range(8):
                        S.op("pe", lambda e, kc=kc, tt=tt, b=b: e.matmul(
                            PS[b][:, :], BIGA[:, kc * 2048 + tt * 128: kc * 2048 + (tt + 1) * 128],
                            WTM[:, kc * 1040: kc * 1040 + 512],
                            start=(kc == 0), stop=(kc == 7)),
                            reads=[B_WTM, B_H[kc][tt // 4]], writes=[PB[b]])
                    S.op("act", lambda e, tt=tt, b=b: e.activation(
                        AP(VA, 16384, tt * 1024, [[256, 4], [1, 64]]),
                        AP(PS[b], 512, 0, [[128, 4], [1, 64]]), AF.Copy),
                        writes=[PB[b], B_VA[tt]])
                    S.op("dve", lambda e, tt=tt, b=b: e.tensor_copy(
                        AP(VA, 16384, tt * 1024 + 128 + 64, [[256, 4], [1, 64]]),
                        AP(PS[b], 512, 64, [[128, 4], [1, 64]])),
                        writes=[PB[b], B_VA[tt]])
                    gq_pop(3)
                gq_pop(10 ** 6)
                PT = [sbt(ph, f"ptf{i}", 256, BF16) for i in range(6)]
                B_PT = [Buf(f"ptf{i}") for i in range(6)]
                RD = [sbt(ph, f"rdf{i}", 256, F32) for i in range(2)]
                B_RD = [Buf(f"rdf{i}") for i in range(2)]
                LA = 4
                items = [(h, qb, j) for h in range(8) for qb in range(8) for j in range(2 * qb + 2)]

                def fox_front(idx):
                    h, qb, j = items[idx]
                    c = h // 2
                    p0 = (h % 2) * 64
                    last = (j == 2 * qb + 1)
                    diag = (j >= 2 * qb)
                    q0 = qb * 256 + (128 if last else 0)
                    nq = 128 if last else 256
                    sbk = idx % 4
                    pt = idx % 6
                    S.op("pe", lambda e: e.matmul(
                        PS[sbk][:, 0:nq],
                        QK[:, c * 2048 + j * 128: c * 2048 + (j + 1) * 128],
                        QZ[:, h * 2048 + q0: h * 2048 + q0 + nq],
                        start=True, stop=(not diag)),
                        reads=[B_QK[4 + c][j // 4], B_QK[c][q0 // 512]], writes=[PB[sbk]])
                    if diag:
                        S.op("pe", lambda e: e.matmul(
                            PS[sbk][:, 0:128], ident_b, trineg_b, start=False, stop=True),
                            reads=[B_cb], writes=[PB[sbk]])
                    bcol = j * 64 + qb * 8 + h
                    S.op("act", lambda e: e.activation(
                        PT[pt][:, 0:nq], PS[sbk][:, 0:nq], AF.Exp, bias=BIASF[:, bcol:bcol + 1]),
                        reads=[B_G], writes=[PB[sbk], B_PT[pt]])

                def fox_back(idx):
                    h, qb, j = items[idx]
                    c = h // 2
                    p0 = (h % 2) * 64
                    last = (j == 2 * qb + 1)
                    nq = 128 if last else 256
                    pt = idx % 6
                    ob = 4 + (qb % 2)
                    oc0 = 128 if last else 0
                    S.op("pe", lambda e: e.matmul(
                        PS[ob][:, oc0:oc0 + nq],
                        VA[:, (j * 8 + h) * 128:(j * 8 + h + 1) * 128], PT[pt][:, 0:nq],
                        start=(j == 0), stop=last),
                        reads=[B_VA[j], B_PT[pt]], writes=[PB[ob]])
                    if last:
                        dp0 = 64 - p0
                        rd = qb % 2
                        S.op("dve", lambda e: e.reciprocal(
                            RD[rd][p0:p0 + 64, 0:256], PS[ob][dp0:dp0 + 64, 0:256]),
                            writes=[PB[ob], B_RD[rd]])
                        S.op("dve", lambda e: e.tensor_tensor(
                            OT[p0:p0 + 64, c * 2048 + qb * 256: c * 2048 + (qb + 1) * 256],
                            PS[ob][p0:p0 + 64, 0:256], RD[rd][p0:p0 + 64, 0:256], ALU.mult),
                            reads=[B_RD[rd]], writes=[PB[ob], B_OT[c][qb]])

                for idx in range(len(items) + LA):
                    if idx < len(items):
                        fox_front(idx)
                    if idx >= LA:
                        fox_back(idx - LA)
                    if idx % 17 == 5 and idx // 17 < 32:
                        mod_chunk(16 + idx // 17, 6 + ((idx // 17) % 2), B_mod2)
                mod_finish()
                for ch_ in range(3):
                    S.dma("pool", WA[ch_][:], d_wfm[(8 + ch_) * 128:(9 + ch_) * 128, :], writes=[B_WA[ch_]])
                S.barrier(engines=("act", "dve", "pool", "sp"))

            WO = sbt(phB, "WO", 8 * 1024, BF16)
            B_WO = [Buf(f"wo{m}") for m in range(8)]
            with ExitStack() as ph:
                QKm = sbt(ph, "QKm", 8 * 2048, BF16)
                B_QKm = [[Buf(f"qkm{c}_{b}") for b in range(4)] for c in range(8)]
                MV = sbt(ph, "MV", 16 * 512, BF16)
                B_MV = [Buf(f"mv{j}") for j in range(16)]
                MO = sbt(ph, "MO", 4 * 2048, BF16)
                B_MO = [[Buf(f"mo{c}_{b}") for b in range(4)] for c in range(4)]
                with ExitStack() as ph3:
                    WTM = sbt(ph3, "WTM2", 8 * 512, BF16)
                    B_WTM = Buf("wtm2")
                    S.dma("pool", AP(WTM, 4096, 0, [[512, 8], [1, 512]]),
                          d_wtm.rearrange("p (k c) -> p k c", c=1040)[:, :, 512:1024], writes=[B_WTM])
                    NWS = 3
                    W, B_W = WA, B_WA
                    PRE = [sbt(ph3, f"pre{i}", 2048 + 3, BF16) for i in range(2)]
                    B_PRE = [[Buf(f"pre{i}_{b}") for b in range(4)] for i in range(2)]
                    B_PREpad = [Buf(f"prepad{i}") for i in range(2)]
                    DG = [sbt(ph3, f"dg{i}", 512, BF16) for i in range(2)]
                    B_DG = [Buf(f"dg{i}") for i in range(2)]
                    QF = [sbt(ph3, f"qf{i}", 512, F32) for i in range(2)]
                    B_QF = [Buf(f"qf{i}") for i in range(2)]
                    DGQ = [sbt(ph3, f"dgq{i}", 128, F32) for i in range(8)]
                    B_DGQ = [Buf(f"dgq{i}") for i in range(8)]
                    qfc = [0]
                    dgc = [0]
                    GBANKS = [5, 6, 7]
                    for i in range(2):
                        S.op("pool", lambda e, i=i: e.memset(PRE[i][:, 0:3], 0.0), writes=[B_PREpad[i]])
                    bank = 0
                    for ch in range(8):
                        sl = ch % NWS
                        pr = ch % 2
                        if ch >= 3:
                            S.dma("pool", W[sl][:], d_wfm[(8 + ch) * 128:(9 + ch) * 128, :], writes=[B_W[sl]])
                        for tp in range(4):
                            S.op("dve", lambda e, pr=pr, tp=tp, ch=ch: e.tensor_scalar(
                                DG[pr][:, tp * 128:(tp + 1) * 128], ident_b,
                                smalls[:, C_MCW + ch * 4 + tp: C_MCW + ch * 4 + tp + 1], None, ALU.mult),
                                reads=[B_cb, B_small], writes=[B_DG[pr]])

                        def g_dve(ch_, blk_):
                            for k4 in range(4):
                                tt = blk_ * 4 + k4
                                dgi = (blk_ % 2) * 4 + k4
                                S.op("dve", lambda e, dgi=dgi, tt=tt: e.tensor_scalar(
                                    DGQ[dgi][:, 0:128], ident_f, GINV[:, tt * 4 + ch_: tt * 4 + ch_ + 1], None,
                                    ALU.mult), reads=[B_cf, B_G], writes=[B_DGQ[dgi]])

                        def g_pe(ch_, blk_):
                            gb_ = GBANKS[(ch_ * 4 + blk_) % 3]
                            for k4 in range(4):
                                dgi = (blk_ % 2) * 4 + k4
                                S.op("pe", lambda e, dgi=dgi, k4=k4: e.matmul(
                                    PS[gb_][:, k4 * 128:(k4 + 1) * 128], ones_f, DGQ[dgi][:, 0:128],
                                    start=True, stop=True),
                                    reads=[B_cf, B_DGQ[dgi]], writes=[PB[gb_]])
                        if ch < 4:
                            g_dve(ch, 0)
                            g_dve(ch, 1)
                        for blk in range(4):
                            b = bank % 3
                            bank += 1
                            for kc in range(8):
                                S.op("pe", lambda e, sl=sl, kc=kc, blk=blk, b=b: e.matmul(
                                    PS[b][:, :], W[sl][:, kc * 128:(kc + 1) * 128],
                                    BIGA[:, kc * 2048 + blk * 512: kc * 2048 + (blk + 1) * 512],
                                    start=(kc == 0), stop=(kc == 7)),
                                    reads=[B_W[sl], B_H[kc][blk]], writes=[PB[b]])
                            dst = PRE[pr][:, 3 + blk * 512: 3 + (blk + 1) * 512]
                            if blk % 2 == 0:
                                S.op("act", lambda e, dst=dst, b=b: e.activation(dst, PS[b][:, :], AF.Copy),
                                     writes=[PB[b], B_PRE[pr][blk]])
                            else:
                                S.op("dve", lambda e, dst=dst, b=b: e.tensor_copy(dst, PS[b][:, :]),
                                     writes=[PB[b], B_PRE[pr][blk]])
                        for blk in range(4):
                            b = 3 + (blk % 2)
                            rds = [B_DG[pr], B_PRE[pr][blk], B_PREpad[pr]]
                            if blk > 0:
                                rds.append(B_PRE[pr][blk - 1])
                            for tp in range(4):
                                S.op("pe", lambda e, pr=pr, tp=tp, blk=blk, b=b: e.matmul(
                                    PS[b][:, :], DG[pr][:, tp * 128:(tp + 1) * 128],
                                    PRE[pr][:, blk * 512 + tp: blk * 512 + tp + 512],
                                    start=(tp == 0), stop=(tp == 3)),
                                    reads=rds, writes=[PB[b]])
                            if ch >= 4:
                                S.op("act", lambda e, ch=ch, blk=blk, b=b: e.activation(
                                    QKm[:, ch * 2048 + blk * 512: ch * 2048 + (blk + 1) * 512], PS[b][:, :],
                                    AF.Silu, bias=smalls[:, C_MCB + ch: C_MCB + ch + 1]),
                                    reads=[B_small], writes=[PB[b], B_QKm[ch][blk]])
                            else:
                                qf = qfc[0] % 2
                                qfc[0] += 1
                                gb = GBANKS[(ch * 4 + blk) % 3]
                                S.op("act", lambda e, ch=ch, b=b, qf=qf: e.activation(
                                    QF[qf][:, 0:512], PS[b][:, :],
                                    AF.Silu, bias=smalls[:, C_MCB + ch: C_MCB + ch + 1]),
                                    reads=[B_small], writes=[PB[b], B_QF[qf]])
                                if blk == 0:
                                    g_pe(ch, 0)
                                if blk < 3:
                                    g_pe(ch, blk + 1)
                                S.op("dve", lambda e, ch=ch, blk=blk, qf=qf, gb=gb: e.tensor_tensor(
                                    QKm[:, ch * 2048 + blk * 512: ch * 2048 + (blk + 1) * 512],
                                    QF[qf][:, 0:512], PS[gb][:, 0:512], ALU.mult),
                                    reads=[B_QF[qf]], writes=[PB[gb], B_QKm[ch][blk]])
                                if blk < 2:
                                    g_dve(ch, blk + 2)
                    for ch in range(4):
                        sl = (ch + 2) % NWS
                        S.dma("pool", W[sl][:], d_wfm[(16 + ch) * 128:(17 + ch) * 128, :], writes=[B_W[sl]])
                        for blk in range(4):
                            b = bank % 3
                            bank += 1
                            for kc in range(8):
                                S.op("pe", lambda e, sl=sl, kc=kc, blk=blk, b=b: e.matmul(
                                    PS[b][:, :], W[sl][:, kc * 128:(kc + 1) * 128],
                                    BIGA[:, kc * 2048 + blk * 512: kc * 2048 + (blk + 1) * 512],
                                    start=(kc == 0), stop=(kc == 7)),
                                    reads=[B_W[sl], B_H[kc][blk]], writes=[PB[b]])
                            S.op("act", lambda e, ch=ch, blk=blk, b=b: e.activation(
                                MO[:, ch * 2048 + blk * 512: ch * 2048 + (blk + 1) * 512], PS[b][:, :],
                                AF.Sigmoid), writes=[PB[b], B_MO[ch][blk]])
                    for tt in range(16):
                        b = 5 + (tt % 2)
                        for kc in range(8):
                            S.op("pe", lambda e, kc=kc, tt=tt, b=b: e.matmul(
                                PS[b][:, :], BIGA[:, kc * 2048 + tt * 128: kc * 2048 + (tt + 1) * 128],
                                WTM[:, kc * 512:(kc + 1) * 512],
                                start=(kc == 0), stop=(kc == 7)),
                                reads=[B_WTM, B_H[kc][tt // 4]], writes=[PB[b]])
                        if tt % 2 == 0:
                            S.op("dve", lambda e, tt=tt, b=b: e.tensor_copy(
                                MV[:, tt * 512:(tt + 1) * 512], PS[b][:, :]), writes=[PB[b], B_MV[tt]])
                        else:
                            S.op("act", lambda e, tt=tt, b=b: e.activation(
                                MV[:, tt * 512:(tt + 1) * 512], PS[b][:, :], AF.Copy),
                                writes=[PB[b], B_MV[tt]])

                    S.barrier()
                for m in range(8):
                    S.dma("pool", WO[:, m * 1024:(m + 1) * 1024], d_wout[m * 128:(m + 1) * 128, :],
                          writes=[B_WO[m]])
                PT = [sbt(ph, f"ptm{i}", 256, BF16) for i in range(6)]
                B_PT = [Buf(f"ptm{i}") for i in range(6)]
                DGL = [sbt(ph, f"dgl{i}", 128, F32) for i in range(6)]
                B_DGL = [Buf(f"dgl{i}") for i in range(6)]
                SBANKS = [0, 1, 2, 5]
                NDBANKS = [3, 4, 6]
                T1 = [sbt(ph, f"T1_{i}", 256, F32) for i in range(3)]
                HM = [sbt(ph, f"HM_{i}", 256, F32) for i in range(3)]
                SQh = [sbt(ph, f"SQh_{i}", 256, BF16) for i in range(3)]
                RS = [sbt(ph, f"RS_{i}", 256, F32) for i in range(3)]
                T2 = [sbt(ph, f"T2_{i}", 256, F32) for i in range(3)]
                B_T1 = [Buf(f"T1_{i}") for i in range(3)]
                B_HM = [Buf(f"HM_{i}") for i in range(3)]
                B_SQh = [Buf(f"SQh_{i}") for i in range(3)]
                B_RS = [Buf(f"RS_{i}") for i in range(3)]
                B_T2 = [Buf(f"T2_{i}") for i in range(3)]
                LA = 4
                blocks = [(hm, qb) for hm in range(4) for qb in range(8)]
                items = []
                for n, (hm, qb) in enumerate(blocks):
                    for j in range(2 * qb + 2):
                        items.append((n, hm, qb, j))
                sctr = [0]
                ss_bank = {}

                def m_front(idx):
                    n, hm, qb, j = items[idx]
                    last = (j == 2 * qb + 1)
                    diag = (j >= 2 * qb)
                    q0 = qb * 256 + (128 if last else 0)
                    nq = 128 if last else 256
                    sbk = SBANKS[sctr[0] % 4]
                    sctr[0] += 1
                    pt = idx % 6
                    S.op("pe", lambda e: e.matmul(
                        PS[sbk][:, 0:nq],
                        QKm[:, (4 + hm) * 2048 + j * 128:(4 + hm) * 2048 + (j + 1) * 128],
                        QKm[:, hm * 2048 + q0: hm * 2048 + q0 + nq],
                        start=True, stop=True),
                        reads=[B_QKm[4 + hm][j // 4], B_QKm[hm][q0 // 512]], writes=[PB[sbk]])
                    scol = SCL[:, j * 32 + qb * 4 + hm: j * 32 + qb * 4 + hm + 1]
                    if diag:
                        S.op("dve", lambda e: e.scalar_tensor_tensor(
                            PT[pt][:, 0:128], tri_f, scol, PS[sbk][:, 0:128], ALU.mult, ALU.mult),
                            reads=[B_G, B_cf], writes=[PB[sbk], B_PT[pt]])
                        if nq == 256:
                            S.op("act", lambda e: e.activation(
                                PT[pt][:, 128:256], PS[sbk][:, 128:256], AF.Identity, scale=scol),
                                reads=[B_G], writes=[PB[sbk], B_PT[pt]])
                    elif idx % 2 == 0:
                        S.op("act", lambda e: e.activation(
                            PT[pt][:, 0:nq], PS[sbk][:, 0:nq], AF.Identity, scale=scol),
                            reads=[B_G], writes=[PB[sbk], B_PT[pt]])
                    else:
                        S.op("dve", lambda e: e.tensor_scalar(
                            PT[pt][:, 0:nq], PS[sbk][:, 0:nq], scol, None, ALU.mult),
                            reads=[B_G], writes=[PB[sbk], B_PT[pt]])

                def m_back(idx):
                    n, hm, qb, j = items[idx]
                    last = (j == 2 * qb + 1)
                    nq = 128 if last else 256
                    pt = idx % 6
                    nbk = NDBANKS[n % 3]
                    oc0 = 128 if last else 0
                    S.op("pe", lambda e: e.matmul(
                        PS[nbk][:, oc0:oc0 + nq],
                        MV[:, j * 512 + hm * 128: j * 512 + (hm + 1) * 128], PT[pt][:, 0:nq],
                        start=(j == 0), stop=last, skip_group_check=True),
                        reads=[B_MV[j], B_PT[pt]], writes=[PB[nbk]])
                    S.op("pe", lambda e: e.matmul(
                        PS[nbk][:, 256 + oc0:256 + oc0 + nq], ones_b, PT[pt][:, 0:nq],
                        start=False, stop=last, skip_group_check=True),
                        reads=[B_cb, B_PT[pt]], writes=[PB[nbk]])

                def m_lb_pool(n):
                    hm, qb = blocks[n]
                    for k2 in range(2):
                        tt = 2 * qb + k2
                        dg = 2 * (n % 3) + k2
                        S.op("pool", lambda e: e.tensor_scalar(
                            DGL[dg][:, 0:128], ident_f, LB[:, tt * 4 + hm: tt * 4 + hm + 1], None,
                            ALU.mult), reads=[B_cf, B_G], writes=[B_DGL[dg]])

                def m_lb(n):
                    par = n % 2
                    for k2 in range(2):
                        dg = 2 * (n % 3) + k2
                        S.op("pe", lambda e: e.matmul(
                            PS[7][:, par * 256 + k2 * 128: par * 256 + (k2 + 1) * 128], ones_f,
                            DGL[dg][:, 0:128], start=True, stop=True),
                            reads=[B_cf, B_DGL[dg]], writes=[PB[7]])

                def m_stages(n):
                    hm, qb = blocks[n]
                    par = n % 2
                    st3 = n % 3
                    nbk = NDBANKS[st3]
                    sb = {}

                    def a0():
                        S.op("act", lambda e: e.activation(T1[st3][:, 0:256], PS[nbk][:, 256:512], AF.Abs),
                             writes=[PB[nbk], B_T1[st3]])

                    def a1():
                        S.op("dve", lambda e: e.tensor_scalar(
                            T1[st3][:, 0:256], T1[st3][:, 0:256], 1.0, None, ALU.max), writes=[B_T1[st3]])

                    def b0():
                        S.op("act", lambda e: e.activation(T1[st3][:, 0:256], T1[st3][:, 0:256], AF.Ln),
                             writes=[B_T1[st3]])

                    def b1():
                        S.op("act", lambda e: e.activation(T1[st3][:, 0:256], T1[st3][:, 0:256], AF.Exp,
                                                           scale=-1.0), writes=[B_T1[st3]])

                    def c0():
                        S.op("dve", lambda e: e.tensor_tensor(
                            HM[st3][:, 0:256], PS[nbk][:, 0:256], T1[st3][:, 0:256], ALU.mult),
                            reads=[B_T1[st3]], writes=[PB[nbk], B_HM[st3]])

                    def d0():
                        S.op("act", lambda e: e.activation(SQh[st3][:, 0:256], HM[st3][:, 0:256], AF.Square),
                             reads=[B_HM[st3]], writes=[B_SQh[st3]])

                    def e0():
                        sbk = SBANKS[sctr[0] % 4]
                        sctr[0] += 1
                        sb["b"] = sbk
                        S.op("pe", lambda e: e.matmul(PS[sbk][:, 0:256], onesH[:, :], SQh[st3][:, 0:256],
                                                      start=True, stop=True),
                             reads=[B_ones, B_SQh[st3]], writes=[PB[sbk]])

                    def e1():
                        sbk = sb["b"]
                        S.op("act", lambda e: e.activation(RS[st3][:, 0:256], PS[sbk][:, 0:256], AF.Ln,
                                                           bias=kcol[:, 0:1]),
                             reads=[B_kcol], writes=[PB[sbk], B_RS[st3]])

                    def f0():
                        S.op("act", lambda e: e.activation(RS[st3][:, 0:256], RS[st3][:, 0:256], AF.Exp,
                                                           scale=-0.5), writes=[B_RS[st3]])

                    def g0():
                        S.op("dve", lambda e: e.tensor_tensor(
                            T2[st3][:, 0:256], HM[st3][:, 0:256], RS[st3][:, 0:256], ALU.mult),
                            reads=[B_HM[st3], B_RS[st3]], writes=[B_T2[st3]])

                    def g1():
                        S.op("dve", lambda e: e.scalar_tensor_tensor(
                            OT[:, (4 + hm) * 2048 + qb * 256:(4 + hm) * 2048 + (qb + 1) * 256],
                            T2[st3][:, 0:256], smalls[:, C_GMH + hm: C_GMH + hm + 1],
                            MO[:, hm * 2048 + qb * 256: hm * 2048 + (qb + 1) * 256],
                            ALU.mult, ALU.mult),
                            reads=[B_T2[st3], B_small, B_MO[hm][qb // 2]], writes=[B_OT[4 + hm][qb]])
                    return [(0, a0), (1, a1), (2, b0), (3, b1), (4, c0), (5, d0), (6, e0), (7, e1),
                            (8, f0), (9, g0), (10, g1)]

                first_idx = {}
                last_idx = {}
                for idx, (n, hm, qb, j) in enumerate(items):
                    first_idx.setdefault(n, idx)
                    last_idx[n] = idx
                lb_after_front = {}
                for n in range(len(blocks) - 1):
                    lb_after_front[first_idx[n] + LA] = n + 1
                e1_after_back = {last_idx[n]: n for n in range(len(blocks))}
                pending = []
                NIT = len(items)
                it = 0
                while it < NIT + LA or pending:
                    if it < NIT:
                        m_front(it)
                    due = sorted([p for p in pending if p[0] <= it], key=lambda p: (p[0], p[1]))
                    for p in due:
                        p[2]()
                        pending.remove(p)
                    if LA <= it < NIT + LA:
                        m_back(it - LA)
                        if (it - LA) in e1_after_back:
                            n = e1_after_back[it - LA]
                            for k, (dly, fn) in enumerate(m_stages(n)):
                                if dly == 0:
                                    fn()
                                else:
                                    pending.append((it + dly, n * 10 + k, fn))
                    it += 1
                S.barrier()
            S.barrier()

            if debug_taps:
                S.dma("sp", d_dbg, OT[:], reads=[B_OT[c][q] for c in range(8) for q in range(8)])
            with ExitStack() as ph:
                MIX = [sbt(ph, f"MIX{i}", 4096, F32) for i in range(2)]
                B_MIX = [[Buf(f"mix{i}_{m}") for m in range(8)] for i in range(2)]
                SQ1 = sbt(ph, "sqC", 4096, BF16)
                B_SQ1 = [Buf(f"sqC_{m}") for m in range(8)]
                SQ2 = sbt(ph, "sqC2", 4096, BF16)
                B_SQ2 = Buf("sqC2")
                XB = [sbt(ph, f"xbC{i}", 4096, F32) for i in range(2)]
                B_XBs = [Buf(f"xbC{i}") for i in range(2)]
                RSTD = sbt(ph, "rstdC", 512, F32)
                B_RSTD = Buf("rstdC")
                RSTD2 = sbt(ph, "rstdC2", 512, F32)
                B_RSTD2 = Buf("rstdC2")
                TMPx = [sbt(ph, f"tmpCx{i}", 512, F32) for i in range(2)]
                B_TMPx = [Buf(f"tmpCx{i}") for i in range(2)]
                TMPs = [sbt(ph, f"tmpCs{i}", 512, F32) for i in range(2)]
                B_TMPs = [Buf(f"tmpCs{i}") for i in range(2)]

                def c_chain(blk):
                    i = blk % 2
                    ops = []

                    def load():
                        S.dma("sp", AP(XB[i], 4096, 0, [[512, 8], [1, 512]]), xblk(d_xT, blk),
                              writes=[B_XBs[i]])

                    def s1(m):
                        def f():
                            b = m % 4
                            for kc in range(8):
                                S.op("pe", lambda e, kc=kc: e.matmul(
                                    PS[b][:, :], WO[:, m * 1024 + kc * 128: m * 1024 + (kc + 1) * 128],
                                    OT[:, kc * 2048 + blk * 512: kc * 2048 + (blk + 1) * 512],
                                    start=(kc == 0), stop=(kc == 7)),
                                    reads=[B_WO[m], B_OT[kc][2 * blk], B_OT[kc][2 * blk + 1]],
                                    writes=[PB[b]])
                            S.op("act", lambda e: e.activation(
                                MIX[i][:, m * 512:(m + 1) * 512], PS[b][:, :], AF.Copy),
                                writes=[PB[b], B_MIX[i][m]])
                            S.op("act", lambda e: e.activation(
                                SQ1[:, m * 512:(m + 1) * 512], PS[b][:, :], AF.Square),
                                writes=[PB[b], B_SQ1[m]])
                        return f

                    def ss_ln():
                        for m in range(8):
                            S.op("pe", lambda e, m=m: e.matmul(
                                PS[4][:, :], onesD[:, :], SQ1[:, m * 512:(m + 1) * 512],
                                start=(m == 0), stop=(m == 7)),
                                reads=[B_SQ1[m], B_ones], writes=[PB[4]])
                        S.op("act", lambda e: e.activation(RSTD[:, 0:512], PS[4][:, :], AF.Ln, bias=kcol[:, 0:1]),
                             reads=[B_kcol], writes=[PB[4], B_RSTD])

                    def rs_exp():
                        S.op("act", lambda e: e.activation(RSTD[:, 0:512], RSTD[:, 0:512], AF.Exp, scale=-0.5),
                             writes=[B_RSTD])

                    def x1(m):
                        def f():
                            sl = m % 2
                            S.op("dve", lambda e: e.tensor_tensor(
                                TMPx[sl][:, 0:512], MIX[i][:, m * 512:(m + 1) * 512], RSTD[:, 0:512], ALU.mult),
                                reads=[B_MIX[i][m], B_RSTD], writes=[B_TMPx[sl]])
                            S.op("dve", lambda e: e.scalar_tensor_tensor(
                                XB[i][:, m * 512:(m + 1) * 512], TMPx[sl][:, 0:512], der[:, 8 + m: 9 + m],
                                XB[i][:, m * 512:(m + 1) * 512], ALU.mult, ALU.add),
                                reads=[B_TMPx[sl], B_der2], writes=[B_XBs[i]])
                        return f

                    def store_sq():
                        S.dma("sp", xblk(d_x1, blk), AP(XB[i], 4096, 0, [[512, 8], [1, 512]]),
                              reads=[B_XBs[i]])
                        S.op("act", lambda e: e.activation(SQ2[:, 0:4096], XB[i][:, 0:4096], AF.Square),
                             reads=[B_XBs[i]], writes=[B_SQ2])

                    def ss2_ln():
                        for kc in range(8):
                            S.op("pe", lambda e, kc=kc: e.matmul(
                                PS[5][:, :], onesD[:, :], SQ2[:, kc * 512:(kc + 1) * 512],
                                start=(kc == 0), stop=(kc == 7)),
                                reads=[B_SQ2, B_ones], writes=[PB[5]])
                        S.op("act", lambda e: e.activation(RSTD2[:, 0:512], PS[5][:, :], AF.Ln, bias=kcol[:, 0:1]),
                             reads=[B_kcol], writes=[PB[5], B_RSTD2])

                    def rs2_exp():
                        S.op("act", lambda e: e.activation(RSTD2[:, 0:512], RSTD2[:, 0:512], AF.Exp, scale=-0.5),
                             writes=[B_RSTD2])

                    def s2b(c):
                        def f():
                            sl = c % 2
                            S.op("dve", lambda e: e.tensor_tensor(
                                TMPs[sl][:, 0:512], XB[i][:, c * 512:(c + 1) * 512], RSTD2[:, 0:512], ALU.mult),
                                reads=[B_XBs[i], B_RSTD2], writes=[B_TMPs[sl]])
                            S.op("act", lambda e: e.activation(
                                BIGA[:, c * 2048 + blk * 512: c * 2048 + (blk + 1) * 512], TMPs[sl][:, 0:512],
                                AF.Identity, bias=modT[:, 24 + c:25 + c], scale=der[:, 16 + c:17 + c]),
                                reads=[B_TMPs[sl], B_der2, B_mod2], writes=[B_H[c][blk]])
                        return f

                    noop = lambda: None
                    ops += [s1(m) for m in range(8)]
                    ops[5 if blk >= 2 else 0] = (lambda g=ops[5 if blk >= 2 else 0]: (load(), g()))
                    ops += [ss_ln, rs_exp]
                    ops += [x1(m) for m in range(8)]
                    ops += [store_sq, ss2_ln, rs2_exp]
                    ops += [s2b(c) for c in range(8)]
                    return ops

                chains = [c_chain(blk) for blk in range(4)]
                STAG = 12
                for tick in range(3 * STAG + 29):
                    for blk in range(4):
                        k = tick - blk * STAG
                        if 0 <= k < 29:
                            chains[blk][k]()
                for k_, ch_ in enumerate([0, 22, 1, 23]):
                    S.dma("pool", WA[k_][:], d_wup[ch_ * 128:(ch_ + 1) * 128, :], writes=[B_WA[k_]])
                S.barrier(engines=("act", "dve", "pool", "sp"))

        with ExitStack() as ph:
            G = sbt(ph, "G", 22 * 2048, BF16)
            B_Gc = [[Buf(f"g{i}_{b}") for b in range(4)] for i in range(22)]
            WD2 = sbt(ph, "wd2", 3 * 2816, BF16)
            B_WD = [Buf(f"wd{m}") for m in range(8)]
            with ExitStack() as ph2:
                NWS = 4
                W, B_W = WA, B_WA
                PRE = [sbt(ph2, f"pref{i}", 2048 + 2, BF16) for i in range(4)]
                B_PRE = [[Buf(f"pref{i}_{b}") for b in range(4)] for i in range(4)]
                B_PREpad = [Buf(f"prefpad{i}") for i in range(4)]
                DG = [sbt(ph2, f"dgf{i}", 384, BF16) for i in range(4)]
                B_DG = [Buf(f"dgf{i}") for i in range(4)]
                SAFt = sbt(ph2, "SAFt", 2048, F32)
                B_SAFt = [Buf(f"saft{b}") for b in range(4)]
                for i in range(4):
                    S.op("pool", lambda e, i=i: e.memset(PRE[i][:, 0:2], 0.0), writes=[B_PREpad[i]])
                bank = 0
                wcnt = 0
                for i in range(22):
                    if i == 3:
                        for m in range(5, 8):
                            S.dma("pool", WD2[:, (m - 5) * 2816:(m - 4) * 2816],
                                  d_wdown[m * 128:(m + 1) * 128, :], writes=[B_WD[m]])
                    for half in range(2):
                        ch = i + 22 * half
                        sl = wcnt % NWS
                        pr = wcnt % 4
                        wcnt += 1
                        if wcnt > 4:
                            S.dma("pool", W[sl][:], d_wup[ch * 128:(ch + 1) * 128, :], writes=[B_W[sl]])
                        for tp in range(3):
                            S.op("dve", lambda e, pr=pr, tp=tp, ch=ch: e.tensor_scalar(
                                DG[pr][:, tp * 128:(tp + 1) * 128], ident_b,
                                smalls[:, C_FCW + ch * 3 + tp: C_FCW + ch * 3 + tp + 1], None, ALU.mult),
                                reads=[B_cb, B_small], writes=[B_DG[pr]])
                        for blk in range(4):
                            b = bank % 4
                            bank += 1
                            for kc in range(8):
                                S.op("pe", lambda e, sl=sl, kc=kc, blk=blk, b=b: e.matmul(
                                    PS[b][:, :], W[sl][:, kc * 128:(kc + 1) * 128],
                                    BIGA[:, kc * 2048 + blk * 512: kc * 2048 + (blk + 1) * 512],
                                    start=(kc == 0), stop=(kc == 7)),
                                    reads=[B_W[sl], B_H[kc][blk]], writes=[PB[b]])
                            dst = PRE[pr][:, 2 + blk * 512: 2 + (blk + 1) * 512]
                            if blk % 2 == 0:
                                S.op("act", lambda e, dst=dst, b=b: e.activation(dst, PS[b][:, :], AF.Copy),
                                     writes=[PB[b], B_PRE[pr][blk]])
                            else:
                                S.op("dve", lambda e, dst=dst, b=b: e.tensor_copy(dst, PS[b][:, :]),
                                     writes=[PB[b], B_PRE[pr][blk]])
                        for blk in range(4):
                            b = 4 + 2 * half + (blk % 2)
                            rds = [B_DG[pr], B_PRE[pr][blk], B_PREpad[pr]]
                            if blk > 0:
                                rds.append(B_PRE[pr][blk - 1])
                            for tp in range(3):
                                S.op("pe", lambda e, pr=pr, tp=tp, blk=blk, b=b: e.matmul(
                                    PS[b][:, :], DG[pr][:, tp * 128:(tp + 1) * 128],
                                    PRE[pr][:, blk * 512 + tp: blk * 512 + tp + 512],
                                    start=(tp == 0), stop=(tp == 2)),
                                    reads=rds, writes=[PB[b]])
                            if half == 0:
                                S.op("act", lambda e, ch=ch, blk=blk, b=b: e.activation(
                                    SAFt[:, blk * 512:(blk + 1) * 512], PS[b][:, :], AF.Silu,
                                    bias=smalls[:, C_FCB + ch: C_FCB + ch + 1]),
                                    reads=[B_small], writes=[PB[b], B_SAFt[blk]])
                            else:
                                S.op("dve", lambda e, ch=ch, blk=blk, b=b, i=i: e.scalar_tensor_tensor(
                                    G[:, i * 2048 + blk * 512: i * 2048 + (blk + 1) * 512], PS[b][:, :],
                                    smalls[:, C_FCB + ch: C_FCB + ch + 1],
                                    SAFt[:, blk * 512:(blk + 1) * 512], ALU.add, ALU.mult),
                                    reads=[B_small, B_SAFt[blk]], writes=[PB[b], B_Gc[i][blk]])
                S.barrier()

            with ExitStack() as ph2:

                def wd_ap(m, kc):
                    if m < 5:
                        return BIGA[:, m * 2816 + kc * 128: m * 2816 + (kc + 1) * 128]
                    return WD2[:, (m - 5) * 2816 + kc * 128:(m - 5) * 2816 + (kc + 1) * 128]
                for m in range(5):
                    S.dma("pool", BIGA[:, m * 2816:(m + 1) * 2816], d_wdown[m * 128:(m + 1) * 128, :],
                          writes=[B_WD[m]])
                Y = [sbt(ph2, f"Y{i}", 4096, F32) for i in range(2)]
                B_Y = [[Buf(f"y{i}_{m}") for m in range(8)] for i in range(2)]
                SQ = [sbt(ph2, f"sqE{i}", 512, BF16) for i in range(2)]
                B_SQ = [Buf(f"sqE{i}") for i in range(2)]
                XB = sbt(ph2, "xbE", 4096, F32)
                B_XBm = [Buf(f"xbE{m}") for m in range(8)]
                RSTD = sbt(ph2, "rstdE", 512, F32)
                B_RSTD = Buf("rstdE")
                TMP = [sbt(ph2, f"tmpE{i}", 512, F32) for i in range(2)]
                B_TMP = [Buf(f"tmpE{i}") for i in range(2)]

                MORD = [5, 6, 7, 0, 1, 2, 3, 4]

                def e_ss(blk, mi):
                    sq = mi % 2
                    S.op("pe", lambda e: e.matmul(
                        PS[4 + (blk % 2)][:, :], onesD[:, :], SQ[sq][:, 0:512],
                        start=(mi == 0), stop=(mi == 7)),
                        reads=[B_SQ[sq], B_ones], writes=[PB[4 + (blk % 2)]])

                def e_load(blk):
                    S.dma("sp", AP(XB, 4096, 0, [[512, 8], [1, 512]]), xblk(d_x1, blk), writes=B_XBm)

                def e_chain(blk):
                    yb = blk % 2
                    rsqrt_eps(RSTD[:, 0:512], PS[4 + (blk % 2)][:, :], PB[4 + (blk % 2)], B_RSTD)
                    for m in range(8):
                        sl = m % 2
                        S.op("dve", lambda e: e.tensor_tensor(
                            TMP[sl][:, 0:512], Y[yb][:, m * 512:(m + 1) * 512], RSTD[:, 0:512], ALU.mult),
                            reads=[B_Y[yb][m], B_RSTD], writes=[B_TMP[sl]])
                        S.op("dve", lambda e: e.scalar_tensor_tensor(
                            XB[:, m * 512:(m + 1) * 512], TMP[sl][:, 0:512], der[:, 24 + m: 25 + m],
                            XB[:, m * 512:(m + 1) * 512], ALU.mult, ALU.add),
                            reads=[B_TMP[sl], B_der2], writes=[B_XBm[m]])
                        S.dma("sp", xblk(d_out, blk)[:, m, :], XB[:, m * 512:(m + 1) * 512], reads=[B_XBm[m]])

                for blk in range(4):
                    yb = blk % 2
                    for mi in range(8):
                        m = MORD[mi]
                        b = mi % 4
                        for kc in range(22):
                            S.op("pe", lambda e, kc=kc: e.matmul(
                                PS[b][:, :], wd_ap(m, kc),
                                G[:, kc * 2048 + blk * 512: kc * 2048 + (blk + 1) * 512],
                                start=(kc == 0), stop=(kc == 21)),
                                reads=[B_WD[m], B_Gc[kc][blk]], writes=[PB[b]])
                        S.op("act", lambda e: e.activation(
                            Y[yb][:, m * 512:(m + 1) * 512], PS[b][:, :], AF.Copy),
                            writes=[PB[b], B_Y[yb][m]])
                        if mi == 0 and blk > 0:
                            e_ss(blk - 1, 7)
                            e_load(blk - 1)
                        if mi == 3 and blk > 0:
                            e_chain(blk - 1)
                        if mi > 0:
                            e_ss(blk, mi - 1)
                        sq = mi % 2
                        S.op("act", lambda e: e.activation(SQ[sq][:, 0:512], PS[b][:, :], AF.Square),
                             writes=[PB[b], B_SQ[sq]])
                e_ss(3, 7)
                e_load(3)
                e_chain(3)
                S.barrier()
        S.finish("sp")
    return nc


def _chunkify(w, n_k):
    K, N = w.shape
    n = N // 128
    a = w.reshape(n_k, 128, n, 128).transpose(2, 1, 0, 3)
    return np.ascontiguousarray(a.reshape(n * 128, n_k * 128))


def _vec(v):
    return np.ascontiguousarray(v.reshape(-1, 128).T)


def _prep(inputs):
    f = lambda k: np.asarray(inputs[k], dtype=np.float32)
    w_in = f("w_in")[0]
    fm_cols = np.r_[0:512, 512:1024, 1544:2056, 2056:2568, 3088:3600]
    wfm = _chunkify(w_in[:, fm_cols], 8)
    tm_cols = np.r_[1024:1536, 2568:3080, 1536:1544, 3080:3084, 3084:3088]
    wtm = np.ascontiguousarray(
        w_in[:, tm_cols].reshape(8, 128, 1040).transpose(1, 0, 2).reshape(128, 8 * 1040))
    wout = _chunkify(f("w_out")[0], 8)
    wup = _chunkify(f("w_up")[0], 8)
    wdown = _chunkify(f("w_down")[0], 22)
    wada = _chunkify(f("w_ada")[0], 8)
    sm = np.zeros((128, NSM), np.float32)
    sm[:, C_BADA:C_BADA + 48] = _vec(f("b_ada")[0])
    sm[:, C_GPM:C_GPM + 8] = _vec(f("g_pre_mix")[0])
    sm[:, C_GOM:C_GOM + 8] = _vec(f("g_post_mix")[0])
    sm[:, C_GPF:C_GPF + 8] = _vec(f("g_pre_ffn")[0])
    sm[:, C_GOF:C_GOF + 8] = _vec(f("g_post_ffn")[0])
    mcw = f("mlstm_conv_w")[0]
    sm[:, C_MCW:C_MCW + 32] = mcw.reshape(4, 8, 128).transpose(2, 1, 0).reshape(128, 32)
    sm[:, C_MCB:C_MCB + 8] = _vec(f("mlstm_conv_b")[0])
    sm[:, C_GMH:C_GMH + 4] = _vec(f("g_mlstm_head")[0])
    fcw = f("ffn_conv_w")[0]
    sm[:, C_FCW:C_FCW + 132] = fcw.reshape(3, 44, 128).transpose(2, 1, 0).reshape(128, 132)
    sm[:, C_FCB:C_FCB + 44] = _vec(f("ffn_conv_b")[0])
    sm[:, C_BG:C_BG + 16] = f("b_gate")[0][None, :]
    consts = np.zeros((128, 512), np.float32)
    k = np.arange(128)[:, None]
    q = np.arange(128)[None, :]
    consts[:, 0:128] = (k == q)
    consts[:, 128:256] = (q >= k)
    consts[:, 256:384] = 1.0
    consts[:, 384:512] = np.where(k > q, NEG, 0.0)
    x = f("x")
    c = f("c")
    in_maps = []
    for b in range(NCORES):
        smb = sm.copy()
        smb[:, C_C:C_C + 8] = _vec(c[b])
        in_maps.append({"xT": np.ascontiguousarray(x[b].T), "wfm": wfm, "wtm": wtm, "wout": wout,
                        "wup": wup, "wdown": wdown, "wada": wada, "smalls": smb, "consts": consts})
    return in_maps


_NC_CACHE = {}


def kernel(**inputs):
    in_maps = _prep(inputs)
    if "nc" not in _NC_CACHE:
        _NC_CACHE["nc"] = build_nc()
    res = run_bass_kernel_spmd(_NC_CACHE["nc"], in_maps, core_ids=list(range(NCORES)))
    out = np.stack([np.ascontiguousarray(r["outT"].T) for r in res.results], axis=0)
    return out.astype(np.float32)
```

```python
import bisect
import math
from contextlib import ExitStack

import numpy as np
import concourse.bass as bass
import concourse.mybir as mybir
from concourse.bass_utils import run_bass_kernel_spmd

F32 = mybir.dt.float32
BF16 = mybir.dt.bfloat16
AF = mybir.ActivationFunctionType
ALU = mybir.AluOpType

NCORES = 8
SEQ = 2048
DM = 1024
DFF = 2816
EPS = 1e-6
NEG = -30000.0
EXPSHIFT = 9.0

C_C = 0
C_BADA = 8
C_GPM = 56
C_GOM = 64
C_GPF = 72
C_GOF = 80
C_MCW = 88
C_MCB = 120
C_GMH = 128
C_FCW = 132
C_FCB = 264
C_BG = 308
NSM = 324


class Buf:
    __slots__ = ("name", "w", "r")

    def __init__(self, name):
        self.name = name
        self.w = None
        self.r = []


class Sched:
    EPOCH = 4000
    NDMA = 12

    def __init__(self, nc, stack):
        self.nc = nc
        self.stack = stack
        self.eng = {"pe": nc.tensor, "act": nc.scalar, "dve": nc.vector,
                    "pool": nc.gpsimd, "sp": nc.sync}
        self.seq = {e: 0 for e in self.eng}
        self.pending = {e: {} for e in self.eng}
        self.signaled = {e: [] for e in self.eng}
        self.sems = {e: [] for e in self.eng}
        self.waited = {}
        self.dma_sems = {}
        self.dma_cnt = {}
        self.ndma = {}
        self.nwaits = 0

    def _sem(self, e, idx):
        while len(self.sems[e]) <= idx:
            self.sems[e].append(self.stack.enter_context(
                self.nc.semaphore(f"s_{e}_{len(self.sems[e])}")))
        return self.sems[e][idx]

    def _target(self, tok):
        if tok[0] == "dma":
            _, q, s, cnt = tok
            return ("dma", q, s), self.dma_sems[q][s], 16 * cnt
        e, n = tok
        sg = self.signaled[e]
        i = bisect.bisect_left(sg, n)
        if i == len(sg):
            ins = self.pending[e][n]
            rank = len(sg)
            ins.then_inc(self._sem(e, rank // self.EPOCH), 1)
            sg.append(n)
            for k in [k for k in self.pending[e] if k <= n]:
                del self.pending[e][k]
            i = rank
        return (e, i // self.EPOCH), self._sem(e, i // self.EPOCH), i % self.EPOCH + 1

    def _wait(self, waiter, toks):
        need = {}
        for tok in toks:
            if tok is None:
                continue
            if tok[0] == "pe" and waiter == "pe":
                continue
            key, sem, val = self._target(tok)
            if key[0] == "dma":
                if self.waited.get((waiter, key), 0) >= val:
                    continue
                if need.get(key, (None, 0))[1] < val:
                    need[key] = (sem, val)
            else:
                e, ep = key
                if self.waited.get((waiter, e), (-1, 0)) >= (ep, val):
                    continue
                k2 = ("eng", e)
                if need.get(k2, (None, (-1, 0)))[1] < (ep, val):
                    need[k2] = (sem, (ep, val))
        for key, (sem, val) in need.items():
            if key[0] == "dma":
                self.eng[waiter].wait_ge(sem, val)
                self.waited[(waiter, key)] = val
            else:
                self.eng[waiter].wait_ge(sem, val[1])
                self.waited[(waiter, key[1])] = val
            self.nwaits += 1

    @staticmethod
    def _deps(reads, writes):
        toks = []
        for b in reads:
            toks.append(b.w)
        for b in writes:
            toks.append(b.w)
            toks.extend(b.r)
        return toks

    @staticmethod
    def _commit(tok, reads, writes):
        for b in writes:
            b.w = tok
            b.r = []
        for b in reads:
            if b not in writes:
                b.r.append(tok)

    def op(self, e, fn, reads=(), writes=()):
        reads = list(reads)
        writes = list(writes)
        self._wait(e, self._deps(reads, writes))
        ins = fn(self.eng[e])
        n = self.seq[e]
        self.seq[e] += 1
        if e == "pe":
            self.pending[e][n] = ins
        else:
            rank = len(self.signaled[e])
            ins.then_inc(self._sem(e, rank // self.EPOCH), 1)
            self.signaled[e].append(n)
        self._commit((e, n), reads, writes)
        return ins

    def dma(self, q, out, in_, reads=(), writes=()):
        reads = list(reads)
        writes = list(writes)
        if q not in self.dma_sems:
            self.dma_sems[q] = [self.stack.enter_context(self.nc.semaphore(f"dq_{q}_{i}"))
                                for i in range(self.NDMA)]
            self.dma_cnt[q] = [0] * self.NDMA
            self.ndma[q] = 0
        s = self.ndma[q] % self.NDMA
        self.ndma[q] += 1
        toks = self._deps(reads, writes)
        if self.dma_cnt[q][s] > 0:
            toks.append(("dma", q, s, self.dma_cnt[q][s]))
        self._wait(q, toks)
        self.dma_cnt[q][s] += 1
        self.eng[q].dma_start(out=out, in_=in_).then_inc(self.dma_sems[q][s], 16)
        self._commit(("dma", q, s, self.dma_cnt[q][s]), reads, writes)

    def _all_tokens(self):
        toks = []
        for e in self.eng:
            if self.seq[e] > 0:
                toks.append((e, self.seq[e] - 1))
        for q in self.dma_sems:
            for s in range(self.NDMA):
                if self.dma_cnt[q][s] > 0:
                    toks.append(("dma", q, s, self.dma_cnt[q][s]))
        return toks

    def barrier(self, engines=("pe", "act", "dve", "pool", "sp")):
        toks = self._all_tokens()
        for e in engines:
            self._wait(e, [t for t in toks if not (t[0] == e)])

    def finish(self, e):
        self._wait(e, self._all_tokens())


def build_nc(debug_taps=False):
    nc = bass.Bass("TRN2", target_bir_lowering=False)
    d_xT = nc.dram_tensor("xT", [DM, SEQ], F32, kind="ExternalInput").ap()
    d_wfm = nc.dram_tensor("wfm", [20 * 128, 1024], F32, kind="ExternalInput").ap()
    d_wtm = nc.dram_tensor("wtm", [128, 8 * 1040], F32, kind="ExternalInput").ap()
    d_wout = nc.dram_tensor("wout", [8 * 128, 1024], F32, kind="ExternalInput").ap()
    d_wup = nc.dram_tensor("wup", [44 * 128, 1024], F32, kind="ExternalInput").ap()
    d_wdown = nc.dram_tensor("wdown", [8 * 128, 22 * 128], F32, kind="ExternalInput").ap()
    d_wada = nc.dram_tensor("wada", [48 * 128, 1024], F32, kind="ExternalInput").ap()
    d_smalls = nc.dram_tensor("smalls", [128, NSM], F32, kind="ExternalInput").ap()
    d_consts = nc.dram_tensor("consts", [128, 512], F32, kind="ExternalInput").ap()
    d_out = nc.dram_tensor("outT", [DM, SEQ], F32, kind="ExternalOutput").ap()
    d_x1 = nc.dram_tensor("x1s", [DM, SEQ], F32).ap()
    d_dbg = nc.dram_tensor("dbg_ot", [128, 8 * 2048], BF16, kind="ExternalOutput").ap() if debug_taps else None

    def xblk(d, blk):
        return d.rearrange("(c p) t -> p c t", p=128)[:, :, blk * 512:(blk + 1) * 512]

    with ExitStack() as st:
        S = Sched(nc, st)

        def sbt(stack, name, F, dt):
            return stack.enter_context(nc.sbuf_tensor("sb_" + name, [128, F], dt))

        def AP(t, F, off, dims, p0=0, npart=128):
            return bass.AP(t, p0 * F + off, [[F, npart]] + [list(d) for d in dims])

        PS = [st.enter_context(nc.psum_tensor(f"ps{i}", [128, 512], F32)) for i in range(8)]
        PB = [Buf(f"ps{i}") for i in range(8)]

        smalls = sbt(st, "smalls", NSM, F32)
        cf = sbt(st, "cf", 512, F32)
        cb = sbt(st, "cb", 512, BF16)
        onesD = sbt(st, "onesD", 128, BF16)
        onesH = sbt(st, "onesH", 128, BF16)
        cs_bf = sbt(st, "cs_bf", 8, BF16)
        modT = sbt(st, "modT", 48, F32)
        der = sbt(st, "der", 32, F32)
        BIGA = sbt(st, "BIGA", 8 * 2048, BF16)
        B_small, B_cf, B_cb = Buf("smalls"), Buf("cf"), Buf("cb")
        B_ones, B_cs, B_mod, B_der = Buf("ones"), Buf("cs"), Buf("mod"), Buf("der")
        B_H = [[Buf(f"h{c}_{b}") for b in range(4)] for c in range(8)]
        B_OT = [[Buf(f"ot{c}_{q}") for q in range(8)] for c in range(8)]

        ident_f = cf[:, 0:128]
        tri_f = cf[:, 128:256]
        ones_f = cf[:, 256:384]
        ident_b = cb[:, 0:128]
        tri_b = cb[:, 128:256]
        ones_b = cb[:, 256:384]
        trineg_b = cb[:, 384:512]

        kcol = sbt(st, "kcol", 8, F32)
        B_kcol = Buf("kcol")
        S.op("dve", lambda e: e.memset(kcol[:, 0:1], EPS), writes=[B_kcol])
        S.op("dve", lambda e: e.memset(kcol[:, 1:2], EXPSHIFT - 0.5 * math.log(128.0)), writes=[B_kcol])
        S.op("dve", lambda e: e.memset(kcol[:, 3:4], EXPSHIFT), writes=[B_kcol])
        S.op("dve", lambda e: e.memset(kcol[:, 4:5], -EXPSHIFT), writes=[B_kcol])
        S.op("dve", lambda e: e.memset(kcol[:, 2:3], 1.0), writes=[B_kcol])

        def rsqrt_eps(out_ap, in_ap, B_in, B_out):
            S.op("act", lambda e: e.activation(out_ap, in_ap, AF.Ln, bias=kcol[:, 0:1]),
                 reads=[B_kcol], writes=[B_in, B_out])
            S.op("act", lambda e: e.activation(out_ap, out_ap, AF.Exp, scale=-0.5), writes=[B_out])

        S.dma("sp", smalls[:], d_smalls, writes=[B_small])
        S.dma("sp", cf[:], d_consts, writes=[B_cf])
        S.dma("pool", cb[:], d_consts, writes=[B_cb])
        S.op("dve", lambda e: e.memset(onesD[:], 2.0 ** -10), writes=[B_ones])
        S.op("dve", lambda e: e.memset(onesH[:], 2.0 ** -7), writes=[B_ones])

        NW = 4
        WA = [sbt(st, f"wa{i}", 1024, BF16) for i in range(NW)]
        B_WA = [Buf(f"wa{i}") for i in range(NW)]
        B_mod2, B_der2 = Buf("mod2"), Buf("der2")
        S.op("act", lambda e: e.activation(cs_bf[:], smalls[:, C_C:C_C + 8], AF.Silu),
             reads=[B_small], writes=[B_cs])

        B_W16 = [Buf(f"w16_{j}") for j in range(16)]

        def mod_chunk(j, bank, B_m):
            sl = j % NW
            if j < 16:
                wsl, B_wsl = BIGA[:, j * 1024:(j + 1) * 1024], B_W16[j]
            else:
                wsl, B_wsl = WA[sl][:], B_WA[sl]
            S.dma("pool", wsl, d_wada[j * 128:(j + 1) * 128, :], writes=[B_wsl])
            for kc in range(8):
                S.op("pe", lambda e, kc=kc: e.matmul(
                    PS[bank][:, 0:1], wsl[:, kc * 128:(kc + 1) * 128], cs_bf[:, kc:kc + 1],
                    start=(kc == 0), stop=(kc == 7)),
                    reads=[B_wsl, B_cs], writes=[PB[bank]])
            if j < 16:
                S.op("act", lambda e: e.activation(modT[:, j:j + 1], PS[bank][:, 0:1], AF.Identity,
                                                   bias=smalls[:, C_BADA + j:C_BADA + j + 1]),
                     reads=[B_small], writes=[PB[bank], B_m])
            else:
                S.op("dve", lambda e: e.tensor_tensor(modT[:, j:j + 1], PS[bank][:, 0:1],
                                                      smalls[:, C_BADA + j:C_BADA + j + 1], ALU.add),
                     reads=[B_small], writes=[PB[bank], B_m])

        for j in range(16):
            mod_chunk(j, 6 + (j % 2), B_mod)

        def mod_finish():
            S.op("dve", lambda e: e.tensor_tensor(
                der[:, 8:16], modT[:, 16:24], smalls[:, C_GOM:C_GOM + 8], ALU.mult),
                reads=[B_mod2, B_small], writes=[B_der2])
            S.op("dve", lambda e: e.scalar_tensor_tensor(
                der[:, 16:24], modT[:, 32:40], 1.0, smalls[:, C_GPF:C_GPF + 8], ALU.add, ALU.mult),
                reads=[B_mod2, B_small], writes=[B_der2])
            S.op("dve", lambda e: e.tensor_tensor(
                der[:, 24:32], modT[:, 40:48], smalls[:, C_GOF:C_GOF + 8], ALU.mult),
                reads=[B_mod2, B_small], writes=[B_der2])

        def norm_mod(XBt, B_XB, blk, gs0, sh0, SQ, B_SQ, RSTD, B_RSTD, TMP, B_TMP, bank):
            S.op("act", lambda e: e.activation(SQ[:, 0:4096], XBt[:, 0:4096], AF.Square),
                 reads=[B_XB], writes=[B_SQ])
            for kc in range(8):
                S.op("pe", lambda e, kc=kc: e.matmul(
                    PS[bank][:, :], onesD[:, :], SQ[:, kc * 512:(kc + 1) * 512],
                    start=(kc == 0), stop=(kc == 7)),
                    reads=[B_SQ, B_ones], writes=[PB[bank]])
            rsqrt_eps(RSTD[:, 0:512], PS[bank][:, :], PB[bank], B_RSTD)
            for c in range(8):
                sl = c % 2
                S.op("dve", lambda e, c=c, sl=sl: e.tensor_tensor(
                    TMP[sl][:, 0:512], XBt[:, c * 512:(c + 1) * 512], RSTD[:, 0:512], ALU.mult),
                    reads=[B_XB, B_RSTD], writes=[B_TMP[sl]])
                S.op("act", lambda e, c=c, sl=sl: e.activation(
                    BIGA[:, c * 2048 + blk * 512: c * 2048 + (blk + 1) * 512], TMP[sl][:, 0:512],
                    AF.Identity, bias=sh0(c), scale=gs0(c)),
                    reads=[B_TMP[sl], B_der, B_mod, B_der2, B_mod2], writes=[B_H[c][blk]])

        with ExitStack() as phB:
            OT = sbt(phB, "OT", 8 * 2048, BF16)
            GT = sbt(phB, "GT", 256, F32)
            GX = sbt(phB, "GX", 256, F32)
            SPc = sbt(phB, "SPc", 256, F32)
            OFFS = sbt(phB, "OFFS", 17 * 16, F32)
            CS = sbt(phB, "CS", 256, F32)
            MID = sbt(phB, "MID", 128, F32)
            BIASF = sbt(phB, "BIASF", 1024, F32)
            AI = sbt(phB, "AI", 64, F32)
            ARG = sbt(phB, "ARG", 512, F32)
            SCL = sbt(phB, "SCL", 512, F32)
            LBA = sbt(phB, "LBA", 64, F32)
            LB = sbt(phB, "LB", 64, F32)
            GINV = sbt(phB, "GINV", 64, F32)
            B_G = Buf("gates")

            with ExitStack() as ph:
                WTM = sbt(ph, "WTM", 8 * 1040, BF16)
                B_WTM = Buf("wtm")
                S.dma("pool", WTM[:], d_wtm, writes=[B_WTM])
                QK = sbt(ph, "QKf", 4 * 2048, BF16)
                QZ = sbt(ph, "QZf", 8 * 2048, BF16)
                B_QK = [[Buf(f"qk{c}_{b}") for b in range(4)] for c in range(8)]
                B_QZinit = Buf("qzinit")
                VA = sbt(ph, "VA", 16 * 1024, BF16)
                B_VA = [Buf(f"va{j}") for j in range(16)]
                B_VAinit = Buf("vainit")
                NWS = 4
                W, B_W = WA, B_WA

                XBv = [OT[:, 0:8192].bitcast(F32), OT[:, 8192:16384].bitcast(F32)]
                S.op("dve", lambda e: e.memset(VA[:], 1.0), writes=B_VA)
                S.op("dve", lambda e: e.memset(AP(QZ, 16384, 5 * 2048, [[4096, 2], [1, 2048]], p0=0, npart=64), 0.0),
                     writes=[B_QZinit] + [B_QK[c][b] for c in (2, 3) for b in range(4)])
                S.op("dve", lambda e: e.memset(AP(QZ, 16384, 4 * 2048, [[4096, 2], [1, 2048]], p0=64, npart=64), 0.0),
                     writes=[B_QZinit] + [B_QK[c][b] for c in (2, 3) for b in range(4)])
                S.op("dve", lambda e: e.scalar_tensor_tensor(
                    der[:, 0:8], modT[:, 8:16], 1.0, smalls[:, C_GPM:C_GPM + 8], ALU.add, ALU.mult),
                    reads=[B_mod, B_small], writes=[B_der])
                SQv = QZ[:, 0:4096]
                RSTDv = QZ[:, 4096:5120].bitcast(F32)
                TMPv = [QZ[:, 5120:6144].bitcast(F32), QZ[:, 6144:7168].bitcast(F32)]
                B_XBs = [Buf(f"xbA{i}") for i in range(2)]
                B_SQA, B_RSTDA = Buf("sqA"), Buf("rstdA")
                B_TMPA = [Buf(f"tmpA{i}") for i in range(2)]
                A_BUFS = B_XBs + [B_SQA, B_RSTDA] + B_TMPA
                kch = [4, 5, 6, 7]
                for ci, ch in enumerate(kch):
                    S.dma("pool", W[ci][:], d_wfm[ch * 128:(ch + 1) * 128, :], writes=[B_W[ci]])
                RSTDw = [RSTDv, QZ[:, 7168:8192].bitcast(F32)]
                B_RSTDw = [B_RSTDA, Buf("rstdA1")]
                A_BUFS = A_BUFS + [B_RSTDw[1]]

                def a_load(blk):
                    xi = blk % 2
                    S.dma("sp", XBv[xi].rearrange("p (c t) -> p c t", c=8), xblk(d_xT, blk),
                          reads=(B_WA if blk == 1 else []), writes=[B_XBs[xi]])

                def a_stats(blk):
                    xi = blk % 2
                    pbank = 6 + (blk % 2)
                    S.op("act", lambda e: e.activation(SQv[:, 0:4096], XBv[xi][:, 0:4096], AF.Square),
                         reads=[B_XBs[xi]], writes=[B_SQA])
                    for kc in range(8):
                        S.op("pe", lambda e, kc=kc: e.matmul(
                            PS[pbank][:, :], onesD[:, :], SQv[:, kc * 512:(kc + 1) * 512],
                            start=(kc == 0), stop=(kc == 7)),
                            reads=[B_SQA, B_ones], writes=[PB[pbank]])
                    rsqrt_eps(RSTDw[xi][:, 0:512], PS[pbank][:, :], PB[pbank], B_RSTDw[xi])

                def a_h(blk):
                    xi = blk % 2
                    for c in range(8):
                        sl = c % 2
                        S.op("dve", lambda e, c=c, sl=sl: e.tensor_tensor(
                            TMPv[sl][:, 0:512], XBv[xi][:, c * 512:(c + 1) * 512], RSTDw[xi][:, 0:512],
                            ALU.mult), reads=[B_XBs[xi], B_RSTDw[xi]], writes=[B_TMPA[sl]])
                        S.op("act", lambda e, c=c, sl=sl: e.activation(
                            BIGA[:, c * 2048 + blk * 512: c * 2048 + (blk + 1) * 512], TMPv[sl][:, 0:512],
                            AF.Identity, bias=modT[:, c:c + 1], scale=der[:, c:c + 1]),
                            reads=[B_TMPA[sl], B_der, B_mod], writes=[B_H[c][blk]])

                bank = 0
                a_load(0)
                a_load(1)
                a_stats(0)
                for blk in range(4):
                    a_h(blk)
                    groups = []
                    for ci, ch in enumerate(kch):
                        b = bank % 4
                        bank += 1
                        for kc in range(8):
                            S.op("pe", lambda e, ci=ci, kc=kc, blk=blk, b=b: e.matmul(
                                PS[b][:, :], W[ci][:, kc * 128:(kc + 1) * 128],
                                BIGA[:, kc * 2048 + blk * 512: kc * 2048 + (blk + 1) * 512],
                                start=(kc == 0), stop=(kc == 7)),
                                reads=[B_W[ci], B_H[kc][blk]], writes=[PB[b]])
                        groups.append((ci, ch, b))
                    if blk + 2 < 4:
                        a_load(blk + 2)
                    if blk + 1 < 4:
                        a_stats(blk + 1)
                    for ci, ch, b in groups:
                        dst = QK[:, (ch - 4) * 2048 + blk * 512: (ch - 4) * 2048 + (blk + 1) * 512]
                        if ci % 2 == 0:
                            S.op("dve", lambda e, dst=dst, b=b: e.tensor_copy(dst, PS[b][:, :]),
                                 writes=[PB[b], B_QK[ch][blk]])
                        else:
                            S.op("act", lambda e, dst=dst, b=b: e.activation(dst, PS[b][:, :], AF.Copy),
                                 writes=[PB[b], B_QK[ch][blk]])
                S.op("dve", lambda e: e.memset(AP(QZ, 16384, 2048, [[4096, 2], [1, 2048]], p0=0, npart=64), 0.0),
                     writes=[B_QZinit] + A_BUFS + [B_QK[c][b] for c in (0, 1) for b in range(4)])
                S.op("act", lambda e: e.memzero(AP(QZ, 16384, 0, [[4096, 2], [1, 2048]], p0=64, npart=64)),
                     writes=[B_QZinit] + A_BUFS + [B_QK[c][b] for c in (0, 1) for b in range(4)])
                for tt in range(16):
                    for kc in range(8):
                        S.op("pe", lambda e, kc=kc, tt=tt: e.matmul(
                            PS[6][:, tt * 16:(tt + 1) * 16],
                            BIGA[:, kc * 2048 + tt * 128: kc * 2048 + (tt + 1) * 128],
                            WTM[:, kc * 1040 + 1024: kc * 1040 + 1040],
                            start=(kc == 0), stop=(kc == 7)),
                            reads=[B_WTM, B_H[kc][tt // 4]], writes=[PB[6]])
                GQ = []

                def GQ_add(*a, **k):
                    GQ.append((a, k))

                def gq_pop(n):
                    for _ in range(n):
                        if GQ:
                            a, k = GQ.pop(0)
                            S.op(*a, **k)
                GQ_add("dve", lambda e: e.tensor_copy(GT[:, 0:256], PS[6][:, 0:256]),
                     writes=[PB[6], B_G])

                GQ_add("dve", lambda e: e.tensor_tensor(
                    AP(GX, 256, 0, [[16, 16], [1, 16]]), AP(GT, 256, 0, [[16, 16], [1, 16]]),
                    AP(smalls, NSM, C_BG, [[0, 16], [1, 16]]), ALU.add),
                    reads=[B_small], writes=[B_G])
                GQ_add("act", lambda e: e.activation(SPc[:, 0:256], GX[:, 0:256], AF.Exp, scale=-1.0),
                     writes=[B_G])
                GQ_add("act", lambda e: e.activation(SPc[:, 0:256], SPc[:, 0:256], AF.Ln, bias=kcol[:, 2:3]),
                     writes=[B_G])
                GQ_add("pe", lambda e: e.matmul(PS[6][:, 0:256], tri_f, SPc[:, 0:256],
                                              start=True, stop=True),
                     reads=[B_G, B_cf], writes=[PB[6]])
                GQ_add("pe", lambda e: e.matmul(PS[7][:, 0:256], ones_f, SPc[:, 0:256],
                                              start=True, stop=True),
                     reads=[B_G, B_cf], writes=[PB[7]])
                GQ_add("dve", lambda e: e.memset(OFFS[:, 0:16], 0.0), writes=[B_G])
                for j in range(16):
                    GQ_add("dve", lambda e, j=j: e.tensor_tensor(
                        OFFS[:, (j + 1) * 16:(j + 2) * 16], PS[7][:, j * 16:(j + 1) * 16],
                        OFFS[:, j * 16:(j + 1) * 16], ALU.add),
                        writes=[B_G, PB[7]])
                GQ_add("dve", lambda e: e.tensor_tensor(CS[:, 0:256], PS[6][:, 0:256],
                                                      OFFS[:, 0:256], ALU.add),
                     writes=[B_G, PB[6]])
                GQ_add("dve", lambda e: e.tensor_tensor(
                    AP(MID, 128, 0, [[16, 8], [1, 16]]), AP(OFFS, 272, 0, [[32, 8], [1, 16]]),
                    AP(OFFS, 272, 32, [[32, 8], [1, 16]]), ALU.add), writes=[B_G])
                GQ_add("dve", lambda e: e.tensor_scalar(MID[:, 0:128], MID[:, 0:128], 0.5, None,
                                                      ALU.mult), writes=[B_G])
                for j in range(16):
                    GQ_add("dve", lambda e, j=j: e.tensor_tensor(
                        AP(BIASF, 1024, j * 64, [[8, 8], [1, 8]]),
                        AP(CS, 256, j * 16, [[0, 8], [1, 8]]),
                        AP(MID, 128, 0, [[16, 8], [1, 8]]), ALU.subtract), writes=[B_G])
                GQ_add("dve", lambda e: e.tensor_scalar(BIASF[:, 0:1024], BIASF[:, 0:1024], EXPSHIFT, 78.0,
                                                      ALU.add, ALU.min), writes=[B_G])
                GQ_add("dve", lambda e: e.tensor_tensor(
                    AP(AI, 64, 0, [[4, 16], [1, 4]]), AP(CS, 256, 12, [[16, 16], [1, 4]]),
                    AP(GX, 256, 8, [[16, 16], [1, 4]]), ALU.add), writes=[B_G])
                for j in range(16):
                    GQ_add("dve", lambda e, j=j: e.tensor_tensor(
                        AP(ARG, 512, j * 32, [[4, 8], [1, 4]]),
                        AP(AI, 64, j * 4, [[0, 8], [1, 4]]),
                        AP(MID, 128, 12, [[16, 8], [1, 4]]), ALU.subtract), writes=[B_G])
                GQ_add("dve", lambda e: e.tensor_scalar(ARG[:, 0:512], ARG[:, 0:512], 72.0, None,
                                                      ALU.min), writes=[B_G])
                GQ_add("act", lambda e: e.activation(SCL[:, 0:512], ARG[:, 0:512], AF.Exp,
                                                   bias=kcol[:, 1:2]), reads=[B_kcol], writes=[B_G])
                GQ_add("dve", lambda e: e.tensor_tensor(
                    AP(LBA, 64, 0, [[8, 8], [4, 2], [1, 4]]),
                    AP(CS, 256, 12, [[32, 8], [16, 2], [1, 4]]),
                    AP(MID, 128, 12, [[16, 8], [0, 2], [1, 4]]), ALU.subtract), writes=[B_G])
                GQ_add("act", lambda e: e.activation(LB[:, 0:64], LBA[:, 0:64], AF.Exp, bias=kcol[:, 3:4]),
                     reads=[B_kcol], writes=[B_G])
                GQ_add("act", lambda e: e.activation(GINV[:, 0:64], LBA[:, 0:64], AF.Exp, bias=kcol[:, 4:5],
                                                   scale=-1.0), reads=[B_kcol], writes=[B_G])

                for ci, ch in enumerate([2, 3, 0, 1]):
                    sl = ci % NWS
                    S.dma("pool", W[sl][:], d_wfm[ch * 128:(ch + 1) * 128, :], writes=[B_W[sl]])
                    for blk in range(4):
                        b = bank % 4
                        bank += 1
                        for kc in range(8):
                            S.op("pe", lambda e, sl=sl, kc=kc, blk=blk, b=b: e.matmul(
                                PS[b][:, :], W[sl][:, kc * 128:(kc + 1) * 128],
                                BIGA[:, kc * 2048 + blk * 512: kc * 2048 + (blk + 1) * 512],
                                start=(kc == 0), stop=(kc == 7)),
                                reads=[B_W[sl], B_H[kc][blk]], writes=[PB[b]])
                        for hh in range(2):
                            hq = 2 * ch + hh
                            S.op("act", lambda e, b=b, hh=hh, hq=hq, blk=blk: e.activation(
                                QZ[hh * 64:(hh + 1) * 64, hq * 2048 + blk * 512: hq * 2048 + (blk + 1) * 512],
                                PS[b][hh * 64:(hh + 1) * 64, :], AF.Identity, scale=0.125),
                                writes=[PB[b], B_QK[ch][blk]])
                        gq_pop(3)
                for tt in range(16):
                    b = 4 + (tt % 2)
                    for kc in range(8):
                        S.op("pe", lambda e, kc=kc, tt=tt, b=b: e.matmul(
                            PS[b][:, :], BIGA[:, kc * 2048 + tt * 128: kc * 2048 + (tt + 1) * 128],
                            WTM[:, kc * 1040: kc * 1040 + 512],
                            start=(kc == 0), stop=(kc == 7)),
                            reads=[B_WTM, B_H[kc][tt // 4]], writes=[PB[b]])
                    S.op("act", lambda e, tt=tt, b=b: e.activation(
                        AP(VA, 16384, tt * 1024, [[256, 4], [1, 64]]),
                        AP(PS[b], 512, 0, [[128, 4], [1, 64]]), AF.Copy),
                        writes=[PB[b], B_VA[tt]])
                    S.op("dve", lambda e, tt=tt, b=b: e.tensor_copy(
                        AP(VA, 16384, tt * 1024 + 128 + 64, [[256, 4], [1, 64]]),
                        AP(PS[b], 512, 64, [[128, 4], [1, 64]])),
                        writes=[PB[b], B_VA[tt]])
                    gq_pop(3)
                gq_pop(10 ** 6)
                PT = [sbt(ph, f"ptf{i}", 256, BF16) for i in range(6)]
                B_PT = [Buf(f"ptf{i}") for i in range(6)]
                RD = [sbt(ph, f"rdf{i}", 256, F32) for i in range(2)]
                B_RD = [Buf(f"rdf{i}") for i in range(2)]
                LA = 4
                items = [(h, qb, j) for h in range(8) for qb in range(8) for j in range(2 * qb + 2)]

                def fox_front(idx):
                    h, qb, j = items[idx]
                    c = h // 2
                    p0 = (h % 2) * 64
                    last = (j == 2 * qb + 1)
                    diag = (j >= 2 * qb)
                    q0 = qb * 256 + (128 if last else 0)
                    nq = 128 if last else 256
                    sbk = idx % 4
                    pt = idx % 6
                    S.op("pe", lambda e: e.matmul(
                        PS[sbk][:, 0:nq],
                        QK[:, c * 2048 + j * 128: c * 2048 + (j + 1) * 128],
                        QZ[:, h * 2048 + q0: h * 2048 + q0 + nq],
                        start=True, stop=(not diag)),
                        reads=[B_QK[4 + c][j // 4], B_QK[c][q0 // 512]], writes=[PB[sbk]])
                    if diag:
                        S.op("pe", lambda e: e.matmul(
                            PS[sbk][:, 0:128], ident_b, trineg_b, start=False, stop=True),
                            reads=[B_cb], writes=[PB[sbk]])
                    bcol = j * 64 + qb * 8 + h
                    S.op("act", lambda e: e.activation(
                        PT[pt][:, 0:nq], PS[sbk][:, 0:nq], AF.Exp, bias=BIASF[:, bcol:bcol + 1]),
                        reads=[B_G], writes=[PB[sbk], B_PT[pt]])

                def fox_back(idx):
                    h, qb, j = items[idx]
                    c = h // 2
                    p0 = (h % 2) * 64
                    last = (j == 2 * qb + 1)
                    nq = 128 if last else 256
                    pt = idx % 6
                    ob = 4 + (qb % 2)
                    oc0 = 128 if last else 0
                    S.op("pe", lambda e: e.matmul(
                        PS[ob][:, oc0:oc0 + nq],
                        VA[:, (j * 8 + h) * 128:(j * 8 + h + 1) * 128], PT[pt][:, 0:nq],
                        start=(j == 0), stop=last),
                        reads=[B_VA[j], B_PT[pt]], writes=[PB[ob]])
                    if last:
                        dp0 = 64 - p0
                        rd = qb % 2
                        S.op("dve", lambda e: e.reciprocal(
                            RD[rd][p0:p0 + 64, 0:256], PS[ob][dp0:dp0 + 64, 0:256]),
                            writes=[PB[ob], B_RD[rd]])
                        S.op("dve", lambda e: e.tensor_tensor(
                            OT[p0:p0 + 64, c * 2048 + qb * 256: c * 2048 + (qb + 1) * 256],
                            PS[ob][p0:p0 + 64, 0:256], RD[rd][p0:p0 + 64, 0:256], ALU.mult),
                            reads=[B_RD[rd]], writes=[PB[ob], B_OT[c][qb]])

                for idx in range(len(items) + LA):
                    if idx < len(items):
                        fox_front(idx)
                    if idx >= LA:
                        fox_back(idx - LA)
                    if idx % 17 == 5 and idx // 17 < 32:
                        mod_chunk(16 + idx // 17, 6 + ((idx // 17) % 2), B_mod2)
                mod_finish()
                for ch_ in range(3):
                    S.dma("pool", WA[ch_][:], d_wfm[(8 + ch_) * 128:(9 + ch_) * 128, :], writes=[B_WA[ch_]])
                S.barrier(engines=("act", "dve", "pool", "sp"))

            WO = sbt(phB, "WO", 8 * 1024, BF16)
            B_WO = [Buf(f"wo{m}") for m in range(8)]
            with ExitStack() as ph:
                QKm = sbt(ph, "QKm", 8 * 2048, BF16)
                B_QKm = [[Buf(f"qkm{c}_{b}") for b in range(4)] for c in range(8)]
                MV = sbt(ph, "MV", 16 * 512, BF16)
                B_MV = [Buf(f"mv{j}") for j in range(16)]
                MO = sbt(ph, "MO", 4 * 2048, BF16)
                B_MO = [[Buf(f"mo{c}_{b}") for b in range(4)] for c in range(4)]
                with ExitStack() as ph3:
                    WTM = sbt(ph3, "WTM2", 8 * 512, BF16)
                    B_WTM = Buf("wtm2")
                    S.dma("pool", AP(WTM, 4096, 0, [[512, 8], [1, 512]]),
                          d_wtm.rearrange("p (k c) -> p k c", c=1040)[:, :, 512:1024], writes=[B_WTM])
                    NWS = 3
                    W, B_W = WA, B_WA
                    PRE = [sbt(ph3, f"pre{i}", 2048 + 3, BF16) for i in range(2)]
                    B_PRE = [[Buf(f"pre{i}_{b}") for b in range(4)] for i in range(2)]
                    B_PREpad = [Buf(f"prepad{i}") for i in range(2)]
                    DG = [sbt(ph3, f"dg{i}", 512, BF16) for i in range(2)]
                    B_DG = [Buf(f"dg{i}") for i in range(2)]
                    QF = [sbt(ph3, f"qf{i}", 512, F32) for i in range(2)]
                    B_QF = [Buf(f"qf{i}") for i in range(2)]
                    DGQ = [sbt(ph3, f"dgq{i}", 128, F32) for i in range(8)]
                    B_DGQ = [Buf(f"dgq{i}") for i in range(8)]
                    qfc = [0]
                    dgc = [0]
                    GBANKS = [5, 6, 7]
                    for i in range(2):
                        S.op("pool", lambda e, i=i: e.memset(PRE[i][:, 0:3], 0.0), writes=[B_PREpad[i]])
                    bank = 0
                    for ch in range(8):
                        sl = ch % NWS
                        pr = ch % 2
                        if ch >= 3:
                            S.dma("pool", W[sl][:], d_wfm[(8 + ch) * 128:(9 + ch) * 128, :], writes=[B_W[sl]])
                        for tp in range(4):
                            S.op("dve", lambda e, pr=pr, tp=tp, ch=ch: e.tensor_scalar(
                                DG[pr][:, tp * 128:(tp + 1) * 128], ident_b,
                                smalls[:, C_MCW + ch * 4 + tp: C_MCW + ch * 4 + tp + 1], None, ALU.mult),
                                reads=[B_cb, B_small], writes=[B_DG[pr]])

                        def g_dve(ch_, blk_):
                            for k4 in range(4):
                                tt = blk_ * 4 + k4
                                dgi = (blk_ % 2) * 4 + k4
                                S.op("dve", lambda e, dgi=dgi, tt=tt: e.tensor_scalar(
                                    DGQ[dgi][:, 0:128], ident_f, GINV[:, tt * 4 + ch_: tt * 4 + ch_ + 1], None,
                                    ALU.mult), reads=[B_cf, B_G], writes=[B_DGQ[dgi]])

                        def g_pe(ch_, blk_):
                            gb_ = GBANKS[(ch_ * 4 + blk_) % 3]
                            for k4 in range(4):
                                dgi = (blk_ % 2) * 4 + k4
                                S.op("pe", lambda e, dgi=dgi, k4=k4: e.matmul(
                                    PS[gb_][:, k4 * 128:(k4 + 1) * 128], ones_f, DGQ[dgi][:, 0:128],
                                    start=True, stop=True),
                                    reads=[B_cf, B_DGQ[dgi]], writes=[PB[gb_]])
                        if ch < 4:
                            g_dve(ch, 0)
                            g_dve(ch, 1)
                        for blk in range(4):
                            b = bank % 3
                            bank += 1
                            for kc in range(8):
                                S.op("pe", lambda e, sl=sl, kc=kc, blk=blk, b=b: e.matmul(
                                    PS[b][:, :], W[sl][:, kc * 128:(kc + 1) * 128],
                                    BIGA[:, kc * 2048 + blk * 512: kc * 2048 + (blk + 1) * 512],
                                    start=(kc == 0), stop=(kc == 7)),
                                    reads=[B_W[sl], B_H[kc][blk]], writes=[PB[b]])
                            dst = PRE[pr][:, 3 + blk * 512: 3 + (blk + 1) * 512]
                            if blk % 2 == 0:
                                S.op("act", lambda e, dst=dst, b=b: e.activation(dst, PS[b][:, :], AF.Copy),
                                     writes=[PB[b], B_PRE[pr][blk]])
                            else:
                                S.op("dve", lambda e, dst=dst, b=b: e.tensor_copy(dst, PS[b][:, :]),
                                     writes=[PB[b], B_PRE[pr][blk]])
                        for blk in range(4):
                            b = 3 + (blk % 2)
                            rds = [B_DG[pr], B_PRE[pr][blk], B_PREpad[pr]]
                            if blk > 0:
                                rds.append(B_PRE[pr][blk - 1])
                            for tp in range(4):
                                S.op("pe", lambda e, pr=pr, tp=tp, blk=blk, b=b: e.matmul(
                                    PS[b][:, :], DG[pr][:, tp * 128:(tp + 1) * 128],
                                    PRE[pr][:, blk * 512 + tp: blk * 512 + tp + 512],
                                    start=(tp == 0), stop=(tp == 3)),
                                    reads=rds, writes=[PB[b]])
                            if ch >= 4:
                                S.op("act", lambda e, ch=ch, blk=blk, b=b: e.activation(
                                    QKm[:, ch * 2048 + blk * 512: ch * 2048 + (blk + 1) * 512], PS[b][:, :],
                                    AF.Silu, bias=smalls[:, C_MCB + ch: C_MCB + ch + 1]),
                                    reads=[B_small], writes=[PB[b], B_QKm[ch][blk]])
                            else:
                                qf = qfc[0] % 2
                                qfc[0] += 1
                                gb = GBANKS[(ch * 4 + blk) % 3]
                                S.op("act", lambda e, ch=ch, b=b, qf=qf: e.activation(
                                    QF[qf][:, 0:512], PS[b][:, :],
                                    AF.Silu, bias=smalls[:, C_MCB + ch: C_MCB + ch + 1]),
                                    reads=[B_small], writes=[PB[b], B_QF[qf]])
                                if blk == 0:
                                    g_pe(ch, 0)
                                if blk < 3:
                                    g_pe(ch, blk + 1)
                                S.op("dve", lambda e, ch=ch, blk=blk, qf=qf, gb=gb: e.tensor_tensor(
                                    QKm[:, ch * 2048 + blk * 512: ch * 2048 + (blk + 1) * 512],
                                    QF[qf][:, 0:512], PS[gb][:, 0:512], ALU.mult),
                                    reads=[B_QF[qf]], writes=[PB[gb], B_QKm[ch][blk]])
                                if blk < 2:
                                    g_dve(ch, blk + 2)
                    for ch in range(4):
                        sl = (ch + 2) % NWS
                        S.dma("pool", W[sl][:], d_wfm[(16 + ch) * 128:(17 + ch) * 128, :], writes=[B_W[sl]])
                        for blk in range(4):
                            b = bank % 3
                            bank += 1
                            for kc in range(8):
                                S.op("pe", lambda e, sl=sl, kc=kc, blk=blk, b=b: e.matmul(
                                    PS[b][:, :], W[sl][:, kc * 128:(kc + 1) * 128],
                                    BIGA[:, kc * 2048 + blk * 512: kc * 2048 + (blk + 1) * 512],
                                    start=(kc == 0), stop=(kc == 7)),
                                    reads=[B_W[sl], B_H[kc][blk]], writes=[PB[b]])
                            S.op("act", lambda e, ch=ch, blk=blk, b=b: e.activation(
                                MO[:, ch * 2048 + blk * 512: ch * 2048 + (blk + 1) * 512], PS[b][:, :],
                                AF.Sigmoid), writes=[PB[b], B_MO[ch][blk]])
                    for tt in range(16):
                        b = 5 + (tt % 2)
                        for kc in range(8):
                            S.op("pe", lambda e, kc=kc, tt=tt, b=b: e.matmul(
                                PS[b][:, :], BIGA[:, kc * 2048 + tt * 128: kc * 2048 + (tt + 1) * 128],
                                WTM[:, kc * 512:(kc + 1) * 512],
                                start=(kc == 0), stop=(kc == 7)),
                                reads=[B_WTM, B_H[kc][tt // 4]], writes=[PB[b]])
                        if tt % 2 == 0:
                            S.op("dve", lambda e, tt=tt, b=b: e.tensor_copy(
                                MV[:, tt * 512:(tt + 1) * 512], PS[b][:, :]), writes=[PB[b], B_MV[tt]])
                        else:
                            S.op("act", lambda e, tt=tt, b=b: e.activation(
                                MV[:, tt * 512:(tt + 1) * 512], PS[b][:, :], AF.Copy),
                                writes=[PB[b], B_MV[tt]])

                    S.barrier()
                for m in range(8):
                    S.dma("pool", WO[:, m * 1024:(m + 1) * 1024], d_wout[m * 128:(m + 1) * 128, :],
                          writes=[B_WO[m]])
                PT = [sbt(ph, f"ptm{i}", 256, BF16) for i in range(6)]
                B_PT = [Buf(f"ptm{i}") for i in range(6)]
                DGL = [sbt(ph, f"dgl{i}", 128, F32) for i in range(6)]
                B_DGL = [Buf(f"dgl{i}") for i in range(6)]
                SBANKS = [0, 1, 2, 5]
                NDBANKS = [3, 4, 6]
                T1 = [sbt(ph, f"T1_{i}", 256, F32) for i in range(3)]
                HM = [sbt(ph, f"HM_{i}", 256, F32) for i in range(3)]
                SQh = [sbt(ph, f"SQh_{i}", 256, BF16) for i in range(3)]
                RS = [sbt(ph, f"RS_{i}", 256, F32) for i in range(3)]
                T2 = [sbt(ph, f"T2_{i}", 256, F32) for i in range(3)]
                B_T1 = [Buf(f"T1_{i}") for i in range(3)]
                B_HM = [Buf(f"HM_{i}") for i in range(3)]
                B_SQh = [Buf(f"SQh_{i}") for i in range(3)]
                B_RS = [Buf(f"RS_{i}") for i in range(3)]
                B_T2 = [Buf(f"T2_{i}") for i in range(3)]
                LA = 4
                blocks = [(hm, qb) for hm in range(4) for qb in range(8)]
                items = []
                for n, (hm, qb) in enumerate(blocks):
                    for j in range(2 * qb + 2):
                        items.append((n, hm, qb, j))
                sctr = [0]
                ss_bank = {}

                def m_front(idx):
                    n, hm, qb, j = items[idx]
                    last = (j == 2 * qb + 1)
                    diag = (j >= 2 * qb)
                    q0 = qb * 256 + (128 if last else 0)
                    nq = 128 if last else 256
                    sbk = SBANKS[sctr[0] % 4]
                    sctr[0] += 1
                    pt = idx % 6
                    S.op("pe", lambda e: e.matmul(
                        PS[sbk][:, 0:nq],
                        QKm[:, (4 + hm) * 2048 + j * 128:(4 + hm) * 2048 + (j + 1) * 128],
                        QKm[:, hm * 2048 + q0: hm * 2048 + q0 + nq],
                        start=True, stop=True),
                        reads=[B_QKm[4 + hm][j // 4], B_QKm[hm][q0 // 512]], writes=[PB[sbk]])
                    scol = SCL[:, j * 32 + qb * 4 + hm: j * 32 + qb * 4 + hm + 1]
                    if diag:
                        S.op("dve", lambda e: e.scalar_tensor_tensor(
                            PT[pt][:, 0:128], tri_f, scol, PS[sbk][:, 0:128], ALU.mult, ALU.mult),
                            reads=[B_G, B_cf], writes=[PB[sbk], B_PT[pt]])
                        if nq == 256:
                            S.op("act", lambda e: e.activation(
                                PT[pt][:, 128:256], PS[sbk][:, 128:256], AF.Identity, scale=scol),
                                reads=[B_G], writes=[PB[sbk], B_PT[pt]])
                    elif idx % 2 == 0:
                        S.op("act", lambda e: e.activation(
                            PT[pt][:, 0:nq], PS[sbk][:, 0:nq], AF.Identity, scale=scol),
                            reads=[B_G], writes=[PB[sbk], B_PT[pt]])
                    else:
                        S.op("dve", lambda e: e.tensor_scalar(
                            PT[pt][:, 0:nq], PS[sbk][:, 0:nq], scol, None, ALU.mult),
                            reads=[B_G], writes=[PB[sbk], B_PT[pt]])

                def m_back(idx):
                    n, hm, qb, j = items[idx]
                    last = (j == 2 * qb + 1)
                    nq = 128 if last else 256
                    pt = idx % 6
                    nbk = NDBANKS[n % 3]
                    oc0 = 128 if last else 0
                    S.op("pe", lambda e: e.matmul(
                        PS[nbk][:, oc0:oc0 + nq],
                        MV[:, j * 512 + hm * 128: j * 512 + (hm + 1) * 128], PT[pt][:, 0:nq],
                        start=(j == 0), stop=last, skip_group_check=True),
                        reads=[B_MV[j], B_PT[pt]], writes=[PB[nbk]])
                    S.op("pe", lambda e: e.matmul(
                        PS[nbk][:, 256 + oc0:256 + oc0 + nq], ones_b, PT[pt][:, 0:nq],
                        start=False, stop=last, skip_group_check=True),
                        reads=[B_cb, B_PT[pt]], writes=[PB[nbk]])

                def m_lb_pool(n):
                    hm, qb = blocks[n]
                    for k2 in range(2):
                        tt = 2 * qb + k2
                        dg = 2 * (n % 3) + k2
                        S.op("pool", lambda e: e.tensor_scalar(
                            DGL[dg][:, 0:128], ident_f, LB[:, tt * 4 + hm: tt * 4 + hm + 1], None,
                            ALU.mult), reads=[B_cf, B_G], writes=[B_DGL[dg]])

                def m_lb(n):
                    par = n % 2
                    for k2 in range(2):
                        dg = 2 * (n % 3) + k2
                        S.op("pe", lambda e: e.matmul(
                            PS[7][:, par * 256 + k2 * 128: par * 256 + (k2 + 1) * 128], ones_f,
                            DGL[dg][:, 0:128], start=True, stop=True),
                            reads=[B_cf, B_DGL[dg]], writes=[PB[7]])

                def m_stages(n):
                    hm, qb = blocks[n]
                    par = n % 2
                    st3 = n % 3
                    nbk = NDBANKS[st3]
                    sb = {}

                    def a0():
                        S.op("act", lambda e: e.activation(T1[st3][:, 0:256], PS[nbk][:, 256:512], AF.Abs),
                             writes=[PB[nbk], B_T1[st3]])

                    def a1():
                        S.op("dve", lambda e: e.tensor_scalar(
                            T1[st3][:, 0:256], T1[st3][:, 0:256], 1.0, None, ALU.max), writes=[B_T1[st3]])

                    def b0():
                        S.op("act", lambda e: e.activation(T1[st3][:, 0:256], T1[st3][:, 0:256], AF.Ln),
                             writes=[B_T1[st3]])

                    def b1():
                        S.op("act", lambda e: e.activation(T1[st3][:, 0:256], T1[st3][:, 0:256], AF.Exp,
                                                           scale=-1.0), writes=[B_T1[st3]])

                    def c0():
                        S.op("dve", lambda e: e.tensor_tensor(
                            HM[st3][:, 0:256], PS[nbk][:, 0:256], T1[st3][:, 0:256], ALU.mult),
                            reads=[B_T1[st3]], writes=[PB[nbk], B_HM[st3]])

                    def d0():
                        S.op("act", lambda e: e.activation(SQh[st3][:, 0:256], HM[st3][:, 0:256], AF.Square),
                             reads=[B_HM[st3]], writes=[B_SQh[st3]])

                    def e0():
                        sbk = SBANKS[sctr[0] % 4]
                        sctr[0] += 1
                        sb["b"] = sbk
                        S.op("pe", lambda e: e.matmul(PS[sbk][:, 0:256], onesH[:, :], SQh[st3][:, 0:256],
                                                      start=True, stop=True),
                             reads=[B_ones, B_SQh[st3]], writes=[PB[sbk]])

                    def e1():
                        sbk = sb["b"]
                        S.op("act", lambda e: e.activation(RS[st3][:, 0:256], PS[sbk][:, 0:256], AF.Ln,
                                                           bias=kcol[:, 0:1]),
                             reads=[B_kcol], writes=[PB[sbk], B_RS[st3]])

                    def f0():
                        S.op("act", lambda e: e.activation(RS[st3][:, 0:256], RS[st3][:, 0:256], AF.Exp,
                                                           scale=-0.5), writes=[B_RS[st3]])

                    def g0():
                        S.op("dve", lambda e: e.tensor_tensor(
                            T2[st3][:, 0:256], HM[st3][:, 0:256], RS[st3][:, 0:256], ALU.mult),
                            reads=[B_HM[st3], B_RS[st3]], writes=[B_T2[st3]])

                    def g1():
                        S.op("dve", lambda e: e.scalar_tensor_tensor(
                            OT[:, (4 + hm) * 2048 + qb * 256:(4 + hm) * 2048 + (qb + 1) * 256],
                            T2[st3][:, 0:256], smalls[:, C_GMH + hm: C_GMH + hm + 1],
                            MO[:, hm * 2048 + qb * 256: hm * 2048 + (qb + 1) * 256],
                            ALU.mult, ALU.mult),
                            reads=[B_T2[st3], B_small, B_MO[hm][qb // 2]], writes=[B_OT[4 + hm][qb]])
                    return [(0, a0), (1, a1), (2, b0), (3, b1), (4, c0), (5, d0), (6, e0), (7, e1),
                            (8, f0), (9, g0), (10, g1)]

                first_idx = {}
                last_idx = {}
                for idx, (n, hm, qb, j) in enumerate(items):
                    first_idx.setdefault(n, idx)
                    last_idx[n] = idx
                lb_after_front = {}
                for n in range(len(blocks) - 1):
                    lb_after_front[first_idx[n] + LA] = n + 1
                e1_after_back = {last_idx[n]: n for n in range(len(blocks))}
                pending = []
                NIT = len(items)
                it = 0
                while it < NIT + LA or pending:
                    if it < NIT:
                        m_front(it)
                    due = sorted([p for p in pending if p[0] <= it], key=lambda p: (p[0], p[1]))
                    for p in due:
                        p[2]()
                        pending.remove(p)
                    if LA <= it < NIT + LA:
                        m_back(it - LA)
                        if (it - LA) in e1_after_back:
                            n = e1_after_back[it - LA]
                            for k, (dly, fn) in enumerate(m_stages(n)):
                                if dly == 0:
                                    fn()
                                else:
                                    pending.append((it + dly, n * 10 + k, fn))
                    it += 1
                S.barrier()
            S.barrier()

            if debug_taps:
                S.dma("sp", d_dbg, OT[:], reads=[B_OT[c][q] for c in range(8) for q in range(8)])
            with ExitStack() as ph:
                MIX = [sbt(ph, f"MIX{i}", 4096, F32) for i in range(2)]
                B_MIX = [[Buf(f"mix{i}_{m}") for m in range(8)] for i in range(2)]
                SQ1 = sbt(ph, "sqC", 4096, BF16)
                B_SQ1 = [Buf(f"sqC_{m}") for m in range(8)]
                SQ2 = sbt(ph, "sqC2", 4096, BF16)
                B_SQ2 = Buf("sqC2")
                XB = [sbt(ph, f"xbC{i}", 4096, F32) for i in range(2)]
                B_XBs = [Buf(f"xbC{i}") for i in range(2)]
                RSTD = sbt(ph, "rstdC", 512, F32)
                B_RSTD = Buf("rstdC")
                RSTD2 = sbt(ph, "rstdC2", 512, F32)
                B_RSTD2 = Buf("rstdC2")
                TMPx = [sbt(ph, f"tmpCx{i}", 512, F32) for i in range(2)]
                B_TMPx = [Buf(f"tmpCx{i}") for i in range(2)]
                TMPs = [sbt(ph, f"tmpCs{i}", 512, F32) for i in range(2)]
                B_TMPs = [Buf(f"tmpCs{i}") for i in range(2)]

                def c_chain(blk):
                    i = blk % 2
                    ops = []

                    def load():
                        S.dma("sp", AP(XB[i], 4096, 0, [[512, 8], [1, 512]]), xblk(d_xT, blk),
                              writes=[B_XBs[i]])

                    def s1(m):
                        def f():
                            b = m % 4
                            for kc in range(8):
                                S.op("pe", lambda e, kc=kc: e.matmul(
                                    PS[b][:, :], WO[:, m * 1024 + kc * 128: m * 1024 + (kc + 1) * 128],
                                    OT[:, kc * 2048 + blk * 512: kc * 2048 + (blk + 1) * 512],
                                    start=(kc == 0), stop=(kc == 7)),
                                    reads=[B_WO[m], B_OT[kc][2 * blk], B_OT[kc][2 * blk + 1]],
                                    writes=[PB[b]])
                            S.op("act", lambda e: e.activation(
                                MIX[i][:, m * 512:(m + 1) * 512], PS[b][:, :], AF.Copy),
                                writes=[PB[b], B_MIX[i][m]])
                            S.op("act", lambda e: e.activation(
                                SQ1[:, m * 512:(m + 1) * 512], PS[b][:, :], AF.Square),
                                writes=[PB[b], B_SQ1[m]])
                        return f

                    def ss_ln():
                        for m in range(8):
                            S.op("pe", lambda e, m=m: e.matmul(
                                PS[4][:, :], onesD[:, :], SQ1[:, m * 512:(m + 1) * 512],
                                start=(m == 0), stop=(m == 7)),
                                reads=[B_SQ1[m], B_ones], writes=[PB[4]])
                        S.op("act", lambda e: e.activation(RSTD[:, 0:512], PS[4][:, :], AF.Ln, bias=kcol[:, 0:1]),
                             reads=[B_kcol], writes=[PB[4], B_RSTD])

                    def rs_exp():
                        S.op("act", lambda e: e.activation(RSTD[:, 0:512], RSTD[:, 0:512], AF.Exp, scale=-0.5),
                             writes=[B_RSTD])

                    def x1(m):
                        def f():
                            sl = m % 2
                            S.op("dve", lambda e: e.tensor_tensor(
                                TMPx[sl][:, 0:512], MIX[i][:, m * 512:(m + 1) * 512], RSTD[:, 0:512], ALU.mult),
                                reads=[B_MIX[i][m], B_RSTD], writes=[B_TMPx[sl]])
                            S.op("dve", lambda e: e.scalar_tensor_tensor(
                                XB[i][:, m * 512:(m + 1) * 512], TMPx[sl][:, 0:512], der[:, 8 + m: 9 + m],
                                XB[i][:, m * 512:(m + 1) * 512], ALU.mult, ALU.add),
                                reads=[B_TMPx[sl], B_der2], writes=[B_XBs[i]])
                        return f

                    def store_sq():
                        S.dma("sp", xblk(d_x1, blk), AP(XB[i], 4096, 0, [[512, 8], [1, 512]]),
                              reads=[B_XBs[i]])
                        S.op("act", lambda e: e.activation(SQ2[:, 0:2048], XB[i][:, 0:2048], AF.Square),
                             reads=[B_XBs[i]], writes=[B_SQ2])
                        S.op("dve", lambda e: e.tensor_tensor(SQ2[:, 2048:4096], XB[i][:, 2048:4096],
                                                              XB[i][:, 2048:4096], ALU.mult),
                             reads=[B_XBs[i]], writes=[B_SQ2])

                    def ss2_ln():
                        for kc in range(8):
                            S.op("pe", lambda e, kc=kc: e.matmul(
                                PS[5][:, :], onesD[:, :], SQ2[:, kc * 512:(kc + 1) * 512],
                                start=(kc == 0), stop=(kc == 7)),
                                reads=[B_SQ2, B_ones], writes=[PB[5]])
                        S.op("act", lambda e: e.activation(RSTD2[:, 0:512], PS[5][:, :], AF.Ln, bias=kcol[:, 0:1]),
                             reads=[B_kcol], writes=[PB[5], B_RSTD2])

                    def rs2_exp():
                        S.op("act", lambda e: e.activation(RSTD2[:, 0:512], RSTD2[:, 0:512], AF.Exp, scale=-0.5),
                             writes=[B_RSTD2])

                    def s2b(c):
                        def f():
                            sl = c % 2
                            S.op("dve", lambda e: e.tensor_tensor(
                                TMPs[sl][:, 0:512], XB[i][:, c * 512:(c + 1) * 512], RSTD2[:, 0:512], ALU.mult),
                                reads=[B_XBs[i], B_RSTD2], writes=[B_TMPs[sl]])
                            S.op("act", lambda e: e.activation(
                                BIGA[:, c * 2048 + blk * 512: c * 2048 + (blk + 1) * 512], TMPs[sl][:, 0:512],
                                AF.Identity, bias=modT[:, 24 + c:25 + c], scale=der[:, 16 + c:17 + c]),
                                reads=[B_TMPs[sl], B_der2, B_mod2], writes=[B_H[c][blk]])
                        return f

                    noop = lambda: None
                    ops += [s1(m) for m in range(8)]
                    ops[5 if blk >= 2 else 0] = (lambda g=ops[5 if blk >= 2 else 0]: (load(), g()))
                    ops += [ss_ln, rs_exp]
                    ops += [x1(m) for m in range(8)]
                    ops += [store_sq, ss2_ln, rs2_exp]
                    ops += [s2b(c) for c in range(8)]
                    return ops

                chains = [c_chain(blk) for blk in range(4)]
                STAG = 12
                for tick in range(3 * STAG + 29):
                    for blk in range(4):
                        k = tick - blk * STAG
                        if 0 <= k < 29:
                            chains[blk][k]()
                for k_, ch_ in enumerate([0, 22, 1, 23]):
                    S.dma("pool", WA[k_][:], d_wup[ch_ * 128:(ch_ + 1) * 128, :], writes=[B_WA[k_]])
                S.barrier(engines=("act", "dve", "pool", "sp"))

        with ExitStack() as ph:
            G = sbt(ph, "G", 22 * 2048, BF16)
            B_Gc = [[Buf(f"g{i}_{b}") for b in range(4)] for i in range(22)]
            WD2 = sbt(ph, "wd2", 3 * 2816, BF16)
            B_WD = [Buf(f"wd{m}") for m in range(8)]
            with ExitStack() as ph2:
                NWS = 4
                W, B_W = WA, B_WA
                PRE = [sbt(ph2, f"pref{i}", 2048 + 2, BF16) for i in range(4)]
                B_PRE = [[Buf(f"pref{i}_{b}") for b in range(4)] for i in range(4)]
                B_PREpad = [Buf(f"prefpad{i}") for i in range(4)]
                DG = [sbt(ph2, f"dgf{i}", 384, BF16) for i in range(4)]
                B_DG = [Buf(f"dgf{i}") for i in range(4)]
                SAFt = sbt(ph2, "SAFt", 2048, F32)
                B_SAFt = [Buf(f"saft{b}") for b in range(4)]
                for i in range(4):
                    S.op("pool", lambda e, i=i: e.memset(PRE[i][:, 0:2], 0.0), writes=[B_PREpad[i]])
                bank = 0
                wcnt = 0
                for i in range(22):
                    if i == 3:
                        for m in range(5, 8):
                            S.dma("pool", WD2[:, (m - 5) * 2816:(m - 4) * 2816],
                                  d_wdown[m * 128:(m + 1) * 128, :], writes=[B_WD[m]])
                    for half in range(2):
                        ch = i + 22 * half
                        sl = wcnt % NWS
                        pr = wcnt % 4
                        wcnt += 1
                        if wcnt > 4:
                            S.dma("pool", W[sl][:], d_wup[ch * 128:(ch + 1) * 128, :], writes=[B_W[sl]])
                        for tp in range(3):
                            S.op("dve", lambda e, pr=pr, tp=tp, ch=ch: e.tensor_scalar(
                                DG[pr][:, tp * 128:(tp + 1) * 128], ident_b,
                                smalls[:, C_FCW + ch * 3 + tp: C_FCW + ch * 3 + tp + 1], None, ALU.mult),
                                reads=[B_cb, B_small], writes=[B_DG[pr]])
                        for blk in range(4):
                            b = bank % 4
                            bank += 1
                            for kc in range(8):
                                S.op("pe", lambda e, sl=sl, kc=kc, blk=blk, b=b: e.matmul(
                                    PS[b][:, :], W[sl][:, kc * 128:(kc + 1) * 128],
                                    BIGA[:, kc * 2048 + blk * 512: kc * 2048 + (blk + 1) * 512],
                                    start=(kc == 0), stop=(kc == 7)),
                                    reads=[B_W[sl], B_H[kc][blk]], writes=[PB[b]])
                            dst = PRE[pr][:, 2 + blk * 512: 2 + (blk + 1) * 512]
                            if blk % 2 == 0:
                                S.op("act", lambda e, dst=dst, b=b: e.activation(dst, PS[b][:, :], AF.Copy),
                                     writes=[PB[b], B_PRE[pr][blk]])
                            else:
                                S.op("dve", lambda e, dst=dst, b=b: e.tensor_copy(dst, PS[b][:, :]),
                                     writes=[PB[b], B_PRE[pr][blk]])
                        for blk in range(4):
                            b = 4 + 2 * half + (blk % 2)
                            rds = [B_DG[pr], B_PRE[pr][blk], B_PREpad[pr]]
                            if blk > 0:
                                rds.append(B_PRE[pr][blk - 1])
                            for tp in range(3):
                                S.op("pe", lambda e, pr=pr, tp=tp, blk=blk, b=b: e.matmul(
                                    PS[b][:, :], DG[pr][:, tp * 128:(tp + 1) * 128],
                                    PRE[pr][:, blk * 512 + tp: blk * 512 + tp + 512],
                                    start=(tp == 0), stop=(tp == 2)),
                                    reads=rds, writes=[PB[b]])
                            if half == 0:
                                S.op("act", lambda e, ch=ch, blk=blk, b=b: e.activation(
                                    SAFt[:, blk * 512:(blk + 1) * 512], PS[b][:, :], AF.Silu,
                                    bias=smalls[:, C_FCB + ch: C_FCB + ch + 1]),
                                    reads=[B_small], writes=[PB[b], B_SAFt[blk]])
                            else:
                                S.op("dve", lambda e, ch=ch, blk=blk, b=b, i=i: e.scalar_tensor_tensor(
                                    G[:, i * 2048 + blk * 512: i * 2048 + (blk + 1) * 512], PS[b][:, :],
                                    smalls[:, C_FCB + ch: C_FCB + ch + 1],
                                    SAFt[:, blk * 512:(blk + 1) * 512], ALU.add, ALU.mult),
                                    reads=[B_small, B_SAFt[blk]], writes=[PB[b], B_Gc[i][blk]])
                S.barrier()

            with ExitStack() as ph2:

                def wd_ap(m, kc):
                    if m < 5:
                        return BIGA[:, m * 2816 + kc * 128: m * 2816 + (kc + 1) * 128]
                    return WD2[:, (m - 5) * 2816 + kc * 128:(m - 5) * 2816 + (kc + 1) * 128]
                for m in range(5):
                    S.dma("pool", BIGA[:, m * 2816:(m + 1) * 2816], d_wdown[m * 128:(m + 1) * 128, :],
                          writes=[B_WD[m]])
                Y = [sbt(ph2, f"Y{i}", 4096, F32) for i in range(2)]
                B_Y = [[Buf(f"y{i}_{m}") for m in range(8)] for i in range(2)]
                SQ = [sbt(ph2, f"sqE{i}", 512, BF16) for i in range(2)]
                B_SQ = [Buf(f"sqE{i}") for i in range(2)]
                XB = sbt(ph2, "xbE", 4096, F32)
                B_XBm = [Buf(f"xbE{m}") for m in range(8)]
                RSTD = sbt(ph2, "rstdE", 512, F32)
                B_RSTD = Buf("rstdE")
                TMP = [sbt(ph2, f"tmpE{i}", 512, F32) for i in range(2)]
                B_TMP = [Buf(f"tmpE{i}") for i in range(2)]

                MORD = [5, 6, 7, 0, 1, 2, 3, 4]

                def e_ss(blk, mi):
                    sq = mi % 2
                    S.op("pe", lambda e: e.matmul(
                        PS[4 + (blk % 2)][:, :], onesD[:, :], SQ[sq][:, 0:512],
                        start=(mi == 0), stop=(mi == 7)),
                        reads=[B_SQ[sq], B_ones], writes=[PB[4 + (blk % 2)]])

                def e_load(blk):
                    S.dma("sp", AP(XB, 4096, 0, [[512, 8], [1, 512]]), xblk(d_x1, blk), writes=B_XBm)

                def e_chain(blk):
                    yb = blk % 2
                    rsqrt_eps(RSTD[:, 0:512], PS[4 + (blk % 2)][:, :], PB[4 + (blk % 2)], B_RSTD)
                    for m in range(8):
                        sl = m % 2
                        S.op("dve", lambda e: e.tensor_tensor(
                            TMP[sl][:, 0:512], Y[yb][:, m * 512:(m + 1) * 512], RSTD[:, 0:512], ALU.mult),
                            reads=[B_Y[yb][m], B_RSTD], writes=[B_TMP[sl]])
                        S.op("dve", lambda e: e.scalar_tensor_tensor(
                            XB[:, m * 512:(m + 1) * 512], TMP[sl][:, 0:512], der[:, 24 + m: 25 + m],
                            XB[:, m * 512:(m + 1) * 512], ALU.mult, ALU.add),
                            reads=[B_TMP[sl], B_der2], writes=[B_XBm[m]])
                        S.dma("sp", xblk(d_out, blk)[:, m, :], XB[:, m * 512:(m + 1) * 512], reads=[B_XBm[m]])

                for blk in range(4):
                    yb = blk % 2
                    for mi in range(8):
                        m = MORD[mi]
                        b = mi % 4
                        for kc in range(22):
                            S.op("pe", lambda e, kc=kc: e.matmul(
                                PS[b][:, :], wd_ap(m, kc),
                                G[:, kc * 2048 + blk * 512: kc * 2048 + (blk + 1) * 512],
                                start=(kc == 0), stop=(kc == 21)),
                                reads=[B_WD[m], B_Gc[kc][blk]], writes=[PB[b]])
                        S.op("act", lambda e: e.activation(
                            Y[yb][:, m * 512:(m + 1) * 512], PS[b][:, :], AF.Copy),
                            writes=[PB[b], B_Y[yb][m]])
                        if mi == 0 and blk > 0:
                            e_ss(blk - 1, 7)
                            e_load(blk - 1)
                        if mi == 3 and blk > 0:
                            e_chain(blk - 1)
                        if mi > 0:
                            e_ss(blk, mi - 1)
                        sq = mi % 2
                        S.op("act", lambda e: e.activation(SQ[sq][:, 0:512], PS[b][:, :], AF.Square),
                             writes=[PB[b], B_SQ[sq]])
                e_ss(3, 7)
                e_load(3)
                e_chain(3)
                S.barrier()
        S.finish("sp")
    return nc


def _chunkify(w, n_k):
    K, N = w.shape
    n = N // 128
    a = w.reshape(n_k, 128, n, 128).transpose(2, 1, 0, 3)
    return np.ascontiguousarray(a.reshape(n * 128, n_k * 128))


def _vec(v):
    return np.ascontiguousarray(v.reshape(-1, 128).T)


def _prep(inputs):
    f = lambda k: np.asarray(inputs[k], dtype=np.float32)
    w_in = f("w_in")[0]
    fm_cols = np.r_[0:512, 512:1024, 1544:2056, 2056:2568, 3088:3600]
    wfm = _chunkify(w_in[:, fm_cols], 8)
    tm_cols = np.r_[1024:1536, 2568:3080, 1536:1544, 3080:3084, 3084:3088]
    wtm = np.ascontiguousarray(
        w_in[:, tm_cols].reshape(8, 128, 1040).transpose(1, 0, 2).reshape(128, 8 * 1040))
    wout = _chunkify(f("w_out")[0], 8)
    wup = _chunkify(f("w_up")[0], 8)
    wdown = _chunkify(f("w_down")[0], 22)
    wada = _chunkify(f("w_ada")[0], 8)
    sm = np.zeros((128, NSM), np.float32)
    sm[:, C_BADA:C_BADA + 48] = _vec(f("b_ada")[0])
    sm[:, C_GPM:C_GPM + 8] = _vec(f("g_pre_mix")[0])
    sm[:, C_GOM:C_GOM + 8] = _vec(f("g_post_mix")[0])
    sm[:, C_GPF:C_GPF + 8] = _vec(f("g_pre_ffn")[0])
    sm[:, C_GOF:C_GOF + 8] = _vec(f("g_post_ffn")[0])
    mcw = f("mlstm_conv_w")[0]
    sm[:, C_MCW:C_MCW + 32] = mcw.reshape(4, 8, 128).transpose(2, 1, 0).reshape(128, 32)
    sm[:, C_MCB:C_MCB + 8] = _vec(f("mlstm_conv_b")[0])
    sm[:, C_GMH:C_GMH + 4] = _vec(f("g_mlstm_head")[0])
    fcw = f("ffn_conv_w")[0]
    sm[:, C_FCW:C_FCW + 132] = fcw.reshape(3, 44, 128).transpose(2, 1, 0).reshape(128, 132)
    sm[:, C_FCB:C_FCB + 44] = _vec(f("ffn_conv_b")[0])
    sm[:, C_BG:C_BG + 16] = f("b_gate")[0][None, :]
    consts = np.zeros((128, 512), np.float32)
    k = np.arange(128)[:, None]
    q = np.arange(128)[None, :]
    consts[:, 0:128] = (k == q)
    consts[:, 128:256] = (q >= k)
    consts[:, 256:384] = 1.0
    consts[:, 384:512] = np.where(k > q, NEG, 0.0)
    x = f("x")
    c = f("c")
    in_maps = []
    for b in range(NCORES):
        smb = sm.copy()
        smb[:, C_C:C_C + 8] = _vec(c[b])
        in_maps.append({"xT": np.ascontiguousarray(x[b].T), "wfm": wfm, "wtm": wtm, "wout": wout,
                        "wup": wup, "wdown": wdown, "wada": wada, "smalls": smb, "consts": consts})
    return in_maps


_NC_CACHE = {}


def kernel(**inputs):
    in_maps = _prep(inputs)
    if "nc" not in _NC_CACHE:
        _NC_CACHE["nc"] = build_nc()
    res = run_bass_kernel_spmd(_NC_CACHE["nc"], in_maps, core_ids=list(range(NCORES)))
    out = np.stack([np.ascontiguousarray(r["outT"].T) for r in res.results], axis=0)
    return out.astype(np.float32)
```
